# Optimizing a Trainium2 kernel written in Bass

```python
import jax, jax.numpy as jnp
from jax import lax
import numpy as np

D_MODEL = 1024
BATCH = 32
SEQ = 2048
DEPTH = 2
DEC_BATCH = 4
DEC_SEQ = 4096
PAST_LEN = 128

N_META = 16
BLOCK = 128
WINDOW = 128
ROPE_THETA = 500000.0
MLA_HEADS = D_MODEL // 64
MLA_NOPE = 64
MLA_ROPE = 32
MLA_V = 64
MLA_KV_RANK = 4 * MLA_NOPE
MLA_Q_RANK = 3 * MLA_KV_RANK
GQA_Q_HEADS = 16
GQA_KV_HEADS = 4
GQA_GROUP = GQA_Q_HEADS // GQA_KV_HEADS
GQA_HEAD_DIM = D_MODEL // GQA_Q_HEADS
GQA_ROT = GQA_HEAD_DIM // 4
D_FF = 4 * D_MODEL
N_MLA_LAYERS = (DEPTH + 1) // 2
N_GQA_LAYERS = DEPTH // 2
DN_ALPHA = (2.0 * DEPTH) ** 0.25
DN_BETA = (8.0 * DEPTH) ** -0.25
LN_EPS = 1e-5
RMS_EPS = 1e-6
NEG_INF = -1e30

kernel_name = 'hybrid_mla_swa_sink_encoder'


def layer_norm(x, g, b):
    xf = x.astype(jnp.float32)
    mu = jnp.mean(xf, axis=-1, keepdims=True)
    xc = xf - mu
    var = jnp.mean(xc * xc, axis=-1, keepdims=True)
    y = xc * lax.rsqrt(var + LN_EPS) * g.astype(jnp.float32) + b.astype(jnp.float32)
    return y.astype(x.dtype)


def rms_norm(x, g):
    xf = x.astype(jnp.float32)
    y = xf * lax.rsqrt(jnp.mean(xf * xf, axis=-1, keepdims=True) + RMS_EPS) * g.astype(jnp.float32)
    return y.astype(x.dtype)


def rope_tables(length, dim, dtype):
    pos = jnp.arange(length, dtype=jnp.float32)
    inv = ROPE_THETA ** (-jnp.arange(0, dim, 2, dtype=jnp.float32) / dim)
    ang = pos[:, None] * inv[None, :]
    return jnp.cos(ang).astype(dtype), jnp.sin(ang).astype(dtype)


def apply_rope(x, cos, sin):
    c = cos[None, :, None, :]
    s = sin[None, :, None, :]
    x1, x2 = jnp.split(x, 2, axis=-1)
    return jnp.concatenate([x1 * c - x2 * s, x2 * c + x1 * s], axis=-1)


def partial_rope(x, cos, sin):
    return jnp.concatenate([apply_rope(x[..., :GQA_ROT], cos, sin), x[..., GQA_ROT:]], axis=-1)


def mla_attend(qn, qr, kn, kr, v):
    scale = (MLA_NOPE + MLA_ROPE) ** -0.5
    s = jnp.einsum('bqhd,bkhd->bhqk', qn, kn) + jnp.einsum('bqhr,bkr->bhqk', qr, kr)
    p = jax.nn.softmax(s.astype(jnp.float32) * scale, axis=-1).astype(v.dtype)
    return jnp.einsum('bhqk,bkhd->bqhd', p, v)


def mla_mixer(x, cos, sin, w_in, g_q, w_uq, g_kv, w_ukv, w_o):
    B, L, _ = x.shape
    S = L - N_META
    nb = S // BLOCK
    c = x @ w_in
    cq = rms_norm(c[..., :MLA_Q_RANK], g_q)
    ckv = rms_norm(c[..., MLA_Q_RANK:MLA_Q_RANK + MLA_KV_RANK], g_kv)
    kr = apply_rope(c[..., MLA_Q_RANK + MLA_KV_RANK:][:, :, None, :], cos, sin)[:, :, 0, :]
    q = (cq @ w_uq).reshape(B, L, MLA_HEADS, MLA_NOPE + MLA_ROPE)
    qn = q[..., :MLA_NOPE]
    qr = apply_rope(q[..., MLA_NOPE:], cos, sin)
    kv = (ckv @ w_ukv).reshape(B, L, MLA_HEADS, MLA_NOPE + MLA_V)
    kn = kv[..., :MLA_NOPE]
    v = kv[..., MLA_NOPE:]
    out_meta = mla_attend(qn[:, :N_META], qr[:, :N_META], kn, kr, v)

    def to_blocks(t):
        return t[:, N_META:].reshape((B, nb, BLOCK) + t.shape[2:]).swapaxes(0, 1)

    out_real = lax.map(lambda qb: mla_attend(qb[0], qb[1], kn, kr, v), (to_blocks(qn), to_blocks(qr)))
    out_real = out_real.swapaxes(0, 1).reshape(B, S, MLA_HEADS, MLA_V)
    out = jnp.concatenate([out_meta, out_real], axis=1).reshape(B, L, MLA_HEADS * MLA_V)
    return out @ w_o


def sink_attend(q, k, v, mask, sink):
    s = jnp.einsum('bqkgd,bjkd->bkgqj', q, k).astype(jnp.float32) * (GQA_HEAD_DIM ** -0.5)
    s = jnp.where(mask, s, NEG_INF)
    sk = sink.astype(jnp.float32)[None, :, :, None, None]
    m = jnp.maximum(jnp.max(s, axis=-1, keepdims=True), sk)
    p = jnp.exp(s - m)
    denom = jnp.sum(p, axis=-1, keepdims=True) + jnp.exp(sk - m)
    p = (p / denom).astype(v.dtype)
    return jnp.einsum('bkgqj,bjkd->bqkgd', p, v)


def gqa_mixer(x, cos, sin, w_qkv, sink, w_o):
    B, L, _ = x.shape
    S = L - N_META
    nb = S // BLOCK
    qd = GQA_Q_HEADS * GQA_HEAD_DIM
    kd = GQA_KV_HEADS * GQA_HEAD_DIM
    qkv = x @ w_qkv
    q = partial_rope(qkv[..., :qd].reshape(B, L, GQA_Q_HEADS, GQA_HEAD_DIM), cos, sin)
    k = partial_rope(qkv[..., qd:qd + kd].reshape(B, L, GQA_KV_HEADS, GQA_HEAD_DIM), cos, sin)
    v = qkv[..., qd + kd:].reshape(B, L, GQA_KV_HEADS, GQA_HEAD_DIM)
    q = q.reshape(B, L, GQA_KV_HEADS, GQA_GROUP, GQA_HEAD_DIM)
    sink = sink.reshape(GQA_KV_HEADS, GQA_GROUP)
    km, vm = k[:, :N_META], v[:, :N_META]
    kr, vr = k[:, N_META:], v[:, N_META:]
    nk = N_META + WINDOW
    meta_mask = jnp.arange(nk)[None, :] <= jnp.arange(N_META)[:, None] + WINDOW
    out_meta = sink_attend(q[:, :N_META], k[:, :nk], v[:, :nk], meta_mask, sink)
    kr_pad = jnp.pad(kr, ((0, 0), (BLOCK, BLOCK), (0, 0), (0, 0)))
    vr_pad = jnp.pad(vr, ((0, 0), (BLOCK, BLOCK), (0, 0), (0, 0)))
    span = 3 * BLOCK
    qi = jnp.arange(BLOCK)[:, None]
    kj = jnp.arange(span)[None, :]
    meta_vis = jnp.ones((BLOCK, N_META), dtype=bool)

    def block_fn(args):
        b, qb = args
        kb = lax.dynamic_slice_in_dim(kr_pad, b * BLOCK, span, axis=1)
        vb = lax.dynamic_slice_in_dim(vr_pad, b * BLOCK, span, axis=1)
        rk = b * BLOCK - BLOCK + kj
        band = (jnp.abs(qi + BLOCK - kj) <= WINDOW) & (rk >= 0) & (rk < S)
        mask = jnp.concatenate([meta_vis, band], axis=1)
        return sink_attend(qb, jnp.concatenate([km, kb], axis=1), jnp.concatenate([vm, vb], axis=1), mask, sink)

    q_blocks = q[:, N_META:].reshape(B, nb, BLOCK, GQA_KV_HEADS, GQA_GROUP, GQA_HEAD_DIM).swapaxes(0, 1)
    out_real = lax.map(block_fn, (jnp.arange(nb), q_blocks)).swapaxes(0, 1).reshape(B, S, qd)
    out = jnp.concatenate([out_meta.reshape(B, N_META, qd), out_real], axis=1)
    return out @ w_o


def sqrelu_mlp(x, w1, w2):
    h = jax.nn.relu(x @ w1)
    return (h * h) @ w2


def trunk(x, meta_tokens, mla_w_in, mla_g_q, mla_w_uq, mla_g_kv, mla_w_ukv, mla_w_o,
          gqa_w_qkv, gqa_sink, gqa_w_o, mlp_w1, mlp_w2, ln1_g, ln1_b, ln2_g, ln2_b):
    B, S, D = x.shape
    meta = jnp.broadcast_to(meta_tokens[None].astype(x.dtype), (B, N_META, D))
    h = jnp.concatenate([meta, x], axis=1)
    L = S + N_META
    cos_a, sin_a = rope_tables(L, MLA_ROPE, x.dtype)
    cos_b, sin_b = rope_tables(L, GQA_ROT, x.dtype)
    for i in range(DEPTH):
        j = i // 2
        if i % 2 == 0:
            mix = mla_mixer(h, cos_a, sin_a, mla_w_in[j], mla_g_q[j], mla_w_uq[j],
                            mla_g_kv[j], mla_w_ukv[j], mla_w_o[j])
        else:
            mix = gqa_mixer(h, cos_b, sin_b, gqa_w_qkv[j], gqa_sink[j], gqa_w_o[j])
        h = layer_norm(DN_ALPHA * h + mix, ln1_g[i], ln1_b[i])
        h = layer_norm(DN_ALPHA * h + sqrelu_mlp(h, mlp_w1[i], mlp_w2[i]), ln2_g[i], ln2_b[i])
    return h[:, N_META:]


def setup_inputs(seed: int = 0) -> dict:
    key = jax.random.key(seed)
    ks = jax.random.split(key, 20)

    def nrm(k, shape, scale):
        return jax.random.normal(k, shape, jnp.float32) * scale

    d = D_MODEL
    a, g = N_MLA_LAYERS, N_GQA_LAYERS
    qkv_out = (GQA_Q_HEADS + 2 * GQA_KV_HEADS) * GQA_HEAD_DIM
    return {
        'x_prompt': nrm(ks[0], (BATCH, SEQ, d), 1.0),
        'x_sample': nrm(ks[1], (DEC_BATCH, DEC_SEQ, d), 1.0),
        'meta_tokens': nrm(ks[2], (N_META, d), 1.0),
        'mla_w_in': nrm(ks[3], (a, d, MLA_Q_RANK + MLA_KV_RANK + MLA_ROPE), d ** -0.5),
        'mla_g_q': 1.0 + nrm(ks[4], (a, MLA_Q_RANK), 0.02),
        'mla_w_uq': nrm(ks[5], (a, MLA_Q_RANK, MLA_HEADS * (MLA_NOPE + MLA_ROPE)), MLA_Q_RANK ** -0.5),
        'mla_g_kv': 1.0 + nrm(ks[6], (a, MLA_KV_RANK), 0.02),
        'mla_w_ukv': nrm(ks[7], (a, MLA_KV_RANK, MLA_HEADS * (MLA_NOPE + MLA_V)), MLA_KV_RANK ** -0.5),
        'mla_w_o': nrm(ks[8], (a, MLA_HEADS * MLA_V, d), DN_BETA * (MLA_HEADS * MLA_V) ** -0.5),
        'gqa_w_qkv': nrm(ks[9], (g, d, qkv_out), d ** -0.5),
        'gqa_sink': nrm(ks[10], (g, GQA_Q_HEADS), 0.5),
        'gqa_w_o': nrm(ks[11], (g, GQA_Q_HEADS * GQA_HEAD_DIM, d), DN_BETA * (GQA_Q_HEADS * GQA_HEAD_DIM) ** -0.5),
        'mlp_w1': nrm(ks[12], (DEPTH, d, D_FF), d ** -0.5),
        'mlp_w2': nrm(ks[13], (DEPTH, D_FF, d), DN_BETA * D_FF ** -0.5),
        'ln1_g': 1.0 + nrm(ks[14], (DEPTH, d), 0.02),
        'ln1_b': nrm(ks[15], (DEPTH, d), 0.02),
        'ln2_g': 1.0 + nrm(ks[16], (DEPTH, d), 0.02),
        'ln2_b': nrm(ks[17], (DEPTH, d), 0.02),
    }


def reference(x_prompt, x_sample, meta_tokens, mla_w_in, mla_g_q, mla_w_uq, mla_g_kv, mla_w_ukv, mla_w_o,
              gqa_w_qkv, gqa_sink, gqa_w_o, mlp_w1, mlp_w2, ln1_g, ln1_b, ln2_g, ln2_b):
    y_prompt = trunk(x_prompt, meta_tokens, mla_w_in, mla_g_q, mla_w_uq, mla_g_kv, mla_w_ukv, mla_w_o,
                     gqa_w_qkv, gqa_sink, gqa_w_o, mlp_w1, mlp_w2, ln1_g, ln1_b, ln2_g, ln2_b)
    y_sample = trunk(x_sample, meta_tokens, mla_w_in, mla_g_q, mla_w_uq, mla_g_kv, mla_w_ukv, mla_w_o,
                     gqa_w_qkv, gqa_sink, gqa_w_o, mlp_w1, mlp_w2, ln1_g, ln1_b, ln2_g, ln2_b)
    return (y_prompt, y_sample)
```

```python
import contextlib
import numpy as np
import ml_dtypes
import concourse.bass as bass
import concourse.mybir as mybir
from concourse.bass_utils import run_bass_kernel_spmd

F32 = mybir.dt.float32
BF16 = mybir.dt.bfloat16
ALU = mybir.AluOpType
AF = mybir.ActivationFunctionType

PE, ACT, DVE, POOL, SP = "tensor", "scalar", "vector", "gpsimd", "sync"
ENGS = [PE, ACT, DVE, POOL, SP]
SEM_EPOCH = 30000

DM = 1024
NMETA = 16
NH = 16
QRANK = 768
KVRANK = 256
DFF = 4096
ALPHA = float((2.0 * 2) ** 0.25)
LN_EPS = 1e-5
RMS_EPS = 1e-6
THETA = 500000.0
SC0 = float(96 ** -0.5)
SC1 = float(64 ** -0.5)
NCORES = 8
AW = 256
BAR_D = False
USE_DIV = False
MERGE_EXP = False
HALFBANK = False
QW = 512


class T:
    __slots__ = ("name", "last_w", "readers")

    def __init__(self, name=""):
        self.name = name
        self.last_w = None
        self.readers = []


class Op:
    __slots__ = ("eng", "fn", "deps", "needs_inc", "is_dma", "semkey", "dma_val", "inc_val")

    def __init__(self, eng, fn):
        self.eng = eng
        self.fn = fn
        self.deps = []
        self.needs_inc = False
        self.is_dma = False
        self.semkey = None
        self.dma_val = 0
        self.inc_val = None


class Prog:
    def __init__(self):
        self.ops = {e: [] for e in ENGS}
        self.dma_last = {}
        self.dma_count = {}
        self.last_op = {e: None for e in ENGS}
        self.pending_dma = []

    def _track(self, op, reads, writes):
        deps = []
        for t in reads:
            if t.last_w is not None:
                deps.append(t.last_w)
        for t in writes:
            if t.last_w is not None:
                deps.append(t.last_w)
            deps.extend(t.readers)
        for t in reads:
            t.readers.append(op)
        for t in writes:
            t.last_w = op
            t.readers = []
        seen = set()
        for d in deps:
            if d is op or id(d) in seen:
                continue
            seen.add(id(d))
            if d.eng == PE and op.eng == PE and not d.is_dma and not op.is_dma:
                continue
            op.deps.append(d)
            if not d.is_dma:
                d.needs_inc = True

    def op(self, eng, fn, reads=(), writes=()):
        o = Op(eng, fn)
        self._track(o, reads, writes)
        self.ops[eng].append(o)
        self.last_op[eng] = o
        return o

    def dma(self, eng, fn, ndma, semkey, reads=(), writes=()):
        o = Op(eng, fn)
        o.is_dma = True
        o.semkey = semkey
        prev = self.dma_last.get(semkey)
        self._track(o, reads, writes)
        if prev is not None and all(d is not prev for d in o.deps):
            o.deps.append(prev)
        self.dma_last[semkey] = o
        self.dma_count[semkey] = self.dma_count.get(semkey, 0) + 16 * ndma
        o.dma_val = self.dma_count[semkey]
        self.ops[eng].append(o)
        self.pending_dma.append(o)
        return o

    def barrier(self):
        lasts = [self.last_op[e] for e in ENGS if self.last_op[e] is not None and not self.last_op[e].is_dma]
        comp = []
        for e in ENGS:
            for o in reversed(self.ops[e]):
                if not o.is_dma and o.fn is not None:
                    comp.append(o)
                    break
        dmas = list(self.pending_dma)
        self.pending_dma = []
        for e in ENGS:
            b = Op(e, None)
            for d in comp:
                if d.eng != e:
                    b.deps.append(d)
                    d.needs_inc = True
            b.deps.extend(dmas)
            self.ops[e].append(b)


def build_block(nc, prog, final_dmas):
    counts = {e: 0 for e in ENGS}
    for e in ENGS:
        for o in prog.ops[e]:
            if (not o.is_dma) and o.needs_inc:
                counts[e] += 1
                o.inc_val = counts[e]
    with contextlib.ExitStack() as st:
        esems = {}
        for e in ENGS:
            n_ep = counts[e] // SEM_EPOCH + 1
            esems[e] = [st.enter_context(nc.semaphore(f"s_{e}_{k}")) for k in range(n_ep)]
        dsems = {}
        for k in prog.dma_count:
            dsems[k] = st.enter_context(nc.semaphore(f"d_{len(dsems)}"))
        block = st.enter_context(nc.Block())

        def sem_of(op):
            if op.is_dma:
                return dsems[op.semkey], op.dma_val, ("d", op.semkey)
            ep = (op.inc_val - 1) // SEM_EPOCH
            return esems[op.eng][ep], op.inc_val - ep * SEM_EPOCH, (op.eng, ep)

        def make(e):
            def body(eng):
                seen = {}
                for o in prog.ops[e]:
                    for d in o.deps:
                        sem, val, key = sem_of(d)
                        if seen.get(key, 0) >= val:
                            continue
                        eng.wait_ge(sem, val)
                        seen[key] = val
                    if o.fn is None:
                        continue
                    if o.is_dma:
                        o.fn(eng, dsems[o.semkey])
                    else:
                        inst = o.fn(eng)
                        if o.needs_inc:
                            ep = (o.inc_val - 1) // SEM_EPOCH
                            inst.then_inc(esems[e][ep], 1)
                if e == SP:
                    for d in final_dmas:
                        sem, val, key = sem_of(d)
                        eng.wait_ge(sem, val)
            return body

        block.tensor(make(PE))
        block.scalar(make(ACT))
        block.vector(make(DVE))
        block.gpsimd(make(POOL))
        block.sync(make(SP))


def tile_w(W, cps):
    Kd, Nd = W.shape
    nkc = Kd // 128
    ns = Nd // cps
    a = W.reshape(nkc, 128, ns, cps).transpose(2, 1, 0, 3)
    return np.ascontiguousarray(a).reshape(-1)


WSPEC = {}


def build_wall(inp):
    parts = []
    off = 0

    def add(name, W, cps):
        nonlocal off
        W = np.asarray(W, np.float32)
        flat = tile_w(W, cps)
        WSPEC[name] = (off, W.shape[0] // 128, cps, W.shape[1] // cps)
        parts.append(flat)
        off += flat.size

    w_in = inp["mla_w_in"][0]
    p32 = (np.arange(32) + 16) % 32
    add("win", np.concatenate([w_in, w_in[:, 1024 + p32]], axis=1), 1088)
    wuq = inp["mla_w_uq"][0].reshape(QRANK, NH, 96)
    add("wqr", np.concatenate([wuq[:, :, 64:].reshape(QRANK, 512), wuq[:, :, 64 + p32].reshape(QRANK, 512)], axis=1), 1024)
    add("wqn", wuq[:, :, :64].reshape(QRANK, 1024), 128)
    wukv = inp["mla_w_ukv"][0].reshape(KVRANK, NH, 128)
    add("wkn", wukv[:, :, :64].reshape(KVRANK, 1024), 128)
    add("wvv", wukv[:, :, 64:].reshape(KVRANK, 1024), 128)
    add("wo0", inp["mla_w_o"][0], 512)
    for l in range(2):
        add(f"w1_{l}", inp["mlp_w1"][l], 512)
        add(f"w2_{l}", inp["mlp_w2"][l], 128)
    wqkv = inp["gqa_w_qkv"][0]
    p64 = np.arange(64)
    p64[:16] = (np.arange(16) + 8) % 16
    wq = wqkv[:, :1024].reshape(DM, 16, 64)
    add("wq1", wq.reshape(DM, 1024), 512)
    add("wq1s", wq[:, :, p64].reshape(DM, 1024), 512)
    wk = wqkv[:, 1024:1280].reshape(DM, 4, 64)
    wkd = np.concatenate([wk, wk], axis=2)
    wks = wk[:, :, p64]
    wksd = np.concatenate([wks, wks], axis=2)
    add("wk1", wkd.reshape(DM, 512), 512)
    add("wk1s", wksd.reshape(DM, 512), 512)
    add("wv1", wqkv[:, 1280:1536], 256)
    add("wo1", inp["gqa_w_o"][0], 512)
    tot = off
    blk = 128 * 2048
    nb = (tot + blk - 1) // blk
    wall = np.zeros(nb * blk, np.float32)
    wall[:tot] = np.concatenate(parts)
    return wall, nb


def rope_cs(pos, dim):
    inv = THETA ** (-np.arange(0, dim, 2, dtype=np.float32) / np.float32(dim))
    ang = pos.astype(np.float32)[:, None] * inv[None, :].astype(np.float32)
    return np.cos(ang).astype(np.float32), np.sin(ang).astype(np.float32)


def mla_tabs(pos):
    c, s = rope_cs(pos, 32)
    C32 = np.concatenate([c, c], axis=1).T
    S32 = np.concatenate([-s, s], axis=1).T
    return np.tile(C32, (4, 1)), np.tile(S32, (4, 1))


def gqa_tabs(pos):
    c, s = rope_cs(pos, 16)
    L = pos.shape[0]
    C64 = np.ones((64, L), np.float32)
    S64 = np.zeros((64, L), np.float32)
    C64[:8] = c.T
    C64[8:16] = c.T
    S64[:8] = -s.T
    S64[8:16] = s.T
    return np.tile(C64, (2, 1)), np.tile(S64, (2, 1))


class Job:
    def __init__(self, n_pre, n_own, n_post, Lk, xoff, t0off, t1off, yidx, special_masks):
        self.n_pre, self.n_own, self.n_post, self.Lk = n_pre, n_own, n_post, Lk
        self.R = n_pre + n_own + n_post
        self.Lq = self.R + NMETA
        self.meta = self.R
        self.xoff, self.t0off, self.t1off, self.yidx = xoff, t0off, t1off, yidx
        self.special = special_masks


LP = 2064
LSQ = 2320
LSK = 4112
XCOLS = 4 * LP + LSK
TAB_P0 = 0
TAB_S0 = 2 * LP
TAB_P1 = TAB_S0 + 2 * LSK
TAB_S1 = TAB_P1 + 2 * LP
TABW = TAB_S1 + 2 * LSQ
NVEC = 88


def chunks(lo, hi, w):
    out = []
    c = lo
    while c < hi:
        out.append((c, min(w, hi - c)))
        c += w
    return out


def build_program(nblk, dbg=None):
    nc = bass.Bass("TRN2", target_bir_lowering=False)
    blk = 128 * 2048
    xT = nc.dram_tensor("xT", [DM, XCOLS], F32, kind="ExternalInput")
    wall = nc.dram_tensor("wall", [nblk * blk], F32, kind="ExternalInput")
    tabs = nc.dram_tensor("tabs", [128, TABW], F32, kind="ExternalInput")
    masks = nc.dram_tensor("masks", [128, 4 * 512], F32, kind="ExternalInput")
    vecd = nc.dram_tensor("vec", [128, NVEC], F32, kind="ExternalInput")
    yT = nc.dram_tensor("yT", [5, DM, 2048], F32, kind="ExternalOutput")
    wbf = nc.dram_tensor("wbf", [nblk * blk], BF16)
    dk = dict(kind="ExternalOutput") if dbg else {}
    h2s = nc.dram_tensor("h2s", [DM, 2048], F32, **dk)
    h2b = nc.dram_tensor("h2b", [DM, LSQ], BF16, **dk)
    if dbg:
        d_cqn = nc.dram_tensor("d_cqn", [128, 6 * LSQ], BF16, kind="ExternalOutput")
        d_ckvn = nc.dram_tensor("d_ckvn", [128, 2 * LSK], BF16, kind="ExternalOutput")
        d_qr = nc.dram_tensor("d_qr", [128, 4 * LSQ], BF16, kind="ExternalOutput")
        d_k0 = nc.dram_tensor("d_k0", [128, LSK], BF16, kind="ExternalOutput")
        d_attn = nc.dram_tensor("d_attn", [128, 8 * LSQ], BF16, kind="ExternalOutput")

    def DAP(t, off, pairs):
        return bass.AP(t, off, [list(p) for p in pairs])

    jobs = [Job(0, 2048, 0, LP, i * LP, TAB_P0, TAB_P1, i, False) for i in range(4)]
    jobs.append(Job(128, 2048, 128, LSK, 4 * LP, TAB_S0, TAB_S1, 4, True))

    st = contextlib.ExitStack()
    with st:
        ARB = 206 * 1024
        arena = st.enter_context(nc.sbuf_tensor("arena", [128, ARB // 2], BF16))
        psall = st.enter_context(nc.psum_tensor("psall", [128, 4096], F32))
        ps = [psall[:, i * 512:(i + 1) * 512] for i in range(8)]
        tps = [T(f"ps{i}") for i in range(8)]
        p = Prog()

        class Alloc:
            def __init__(self, base):
                self.o = base

            def get(self, nbytes):
                o = self.o
                self.o += (nbytes + 63) // 64 * 64
                assert self.o <= ARB, ("SBUF arena overflow", self.o)
                return o

        def vw(off, dt, shape):
            n = int(np.prod(shape))
            if dt == BF16:
                a = arena[:, off // 2: off // 2 + n]
            else:
                a = arena[:, off // 2: off // 2 + 2 * n].bitcast(F32)
            if len(shape) == 2:
                return a.rearrange("p (a b) -> p a b", a=shape[0])
            if len(shape) == 3:
                return a.rearrange("p (a b c) -> p a b c", a=shape[0], b=shape[1])
            return a

        def buf(al, dt, shape):
            nb = int(np.prod(shape)) * (2 if dt == BF16 else 4)
            return vw(al.get(nb), dt, shape)

        com = Alloc(0)
        ones_bf = buf(com, BF16, [128])
        masks_bf = buf(com, BF16, [4, 512])
        estab = buf(com, F32, [4, 512])
        vec = buf(com, F32, [NVEC])
        PT = [buf(com, BF16, [512]) for _ in range(4)]
        tPT = [T() for _ in range(4)]
        tsh = [[T() for _ in range(2)] for _ in range(4)]
        t1b = [buf(com, F32, [512]) for _ in range(2)]
        t2b = [buf(com, F32, [512]) for _ in range(2)]
        tt1 = [T() for _ in range(2)]
        tt2 = [T() for _ in range(2)]
        rlb = [buf(com, F32, [512]) for _ in range(2)]
        trl = [T() for _ in range(2)]
        vbc = [buf(com, BF16, [512]) for _ in range(2)]
        sqc = [buf(com, BF16, [512]) for _ in range(2)]
        tvbc = [T() for _ in range(2)]
        tsqc = [T() for _ in range(2)]
        st_mean = buf(com, F32, [512])
        st_msq = buf(com, F32, [512])
        st_sd = buf(com, F32, [512])
        st_sd2 = buf(com, F32, [512])
        t_mean, t_msq, t_sd, t_sd2 = T(), T(), T(), T()
        tmpd = [buf(com, F32, [512]) for _ in range(2)]
        ttmpd = [T() for _ in range(2)]
        dtb = rlb
        tdtb = trl
        pw = [buf(com, BF16, [10, 128]) for _ in range(2)]
        tpw = [T() for _ in range(2)]
        t_const = T("const")
        COM_END = com.o

        main = Alloc(COM_END)
        attn_off = main.get(8 * LSQ * 2)
        attn_out = vw(attn_off, BF16, [8, LSQ])
        t_attn = [T() for _ in range(8)]
        AB0 = main.o
        ab = Alloc(AB0)
        cqn = buf(ab, BF16, [6, LSQ]); t_cqn = T()
        ckvn = buf(ab, BF16, [2, LSK]); t_ckvn = T()
        qr = buf(ab, BF16, [4, LSQ]); t_qr = T()
        Kb = [buf(ab, BF16, [LSK]) for _ in range(2)]; tK = [T() for _ in range(2)]; tKr = [T() for _ in range(2)]
        QV0 = ab.o
        Qb = [buf(ab, BF16, [LSQ]) for _ in range(2)]; tQ = [T() for _ in range(2)]
        NKT_MAX = (LSK + 127) // 128
        Vb = [buf(ab, BF16, [NKT_MAX, 192]) for _ in range(2)]; tV = [T() for _ in range(2)]
        AB_END = ab.o
        at = Alloc(attn_off)
        win = buf(at, BF16, [8, 1088]); t_win = T()
        wqr = buf(at, BF16, [6, 1024]); t_wqr = T()
        assert at.o <= AB0, at.o
        at2 = Alloc(QV0)
        xf = [buf(at2, F32, [8, AW]) for _ in range(1)] * 2; txf = [T()] * 2
        xb = [buf(at2, BF16, [8, AW]) for _ in range(2)]; txb = [T() for _ in range(2)]
        csb = buf(at2, F32, [8, AW]); tcsb = [T() for _ in range(8)]
        sqa = buf(at2, BF16, [8, AW]); tsqa = [T() for _ in range(8)]
        ctab = [buf(at2, F32, [AW]) for _ in range(2)]; stab = [buf(at2, F32, [AW]) for _ in range(2)]
        ttab = [T() for _ in range(2)]
        rq = st_mean; rkv = st_msq; t_rq, t_rkv = t_mean, t_msq
        A_END = at2.o
        assert A_END <= AB_END, (A_END, AB_END)
        pr = Alloc(AB0)
        NSTG = 6
        stg_f = [buf(pr, F32, [2048]) for _ in range(NSTG)]; tsf = [T() for _ in range(NSTG)]
        stg_b = [buf(pr, BF16, [2048]) for _ in range(NSTG)]; tsb = [T() for _ in range(NSTG)]
        cd = Alloc(AB0)
        slot = [buf(cd, F32, [8, 512]) for _ in range(2)]; tslot = [[T() for _ in range(8)] for _ in range(2)]
        hb = buf(cd, BF16, [8, 512]); thb = [T() for _ in range(8)]
        hid = buf(cd, BF16, [32, 512]); thid = [T() for _ in range(32)]
        wsb = [buf(cd, BF16, [4096]) for _ in range(3)]; tws = [T() for _ in range(3)]
        CD_END = cd.o
        cx = Alloc(CD_END)
        hb2 = buf(cx, BF16, [8, 512]); thb2 = [T() for _ in range(8)]
        hbs = buf(cx, BF16, [8, 512]); thbs = [T() for _ in range(8)]
        assert cx.o <= ARB, cx.o
        dx = Alloc(attn_off)
        WIN_MAX = 768 + NMETA
        hbw = buf(dx, BF16, [8, WIN_MAX]); thbw = T()
        q1 = buf(dx, BF16, [8, 512]); tq1 = T()
        k1 = buf(dx, BF16, [4, WIN_MAX]); tk1 = T()
        a1 = buf(dx, BF16, [8, 512]); ta1 = [T() for _ in range(8)]
        assert dx.o <= AB0, dx.o
        dx2 = Alloc(CD_END)
        v1a = buf(dx2, BF16, [7, 4, 192]); tv1 = T()
        c1t = buf(dx2, F32, [WIN_MAX]); s1t = buf(dx2, F32, [WIN_MAX]); tt1t = T()
        D_END = dx2.o
        print('SBUF', COM_END, AB0, AB_END, A_END, CD_END, D_END, ARB)
        assert max(A_END, D_END, AB_END) <= ARB

        state = {"ps": 0, "ws": 0, "t": 0, "rl": 0, "vs": 0, "pt": 0, "td": 0}

        def nps(lo=0, hi=8):
            i = lo + state["ps"] % (hi - lo)
            state["ps"] += 1
            return i

        def rr(key, n):
            i = state[key] % n
            state[key] += 1
            return i

        t_wbf = T("wbf")
        t_wbf_pro = [T("wbf_pro") for _ in range(8)]
        t_h2s, t_h2b = T("h2s"), T("h2b")

        def wslice(name, s):
            off, nkc, cps, ns = WSPEC[name]
            i = rr("ws", 3)
            n = nkc * cps
            src = DAP(wbf, off + s * 128 * n, [[n, 128], [1, n]])
            dst = wsb[i][:, 0:n]
            if not (dbg and dbg.get("nows") and state["ws"] > 3):
                p.dma(SP, lambda e, sem, dst=dst, src=src: e.dma_start(out=dst, in_=src).then_inc(sem, 16), 1, f"ws{i}",
                      reads=[t_wbf], writes=[tws[i]])
            return wsb[i][:, 0:n].rearrange("p (k c) -> p k c", k=nkc), tws[i]

        def mm_group(bank, rows, w, pieces, reads):
            def f(e, bank=bank, rows=rows, w=w, pieces=pieces):
                n = len(pieces)
                for i, (l, r) in enumerate(pieces):
                    inst = e.matmul(ps[bank][0:rows, 0:w], lhsT=l, rhs=r, start=(i == 0), stop=(i == n - 1))
                return inst
            return p.op(PE, f, reads=reads, writes=[tps[bank]])

        def ld_consts():
            mf = stg_f[0][:, 0:2048]
            p.dma(SP, lambda e, s: e.dma_start(out=mf, in_=masks.ap()).then_inc(s, 16), 1, "sf0", writes=[tsf[0]])
            p.op(DVE, lambda e: e.tensor_copy(out=masks_bf.rearrange("p a b -> p (a b)"), in_=mf), reads=[tsf[0]], writes=[t_const])
            p.dma(SP, lambda e, s: e.dma_start(out=vec, in_=vecd.ap()).then_inc(s, 16), 1, "vec", writes=[t_const])
            p.op(DVE, lambda e: e.memset(ones_bf, 1.0), writes=[t_const])
            p.op(ACT, lambda e: e.activation(out=vec[:, 72:88], in_=vec[:, 72:88], func=AF.Exp), reads=[t_const], writes=[t_const])
            p.op(DVE, lambda e: e.memset(estab.rearrange("p a b -> p (a b)"), 0.0), reads=[t_const], writes=[t_const])
            for kv in range(4):
                for bi, g in enumerate([0, 2, 1, 3]):
                    h = 4 * kv + g
                    p.op(DVE, lambda e, kv=kv, bi=bi, h=h: e.tensor_scalar(
                        out=estab[:, kv, bi * 128:(bi + 1) * 128], in0=estab[:, kv, bi * 128:(bi + 1) * 128],
                        scalar1=vec[:, 72 + h:73 + h], scalar2=None, op0=ALU.add), reads=[t_const], writes=[t_const])

        def prologue():
            engs = [DVE, POOL, ACT]
            for b in range(nblk):
                i = b % NSTG
                src = DAP(wall, b * blk, [[2048, 128], [1, 2048]])
                dstd = DAP(wbf, b * blk, [[2048, 128], [1, 2048]])
                p.dma(SP, lambda e, s, i=i, src=src: e.dma_start(out=stg_f[i], in_=src).then_inc(s, 16), 1, f"sf{i}", writes=[tsf[i]])
                en = engs[b % 3]
                if en == ACT:
                    p.op(ACT, lambda e, i=i: e.activation(out=stg_b[i], in_=stg_f[i], func=AF.Copy), reads=[tsf[i]], writes=[tsb[i]])
                else:
                    p.op(en, lambda e, i=i: e.tensor_copy(out=stg_b[i], in_=stg_f[i]), reads=[tsf[i]], writes=[tsb[i]])
                p.dma(ACT, lambda e, s, i=i, dstd=dstd: e.dma_start(out=dstd, in_=stg_b[i]).then_inc(s, 16), 1, f"sb{i}",
                      reads=[tsb[i]], writes=[t_wbf_pro[i]])

        def layernorm(si, w, gcol, bcol, want_hb=True, hbo=None, thbo=None):
            hbo = hb if hbo is None else hbo
            thbo = thb if thbo is None else thbo
            S = slot[si]
            bs = nps(6, 8)
            bq = nps(6, 8)
            pcs_s, pcs_q = [], []
            for oc in range(8):
                j = rr("vs", 2)
                p.op(ACT, lambda e, j=j, oc=oc: e.activation(out=vbc[j][:, 0:w], in_=S[:, oc, 0:w], func=AF.Copy), reads=[tslot[si][oc]], writes=[tvbc[j]])
                p.op(DVE, lambda e, j=j, oc=oc: e.tensor_tensor(out=sqc[j][:, 0:w], in0=S[:, oc, 0:w], in1=S[:, oc, 0:w], op=ALU.mult),
                     reads=[tslot[si][oc]], writes=[tsqc[j]])
                p.op(PE, lambda e, j=j, oc=oc: e.matmul(ps[bs][:, 0:w], lhsT=ones_bf, rhs=vbc[j][:, 0:w], start=(oc == 0), stop=(oc == 7)),
                     reads=[tvbc[j], t_const], writes=[tps[bs]])
                p.op(PE, lambda e, j=j, oc=oc: e.matmul(ps[bq][:, 0:w], lhsT=ones_bf, rhs=sqc[j][:, 0:w], start=(oc == 0), stop=(oc == 7)),
                     reads=[tsqc[j], t_const], writes=[tps[bq]])
            p.op(DVE, lambda e: e.tensor_scalar(out=st_mean[:, 0:w], in0=ps[bs][:, 0:w], scalar1=1.0 / DM, scalar2=None, op0=ALU.mult),
                 reads=[tps[bs]], writes=[t_mean])
            p.op(DVE, lambda e: e.tensor_tensor(out=st_msq[:, 0:w], in0=st_mean[:, 0:w], in1=st_mean[:, 0:w], op=ALU.mult),
                 reads=[t_mean], writes=[t_msq])
            p.op(DVE, lambda e: e.scalar_tensor_tensor(out=st_sd2[:, 0:w], in0=ps[bq][:, 0:w], scalar=1.0 / DM, in1=st_msq[:, 0:w],
                                                       op0=ALU.mult, op1=ALU.subtract), reads=[tps[bq], t_msq], writes=[t_sd2])
            p.op(ACT, lambda e: e.activation(out=st_sd[:, 0:w], in_=st_sd2[:, 0:w], func=AF.Sqrt, bias=LN_EPS, scale=1.0),
                 reads=[t_sd2], writes=[t_sd])
            if not USE_DIV:
                p.op(DVE, lambda e: e.reciprocal(out=st_sd2[:, 0:w], in_=st_sd[:, 0:w]), reads=[t_sd], writes=[t_sd2])
            for oc in range(8):
                p.op(DVE, lambda e, oc=oc: e.tensor_tensor(out=S[:, oc, 0:w], in0=S[:, oc, 0:w], in1=st_mean[:, 0:w], op=ALU.subtract),
                     reads=[t_mean, tslot[si][oc]], writes=[tslot[si][oc]])
                if USE_DIV:
                    p.op(DVE, lambda e, oc=oc: e.tensor_tensor(out=S[:, oc, 0:w], in0=S[:, oc, 0:w], in1=st_sd[:, 0:w], op=ALU.divide),
                         reads=[t_sd, tslot[si][oc]], writes=[tslot[si][oc]])
                else:
                    p.op(DVE, lambda e, oc=oc: e.tensor_tensor(out=S[:, oc, 0:w], in0=S[:, oc, 0:w], in1=st_sd2[:, 0:w], op=ALU.mult),
                         reads=[t_sd2, tslot[si][oc]], writes=[tslot[si][oc]])
                p.op(ACT, lambda e, oc=oc: e.activation(out=S[:, oc, 0:w], in_=S[:, oc, 0:w], func=AF.Identity,
                                                        bias=vec[:, bcol + oc:bcol + oc + 1], scale=vec[:, gcol + oc:gcol + oc + 1]),
                     reads=[t_const, tslot[si][oc]], writes=[tslot[si][oc]])
                if want_hb:
                    p.op(ACT, lambda e, oc=oc: e.activation(out=hbo[:, oc, 0:w], in_=S[:, oc, 0:w], func=AF.Copy),
                         reads=[tslot[si][oc]], writes=[thbo[oc]])

        def proj_resid(si, w, wname, src_fn, src_reads, pre=None):
            for s in range(2):
                wv, tw = pre[s] if pre is not None else wslice(wname, s)
                for m in range(4):
                    oc = 4 * s + m
                    bk = nps(0, 6)
                    mm_group(bk, 128, w, [(wv[:, kc, m * 128:(m + 1) * 128], src_fn(kc)) for kc in range(8)], [tw] + src_reads)
                    p.op(DVE, lambda e, oc=oc, bk=bk: e.scalar_tensor_tensor(
                        out=slot[si][:, oc, 0:w], in0=slot[si][:, oc, 0:w], scalar=ALPHA, in1=ps[bk][:, 0:w],
                        op0=ALU.mult, op1=ALU.add), reads=[tps[bk], tslot[si][oc]], writes=[tslot[si][oc]])

        def mlp_w1(si, w, l, hbi=None, thbi=None):
            hbi = hb if hbi is None else hbi
            thbi = thb if thbi is None else thbi
            for s in range(8):
                wv, tw = wslice(f"w1_{l}", s)
                for m in range(4):
                    hc = 4 * s + m
                    bk = nps(0, 6)
                    mm_group(bk, 128, w, [(wv[:, kc, m * 128:(m + 1) * 128], hbi[:, kc, 0:w]) for kc in range(8)], [tw] + thbi)
                    j = rr("rl", 2)
                    p.op(ACT, lambda e, j=j, bk=bk: e.activation(out=rlb[j][:, 0:w], in_=ps[bk][:, 0:w], func=AF.Relu),
                         reads=[tps[bk]], writes=[trl[j]])
                    p.op(POOL if hc % 4 == 3 else DVE, lambda e, j=j, hc=hc: e.tensor_tensor(out=hid[:, hc, 0:w], in0=rlb[j][:, 0:w], in1=rlb[j][:, 0:w], op=ALU.mult),
                         reads=[trl[j]], writes=[thid[hc]])

        def mlp_w2(si, w, l):
            for oc in range(8):
                wv, tw = wslice(f"w2_{l}", oc)
                bk = nps(0, 6)
                mm_group(bk, 128, w, [(wv[:, hc, :], hid[:, hc, 0:w]) for hc in range(32)], [tw] + thid)
                p.op(DVE, lambda e, oc=oc, bk=bk: e.scalar_tensor_tensor(
                    out=slot[si][:, oc, 0:w], in0=slot[si][:, oc, 0:w], scalar=ALPHA, in1=ps[bk][:, 0:w],
                    op0=ALU.mult, op1=ALU.add), reads=[tps[bk], tslot[si][oc]], writes=[tslot[si][oc]])

        def mlp(si, w, l):
            mlp_w1(si, w, l)
            mlp_w2(si, w, l)

        def phase_A(J):
            state['dmode'] = False
            o, nkc, cps, ns = WSPEC["win"]
            p.dma(SP, lambda e, s: e.dma_start(out=win.rearrange("p a b -> p (a b)"), in_=DAP(wbf, o, [[8 * 1088, 128], [1, 8 * 1088]])).then_inc(s, 16),
                  1, "win", reads=[t_wbf], writes=[t_win])
            o2 = WSPEC["wqr"][0]
            p.dma(SP, lambda e, s: e.dma_start(out=wqr.rearrange("p a b -> p (a b)"), in_=DAP(wbf, o2, [[6 * 1024, 128], [1, 6 * 1024]])).then_inc(s, 16),
                  1, "wqr", reads=[t_wbf], writes=[t_wqr])
            ci = 0
            for (c0, w) in chunks(0, J.Lq, AW) + chunks(J.Lq, J.Lk, AW):
                is_q = c0 < J.Lq
                i = ci % 2
                ci += 1
                src = DAP(xT, J.xoff + c0, [[XCOLS, 128], [128 * XCOLS, 8], [1, w]])
                p.dma(SP, lambda e, s, i=i, src=src, w=w: e.dma_start(out=xf[i][:, :, 0:w], in_=src).then_inc(s, 16), 1, "xf0", writes=[txf[i]])
                tc_src = DAP(tabs, J.t0off + c0, [[TABW, 128], [1, w]])
                ts_src = DAP(tabs, J.t0off + J.Lk + c0, [[TABW, 128], [1, w]])

                def ldt(e, s, i=i, w=w, tc_src=tc_src, ts_src=ts_src):
                    e.dma_start(out=ctab[i][:, 0:w], in_=tc_src).then_inc(s, 16)
                    e.dma_start(out=stab[i][:, 0:w], in_=ts_src).then_inc(s, 16)
                p.dma(SP, ldt, 2, f"tab{i}", writes=[ttab[i]])
                p.op(ACT, lambda e, i=i, w=w: e.activation(out=xb[i][:, :, 0:w], in_=xf[i][:, :, 0:w], func=AF.Copy), reads=[txf[i]], writes=[txb[i]])
                mlist = (list(range(6)) if is_q else []) + [6, 7]
                for m in mlist:
                    bk = nps(0, 5)
                    mm_group(bk, 128, w, [(win[:, kc, m * 128:(m + 1) * 128], xb[i][:, kc, 0:w]) for kc in range(8)], [t_win, txb[i]])
                    p.op(DVE, lambda e, m=m, bk=bk, w=w: e.tensor_copy(out=csb[:, m, 0:w], in_=ps[bk][:, 0:w]), reads=[tps[bk]], writes=[tcsb[m]])
                    p.op(DVE, lambda e, m=m, w=w: e.tensor_tensor(out=sqa[:, m, 0:w], in0=csb[:, m, 0:w], in1=csb[:, m, 0:w], op=ALU.mult),
                         reads=[tcsb[m]], writes=[tsqa[m]])
                groups = ([("q", range(6), 1.0 / QRANK, rq, t_rq)] if is_q else []) + [("kv", range(6, 8), 1.0 / KVRANK, rkv, t_rkv)]
                for (nm, ms, sc, rbuf, trb) in groups:
                    ms = list(ms)
                    bk = nps(5, 8)
                    mm_group(bk, 128, w, [(ones_bf, sqa[:, m, 0:w]) for m in ms], [t_const] + [tsqa[m] for m in ms])
                    p.op(ACT, lambda e, bk=bk, sc=sc, rbuf=rbuf, w=w: e.activation(out=rbuf[:, 0:w], in_=ps[bk][:, 0:w], func=AF.Sqrt, bias=RMS_EPS, scale=sc),
                         reads=[tps[bk]], writes=[trb])
                    p.op(DVE, lambda e, rbuf=rbuf, w=w: e.reciprocal(out=rbuf[:, 0:w], in_=rbuf[:, 0:w]), reads=[trb], writes=[trb])
                    for m in ms:
                        if m < 6:
                            dst, td, gc = cqn[:, m, c0:c0 + w], t_cqn, m
                        else:
                            dst, td, gc = ckvn[:, m - 6, c0:c0 + w], t_ckvn, m
                        p.op(DVE, lambda e, m=m, dst=dst, gc=gc, rbuf=rbuf, w=w: e.scalar_tensor_tensor(
                            out=dst, in0=csb[:, m, 0:w], scalar=vec[:, gc:gc + 1], in1=rbuf[:, 0:w], op0=ALU.mult, op1=ALU.mult),
                            reads=[tcsb[m], trb, t_const], writes=[td])
                b1 = nps(0, 5)
                mm_group(b1, 96, w, [(win[:, kc, 960:1056], xb[i][:, kc, 0:w]) for kc in range(8)], [t_win, txb[i]])
                b2 = nps(0, 5)
                mm_group(b2, 96, w, [(win[:, kc, 992:1088], xb[i][:, kc, 0:w]) for kc in range(8)], [t_win, txb[i]])
                j = rr("t", 2)
                p.op(DVE, lambda e, j=j, b1=b1, i=i, w=w: e.tensor_tensor(out=t1b[j][64:96, 0:w], in0=ps[b1][64:96, 0:w], in1=ctab[i][64:96, 0:w], op=ALU.mult),
                     reads=[tps[b1], ttab[i]], writes=[tt1[j]])
                p.op(DVE, lambda e, j=j, b2=b2, i=i, w=w: e.tensor_tensor(out=t2b[j][64:96, 0:w], in0=ps[b2][64:96, 0:w], in1=stab[i][64:96, 0:w], op=ALU.mult),
                     reads=[tps[b2], ttab[i]], writes=[tt2[j]])
                for kb in range(2):
                    p.op(POOL, lambda e, j=j, kb=kb, c0=c0, w=w: e.tensor_tensor(out=Kb[kb][64:96, c0:c0 + w], in0=t1b[j][64:96, 0:w], in1=t2b[j][64:96, 0:w], op=ALU.add),
                         reads=[tt1[j], tt2[j]], writes=[tKr[kb]])
                if is_q:
                    for rc in range(4):
                        ba = nps(0, 5)
                        mm_group(ba, 128, w, [(wqr[:, kc, rc * 128:(rc + 1) * 128], cqn[:, kc, c0:c0 + w]) for kc in range(6)], [t_wqr, t_cqn])
                        bb = nps(0, 5)
                        mm_group(bb, 128, w, [(wqr[:, kc, 512 + rc * 128:512 + (rc + 1) * 128], cqn[:, kc, c0:c0 + w]) for kc in range(6)], [t_wqr, t_cqn])
                        j = rr("t", 2)
                        p.op(DVE, lambda e, j=j, ba=ba, i=i, w=w: e.tensor_tensor(out=t1b[j][:, 0:w], in0=ps[ba][:, 0:w], in1=ctab[i][:, 0:w], op=ALU.mult),
                             reads=[tps[ba], ttab[i]], writes=[tt1[j]])
                        p.op(DVE, lambda e, j=j, bb=bb, i=i, w=w: e.tensor_tensor(out=t2b[j][:, 0:w], in0=ps[bb][:, 0:w], in1=stab[i][:, 0:w], op=ALU.mult),
                             reads=[tps[bb], ttab[i]], writes=[tt2[j]])
                        p.op(POOL, lambda e, j=j, rc=rc, c0=c0, w=w: e.tensor_tensor(out=qr[:, rc, c0:c0 + w], in0=t1b[j][:, 0:w], in1=t2b[j][:, 0:w], op=ALU.add),
                             reads=[tt1[j], tt2[j]], writes=[t_qr])

        def phase_B(J):
            nkt = (J.Lk + 127) // 128
            ktiles = [(t * 128, min(128, J.Lk - t * 128)) for t in range(nkt)]
            qch = chunks(0, J.Lq, QW)
            kch = chunks(0, J.Lk, QW)
            LA = 2
            for vb_ in range(2):
                p.op(POOL, lambda e, vb_=vb_: e.memset(Vb[vb_][:, :, 64:128], 1.0), writes=[tV[vb_]])
            oq, ok, ov = WSPEC["wqn"][0], WSPEC["wkn"][0], WSPEC["wvv"][0]

            def load_pw(pj):
                pi = pj % 2

                def ldp(e, s, pi=pi, pj=pj):
                    e.dma_start(out=pw[pi][:, 0:6, :], in_=DAP(wbf, oq + pj * 128 * 768, [[768, 128], [128, 6], [1, 128]])).then_inc(s, 16)
                    e.dma_start(out=pw[pi][:, 6:8, :], in_=DAP(wbf, ok + pj * 128 * 256, [[256, 128], [128, 2], [1, 128]])).then_inc(s, 16)
                    e.dma_start(out=pw[pi][:, 8:10, :], in_=DAP(wbf, ov + pj * 128 * 256, [[256, 128], [128, 2], [1, 128]])).then_inc(s, 16)
                p.dma(SP, ldp, 3, f"pw{pi}", reads=[t_wbf], writes=[tpw[pi]])

            def proj_head(h):
                pj, hh = h // 2, h % 2
                pi = pj % 2
                b = h % 2
                src = qr[(h % 4) * 32:(h % 4) * 32 + 32, h // 4, 0:J.Lq]
                p.dma(SP, lambda e, s, b=b, src=src: e.dma_start(out=Qb[b][64:96, 0:J.Lq], in_=src).then_inc(s, 16), 1, f"qrope{b}",
                      reads=[t_qr], writes=[tQ[b]])
                for (c0, w) in qch:
                    bk = nps(5, 8)
                    mm_group(bk, 64, w, [(pw[pi][:, kc, hh * 64:(hh + 1) * 64], cqn[:, kc, c0:c0 + w]) for kc in range(6)], [tpw[pi], t_cqn])
                    p.op(DVE, lambda e, bk=bk, c0=c0, w=w, b=b: e.tensor_copy(out=Qb[b][0:64, c0:c0 + w], in_=ps[bk][0:64, 0:w]), reads=[tps[bk]], writes=[tQ[b]])
                for (c0, w) in kch:
                    bk = nps(5, 8)
                    mm_group(bk, 64, w, [(pw[pi][:, 6 + kc, hh * 64:(hh + 1) * 64], ckvn[:, kc, c0:c0 + w]) for kc in range(2)], [tpw[pi], t_ckvn])
                    p.op(DVE, lambda e, bk=bk, c0=c0, w=w, b=b: e.tensor_copy(out=Kb[b][0:64, c0:c0 + w], in_=ps[bk][0:64, 0:w]), reads=[tps[bk]], writes=[tK[b]])

            def proj_v(pj):
                pi = pj % 2
                for g0 in range(0, nkt, 4):
                    grp = ktiles[g0:g0 + 4]
                    bk = nps(5, 8)

                    def fv(e, grp=grp, bk=bk, pi=pi):
                        for ti, (k0, rows) in enumerate(grp):
                            for kc in range(2):
                                inst = e.matmul(ps[bk][0:rows, ti * 128:(ti + 1) * 128], lhsT=ckvn[:, kc, k0:k0 + rows], rhs=pw[pi][:, 8 + kc, :],
                                                start=(kc == 0), stop=(kc == 1))
                        return inst
                    p.op(PE, fv, reads=[tpw[pi], t_ckvn], writes=[tps[bk]])
                    nfull = sum(1 for (_, r) in grp if r == 128)
                    if nfull:
                        pv = ps[bk][:, 0:nfull * 128].rearrange("p (t c) -> p t c", t=nfull)
                        p.op(DVE, lambda e, pv=pv, g0=g0, nfull=nfull, pi=pi: e.tensor_copy(out=Vb[pi][:, g0:g0 + nfull, 0:64], in_=pv[:, :, 0:64]),
                             reads=[tps[bk]], writes=[tV[pi]])
                        p.op(DVE, lambda e, pv=pv, g0=g0, nfull=nfull, pi=pi: e.tensor_copy(out=Vb[pi][:, g0:g0 + nfull, 128:192], in_=pv[:, :, 64:128]),
                             reads=[tps[bk]], writes=[tV[pi]])
                    if nfull < len(grp):
                        k0, rows = grp[-1]
                        ti = len(grp) - 1
                        t = g0 + ti
                        p.op(DVE, lambda e, bk=bk, ti=ti, t=t, rows=rows, pi=pi: e.tensor_copy(out=Vb[pi][0:rows, t, 0:64], in_=ps[bk][0:rows, ti * 128:ti * 128 + 64]),
                             reads=[tps[bk]], writes=[tV[pi]])
                        p.op(DVE, lambda e, bk=bk, ti=ti, t=t, rows=rows, pi=pi: e.tensor_copy(out=Vb[pi][0:rows, t, 128:192], in_=ps[bk][0:rows, ti * 128 + 64:ti * 128 + 128]),
                             reads=[tps[bk]], writes=[tV[pi]])

            def side_work(h):
                if h >= NH:
                    return
                if h % 2 == 0:
                    load_pw(h // 2)
                    proj_head(h)
                    proj_v(h // 2)
                else:
                    proj_head(h)

            side_work(0)
            side_work(1)
            steps = []
            for h in range(NH):
                for ci, (c0, w) in enumerate(qch):
                    for t, (k0, rows) in enumerate(ktiles):
                        steps.append((h, ci, c0, w, t, k0, rows))
            N = len(steps)
            sinfo = {}
            obank = {}
            for s in range(N + LA):
                if s < N:
                    h, ci, c0, w, t, k0, rows = steps[s]
                    b = h % 2
                    sbk = nps(0, 3)
                    p.op(PE, lambda e, sbk=sbk, rows=rows, w=w, k0=k0, c0=c0, b=b: e.matmul(
                        ps[sbk][0:rows, 0:w], lhsT=Kb[b][0:96, k0:k0 + rows], rhs=Qb[b][0:96, c0:c0 + w], start=True, stop=True),
                        reads=[tK[b], tKr[b], tQ[b]], writes=[tps[sbk]])
                    pt = rr("pt", 3)
                    p.op(ACT, lambda e, pt=pt, sbk=sbk, rows=rows, w=w: e.activation(out=PT[pt][0:rows, 0:w], in_=ps[sbk][0:rows, 0:w], func=AF.Exp, scale=SC0),
                         reads=[tps[sbk]], writes=[tPT[pt]])
                    sinfo[s] = pt
                s2 = s - LA
                if s2 >= 0:
                    h, ci, c0, w, t, k0, rows = steps[s2]
                    pt = sinfo.pop(s2)
                    pj, hh = h // 2, h % 2
                    pi = pj % 2
                    vlo = 0 if hh == 0 else 64
                    if t == 0:
                        obank[(h, ci)] = 3 + rr("td", 2)
                    ob = obank[(h, ci)]
                    p.op(PE, lambda e, ob=ob, pt=pt, rows=rows, w=w, t=t, vlo=vlo, pi=pi, last=(t == nkt - 1): e.matmul(
                        ps[ob][:, 0:w], lhsT=Vb[pi][0:rows, t, vlo:vlo + 128], rhs=PT[pt][0:rows, 0:w], start=(t == 0), stop=last),
                        reads=[tV[pi], tPT[pt]], writes=[tps[ob]])
                    if t == nkt - 1:
                        j = rr("t", 2)
                        if hh == 0:
                            olo, dlo = 0, 64
                        else:
                            olo, dlo = 64, 0
                        p.op(DVE, lambda e, j=j, ob=ob, dlo=dlo, w=w: e.reciprocal(out=t1b[j][dlo:dlo + 64, 0:w], in_=ps[ob][dlo:dlo + 64, 0:w]),
                             reads=[tps[ob]], writes=[tt1[j]])
                        p.op(POOL, lambda e, j=j, olo=olo, dlo=dlo, w=w: e.tensor_copy(out=t2b[j][olo:olo + 64, 0:w], in_=t1b[j][dlo:dlo + 64, 0:w]),
                             reads=[tt1[j]], writes=[tt2[j]])
                        p.op(DVE, lambda e, j=j, ob=ob, olo=olo, w=w, c0=c0, pj=pj: e.tensor_tensor(
                            out=attn_out[olo:olo + 64, pj, c0:c0 + w], in0=ps[ob][olo:olo + 64, 0:w], in1=t2b[j][olo:olo + 64, 0:w], op=ALU.mult),
                            reads=[tps[ob], tt2[j]], writes=[t_attn[pj]])
                        if ci == len(qch) - 1:
                            side_work(h + 2)

        def phase_C(J):
            blocks = chunks(0, J.Lq, QW)
            nb_ = len(blocks)
            hbufs = [(hb, thb), (hb2, thb2)]

            def pr_pre(bi):
                return [wslice("wo0", 0), wslice("wo0", 1)]

            def pr(bi, pre=None):
                c0, w = blocks[bi]
                si = bi % 2
                src = DAP(xT, J.xoff + c0, [[XCOLS, 128], [128 * XCOLS, 8], [1, w]])
                p.dma(SP, lambda e, s, si=si, src=src, w=w: e.dma_start(out=slot[si][:, :, 0:w], in_=src).then_inc(s, 16), 1, f"slot{si}", writes=tslot[si])
                proj_resid(si, w, "wo0", lambda kc, c0=c0, w=w: attn_out[:, kc, c0:c0 + w], t_attn, pre=pre)

            def ln1(bi):
                c0, w = blocks[bi]
                hbo, thbo = hbufs[bi % 2]
                layernorm(bi % 2, w, 8, 24, hbo=hbo, thbo=thbo)

            def spill(bi):
                c0, w = blocks[bi]
                si = bi % 2
                lo = max(c0, J.n_pre)
                hi = min(c0 + w, J.n_pre + J.n_own)
                if hi > lo:
                    dst = DAP(h2s, lo - J.n_pre, [[2048, 128], [128 * 2048, 8], [1, hi - lo]])
                    p.dma(SP, lambda e, s, si=si, dst=dst, a=lo - c0, b=hi - c0: e.dma_start(out=dst, in_=slot[si][:, :, a:b]).then_inc(s, 16), 1, "h2s_w",
                          reads=tslot[si], writes=[t_h2s])
                dstb = DAP(h2b, c0, [[LSQ, 128], [128 * LSQ, 8], [1, w]])
                p.dma(SP, lambda e, s, dstb=dstb, w=w: e.dma_start(out=dstb, in_=hbs[:, :, 0:w]).then_inc(s, 16), 1, "h2b_w", reads=thbs, writes=[t_h2b])

            pr(0)
            ln1(0)
            if nb_ > 1:
                pr(1)
            mlp_w1(0, blocks[0][1], 0, *hbufs[0])
            for bi, (c0, w) in enumerate(blocks):
                si = bi % 2
                if bi + 1 < nb_:
                    ln1(bi + 1)
                mlp_w2(si, w, 0)
                layernorm(si, w, 40, 56, hbo=hbs, thbo=thbs)
                if bi + 1 < nb_:
                    mlp_w1((bi + 1) % 2, blocks[bi + 1][1], 0, *hbufs[(bi + 1) % 2])
                pre = pr_pre(bi + 2) if bi + 2 < nb_ else None
                spill(bi)
                if bi + 2 < nb_:
                    pr(bi + 2, pre)

        def phase_D(J, finals):
            state['dmode'] = True
            p.op(POOL, lambda e: e.memset(v1a[:, :, :, 64:128], 1.0), writes=[tv1])
            nblk_d = J.n_own // 512
            def geom(jb):
                si = jb % 2
                s0 = J.n_pre + 512 * jb
                lo = max(s0 - 128, 0)
                hi = min(s0 + 640, J.R)
                nwt = (hi - lo) // 128
                ww = hi - lo
                WT = ww + NMETA
                return si, s0, lo, hi, nwt, ww, WT

            def load_proj(jb):
                si, s0, lo, hi, nwt, ww, WT = geom(jb)
                src = DAP(h2s, 512 * jb, [[2048, 128], [128 * 2048, 8], [1, 512]])
                p.dma(SP, lambda e, s, si=si, src=src: e.dma_start(out=slot[si], in_=src).then_inc(s, 16), 1, f"slot{si}", reads=[t_h2s], writes=tslot[si])

                def ldw(e, s, lo=lo, ww=ww):
                    e.dma_start(out=hbw[:, :, 0:ww], in_=DAP(h2b, lo, [[LSQ, 128], [128 * LSQ, 8], [1, ww]])).then_inc(s, 16)
                    e.dma_start(out=hbw[:, :, ww:ww + NMETA], in_=DAP(h2b, J.meta, [[LSQ, 128], [128 * LSQ, 8], [1, NMETA]])).then_inc(s, 16)
                p.dma(SP, ldw, 2, "hbw", reads=[t_h2b], writes=[thbw])
                t1L = J.Lq

                def ldt1(e, s, lo=lo, ww=ww):
                    e.dma_start(out=c1t[:, 0:ww], in_=DAP(tabs, J.t1off + lo, [[TABW, 128], [1, ww]])).then_inc(s, 16)
                    e.dma_start(out=c1t[:, ww:ww + NMETA], in_=DAP(tabs, J.t1off + J.meta, [[TABW, 128], [1, NMETA]])).then_inc(s, 16)
                    e.dma_start(out=s1t[:, 0:ww], in_=DAP(tabs, J.t1off + t1L + lo, [[TABW, 128], [1, ww]])).then_inc(s, 16)
                    e.dma_start(out=s1t[:, ww:ww + NMETA], in_=DAP(tabs, J.t1off + t1L + J.meta, [[TABW, 128], [1, NMETA]])).then_inc(s, 16)
                p.dma(SP, ldt1, 4, "t1t", writes=[tt1t])
                wk, twk = wslice("wk1", 0)
                wks, twks = wslice("wk1s", 0)
                for kv in range(4):
                    for (c0, w) in chunks(0, WT, 512):
                        ba = nps(0, 6)
                        mm_group(ba, 128, w, [(wk[:, kc, kv * 128:(kv + 1) * 128], hbw[:, kc, c0:c0 + w]) for kc in range(8)], [twk, thbw])
                        bb = nps(0, 6)
                        mm_group(bb, 128, w, [(wks[:, kc, kv * 128:(kv + 1) * 128], hbw[:, kc, c0:c0 + w]) for kc in range(8)], [twks, thbw])
                        j = rr("t", 2)
                        p.op(DVE, lambda e, j=j, ba=ba, c0=c0, w=w: e.tensor_tensor(out=t1b[j][:, 0:w], in0=ps[ba][:, 0:w], in1=c1t[:, c0:c0 + w], op=ALU.mult),
                             reads=[tps[ba], tt1t], writes=[tt1[j]])
                        p.op(DVE, lambda e, j=j, bb=bb, c0=c0, w=w: e.tensor_tensor(out=t2b[j][:, 0:w], in0=ps[bb][:, 0:w], in1=s1t[:, c0:c0 + w], op=ALU.mult),
                             reads=[tps[bb], tt1t], writes=[tt2[j]])
                        p.op(POOL, lambda e, j=j, kv=kv, c0=c0, w=w: e.tensor_tensor(out=k1[:, kv, c0:c0 + w], in0=t1b[j][:, 0:w], in1=t2b[j][:, 0:w], op=ALU.add),
                             reads=[tt1[j], tt2[j]], writes=[tk1])
                wv_, twv = wslice("wv1", 0)
                vt = [(t * 128, 128) for t in range(nwt)] + [(ww, NMETA)]
                for ti, (k0, rows) in enumerate(vt):
                    bk = nps(0, 6)
                    mm_group(bk, rows, 256, [(hbw[:, kc, k0:k0 + rows], wv_[:, kc, :]) for kc in range(8)], [twv, thbw])
                    pv = ps[bk][0:rows, 0:256].rearrange("p (k c) -> p k c", k=4)
                    p.op(DVE, lambda e, pv=pv, ti=ti, rows=rows: e.tensor_copy(out=v1a[0:rows, ti, :, 0:64], in_=pv), reads=[tps[bk]], writes=[tv1])
                    p.op(DVE, lambda e, pv=pv, ti=ti, rows=rows: e.tensor_copy(out=v1a[0:rows, ti, :, 128:192], in_=pv), reads=[tps[bk]], writes=[tv1])
                qoff = s0 - lo
                for s in range(2):
                    wq_, twq = wslice("wq1", s)
                    wqs_, twqs = wslice("wq1s", s)
                    for m in range(4):
                        oc = 4 * s + m
                        ba = nps(0, 6)
                        mm_group(ba, 128, 512, [(wq_[:, kc, m * 128:(m + 1) * 128], hbw[:, kc, qoff:qoff + 512]) for kc in range(8)], [twq, thbw])
                        bb = nps(0, 6)
                        mm_group(bb, 128, 512, [(wqs_[:, kc, m * 128:(m + 1) * 128], hbw[:, kc, qoff:qoff + 512]) for kc in range(8)], [twqs, thbw])
                        j = rr("t", 2)
                        p.op(DVE, lambda e, j=j, ba=ba, qoff=qoff: e.tensor_tensor(out=t1b[j], in0=ps[ba][:, :], in1=c1t[:, qoff:qoff + 512], op=ALU.mult),
                             reads=[tps[ba], tt1t], writes=[tt1[j]])
                        p.op(DVE, lambda e, j=j, bb=bb, qoff=qoff: e.tensor_tensor(out=t2b[j], in0=ps[bb][:, :], in1=s1t[:, qoff:qoff + 512], op=ALU.mult),
                             reads=[tps[bb], tt1t], writes=[tt2[j]])
                        p.op(POOL, lambda e, j=j, oc=oc: e.tensor_tensor(out=q1[:, oc, :], in0=t1b[j], in1=t2b[j], op=ALU.add),
                             reads=[tt1[j], tt2[j]], writes=[tq1])

            def attn(jb):
                si, s0, lo, hi, nwt, ww, WT = geom(jb)
                nob = J.n_own // 128
                dsteps = []
                for qb in range(4):
                    gq = (s0 - J.n_pre) // 128 + qb
                    wt_own = (s0 - lo) // 128 + qb
                    tiles = []
                    if wt_own - 1 >= 0:
                        first = (gq == 0)
                        tiles.append((wt_own - 1, 128, 2 if (first and J.special) else 0))
                    tiles.append((wt_own, 128, None))
                    if wt_own + 1 < nwt:
                        lastb = (gq == nob - 1)
                        tiles.append((wt_own + 1, 128, 3 if (lastb and J.special) else 1))
                    tiles.append((nwt, NMETA, None))
                    for kv in range(4):
                        for ii, (ti, rows, mk) in enumerate(tiles):
                            dsteps.append((qb, kv, ii, ti, rows, mk, ii == len(tiles) - 1))
                ND = len(dsteps)
                LAD = 2
                pend = []
                dinfo = {}
                oset = {}
                for s_ in range(ND + LAD):
                    if s_ < ND:
                        qb, kv, ii, ti, rows, mk, lastt = dsteps[s_]
                        hf = 0
                        sbk = 2 * (s_ % 3)
                        sbk2 = sbk + 1
                        c_lo = hf * 256
                        k0 = ti * 128 if ti < nwt else ww
                        p.op(PE, lambda e, sbk=sbk, rows=rows, k0=k0, kv=kv, qb=qb, c_lo=c_lo: e.matmul(
                            ps[sbk][0:rows, c_lo:c_lo + 256], lhsT=k1[0:64, kv, k0:k0 + rows],
                            rhs=q1[0:64, 2 * kv:2 * kv + 2, qb * 128:(qb + 1) * 128], start=True, stop=True),
                            reads=[tk1, tq1], writes=[tps[sbk]])
                        p.op(PE, lambda e, sbk2=sbk2, rows=rows, k0=k0, kv=kv, qb=qb, c_lo=c_lo: e.matmul(
                            ps[sbk2][0:rows, c_lo:c_lo + 256], lhsT=k1[64:128, kv, k0:k0 + rows],
                            rhs=q1[64:128, 2 * kv:2 * kv + 2, qb * 128:(qb + 1) * 128], start=True, stop=True),
                            reads=[tk1, tq1], writes=[tps[sbk2]])
                        pt = rr("pt", 4)
                        if MERGE_EXP:
                            src = psall[0:rows, sbk * 512:(sbk + 2) * 512].rearrange("p (k c) -> p k c", k=2)[:, :, c_lo:c_lo + 256]
                            dstp = PT[pt][0:rows, :].rearrange("p (k c) -> p k c", k=2)
                            p.op(ACT, lambda e, src=src, dstp=dstp: e.activation(out=dstp, in_=src, func=AF.Exp, scale=SC1),
                                 reads=[tsh[sbk][hf], tsh[sbk2][hf]], writes=[tPT[pt]])
                        else:
                            p.op(ACT, lambda e, pt=pt, sbk=sbk, rows=rows, c_lo=c_lo: e.activation(out=PT[pt][0:rows, 0:256], in_=ps[sbk][0:rows, c_lo:c_lo + 256], func=AF.Exp, scale=SC1),
                                 reads=[tps[sbk]], writes=[tPT[pt]])
                            p.op(ACT, lambda e, pt=pt, sbk2=sbk2, rows=rows, c_lo=c_lo: e.activation(out=PT[pt][0:rows, 256:512], in_=ps[sbk2][0:rows, c_lo:c_lo + 256], func=AF.Exp, scale=SC1),
                                 reads=[tps[sbk2]], writes=[tPT[pt]])
                        if mk is not None:
                            p.op(POOL, lambda e, pt=pt, mk=mk: e.tensor_tensor(out=PT[pt], in0=PT[pt], in1=masks_bf[:, mk, :], op=ALU.mult),
                                 reads=[tPT[pt], t_const], writes=[tPT[pt]])
                        dinfo[s_] = pt
                    s2 = s_ - LAD
                    if s2 >= 0:
                        qb, kv, ii, ti, rows, mk, lastt = dsteps[s2]
                        pt = dinfo.pop(s2)
                        obe, obo = 6, 7
                        p.op(PE, lambda e, pt=pt, rows=rows, ti=ti, kv=kv, ii=ii, lastt=lastt, obe=obe: e.matmul(
                            ps[obe][:, 0:256], lhsT=v1a[0:rows, ti, kv, 0:128], rhs=PT[pt][0:rows, 0:256], start=(ii == 0), stop=lastt),
                            reads=[tv1, tPT[pt]], writes=[tps[obe]])
                        p.op(PE, lambda e, pt=pt, rows=rows, ti=ti, kv=kv, ii=ii, lastt=lastt, obo=obo: e.matmul(
                            ps[obo][:, 0:256], lhsT=v1a[0:rows, ti, kv, 64:192], rhs=PT[pt][0:rows, 256:512], start=(ii == 0), stop=lastt),
                            reads=[tv1, tPT[pt]], writes=[tps[obo]])
                        if lastt:
                            jo = rr("t", 2)
                            p.op(ACT, lambda e, jo=jo: e.activation(out=t1b[jo][:, 0:256], in_=ps[6][:, 0:256], func=AF.Copy), reads=[tps[6]], writes=[tt1[jo]])
                            p.op(ACT, lambda e, jo=jo: e.activation(out=t1b[jo][:, 256:512], in_=ps[7][:, 0:256], func=AF.Copy), reads=[tps[7]], writes=[tt1[jo]])
                            j = rr("rl", 2)
                            p.op(DVE, lambda e, j=j, jo=jo, kv=kv: e.tensor_tensor(
                                out=dtb[j][64:128, 0:256], in0=t1b[jo][64:128, 0:256], in1=estab[64:128, kv, 0:256], op=ALU.add),
                                reads=[tt1[jo], t_const], writes=[tdtb[j]])
                            p.op(DVE, lambda e, j=j, jo=jo, kv=kv: e.tensor_tensor(
                                out=dtb[j][0:64, 0:256], in0=t1b[jo][0:64, 256:512], in1=estab[0:64, kv, 256:512], op=ALU.add),
                                reads=[tt1[jo], t_const], writes=[tdtb[j]])
                            p.op(DVE, lambda e, j=j: e.reciprocal(out=dtb[j][:, 0:256], in_=dtb[j][:, 0:256]), reads=[tdtb[j]], writes=[tdtb[j]])
                            p.op(POOL, lambda e, j=j: e.tensor_copy(out=tmpd[j][0:64, 0:256], in_=dtb[j][64:128, 0:256]), reads=[tdtb[j]], writes=[ttmpd[j]])
                            p.op(POOL, lambda e, j=j: e.tensor_copy(out=tmpd[j][64:128, 0:256], in_=dtb[j][0:64, 0:256]), reads=[tdtb[j]], writes=[ttmpd[j]])
                            for fnp in pend:
                                fnp()
                            pend.clear()

                            def fin(j=j, jo=jo, kv=kv, qb=qb):
                                for (olo, ecol) in [(0, 0), (64, 256)]:
                                    p.op(DVE, lambda e, olo=olo, ecol=ecol: e.tensor_tensor(
                                        out=a1[olo:olo + 64, 2 * kv:2 * kv + 2, qb * 128:(qb + 1) * 128],
                                        in0=t1b[jo][olo:olo + 64, ecol:ecol + 256].rearrange("p (a b) -> p a b", a=2),
                                        in1=tmpd[j][olo:olo + 64, 0:256].rearrange("p (a b) -> p a b", a=2), op=ALU.mult),
                                        reads=[tt1[jo], ttmpd[j]], writes=[ta1[2 * kv], ta1[2 * kv + 1]])
                            pend.append(fin)
                for fnp in pend:
                    fnp()
                pend.clear()

            load_proj(0)
            attn(0)
            state['dmode_keep'] = True
            for jb in range(nblk_d):
                si = jb % 2
                proj_resid(si, 512, "wo1", lambda kc: a1[:, kc, :], ta1)
                layernorm(si, 512, 16, 32)
                if jb + 1 < nblk_d:
                    load_proj(jb + 1)
                mlp(si, 512, 1)
                layernorm(si, 512, 48, 64, want_hb=False)
                dst = DAP(yT, J.yidx * DM * 2048 + 512 * jb, [[2048, 128], [128 * 2048, 8], [1, 512]])
                finals.append(p.dma(SP, lambda e, s, si=si, dst=dst: e.dma_start(out=dst, in_=slot[si]).then_inc(s, 16), 1, f"y{si}", reads=tslot[si]))
                if jb + 1 < nblk_d:
                    attn(jb + 1)

        finals = []
        ld_consts()
        prologue()
        p.barrier()
        phs = dbg["phases"] if dbg else "ABCD"
        if dbg:
            jobs = [jobs[i] for i in dbg["jobs"]]

        def dump(dt_, view2d, key):
            finals.append(p.dma(SP, lambda e, s: e.dma_start(out=dt_.ap(), in_=view2d).then_inc(s, 16), 1, key))
        for J in jobs:
            if "A" in phs:
                phase_A(J)
                p.barrier()
                if dbg:
                    dump(d_cqn, cqn.rearrange("p a b -> p (a b)"), "dc1")
                    dump(d_ckvn, ckvn.rearrange("p a b -> p (a b)"), "dc2")
                    dump(d_qr, qr.rearrange("p a b -> p (a b)"), "dc3")
                    dump(d_k0, Kb[0], "dc4")
                    p.barrier()
            if "B" in phs:
                phase_B(J)
                p.barrier()
                if dbg:
                    dump(d_attn, attn_out.rearrange("p a b -> p (a b)"), "dc5")
                    p.barrier()
            if "C" in phs:
                phase_C(J)
                p.barrier()
            if "D" in phs:
                phase_D(J, finals)
                p.barrier()
        last = {}
        for d in finals:
            last[d.semkey] = d
        build_block(nc, p, list(last.values()))
    return nc


_CACHE = {}


def kernel(x_prompt, x_sample, meta_tokens, mla_w_in, mla_g_q, mla_w_uq, mla_g_kv, mla_w_ukv, mla_w_o,
           gqa_w_qkv, gqa_sink, gqa_w_o, mlp_w1, mlp_w2, ln1_g, ln1_b, ln2_g, ln2_b):
    inp = dict(mla_w_in=np.asarray(mla_w_in), mla_w_uq=np.asarray(mla_w_uq), mla_w_ukv=np.asarray(mla_w_ukv), mla_w_o=np.asarray(mla_w_o),
               gqa_w_qkv=np.asarray(gqa_w_qkv), gqa_w_o=np.asarray(gqa_w_o), mlp_w1=np.asarray(mlp_w1), mlp_w2=np.asarray(mlp_w2))
    wall, nblk = build_wall(inp)
    x_prompt = np.asarray(x_prompt, np.float32)
    x_sample = np.asarray(x_sample, np.float32)
    metaT = np.asarray(meta_tokens, np.float32).T

    def pc(v):
        return np.asarray(v, np.float32).reshape(-1, 128).T
    vec = np.zeros((128, NVEC), np.float32)
    vec[:, 0:6] = pc(mla_g_q[0])
    vec[:, 6:8] = pc(mla_g_kv[0])
    vec[:, 8:16] = pc(ln1_g[0]); vec[:, 16:24] = pc(ln1_g[1])
    vec[:, 24:32] = pc(ln1_b[0]); vec[:, 32:40] = pc(ln1_b[1])
    vec[:, 40:48] = pc(ln2_g[0]); vec[:, 48:56] = pc(ln2_g[1])
    vec[:, 56:64] = pc(ln2_b[0]); vec[:, 64:72] = pc(ln2_b[1])
    vec[:, 72:88] = np.broadcast_to(np.asarray(gqa_sink, np.float32).reshape(1, 16), (128, 16))

    posP = np.concatenate([16 + np.arange(2048), np.arange(16)])
    CP0, SP0 = mla_tabs(posP)
    CP1, SP1 = gqa_tabs(posP)
    kk = np.arange(128)[:, None]
    ii = np.arange(128)[None, :]
    tri_ge = np.tile((kk >= ii).astype(np.float32), (1, 4))
    tri_le = np.tile((kk <= ii).astype(np.float32), (1, 4))

    in_maps = []
    for c in range(NCORES):
        sq, half = c // 2, c % 2
        xT = np.empty((DM, XCOLS), np.float32)
        for i in range(4):
            xT[:, i * LP:i * LP + 2048] = x_prompt[4 * c + i].T
            xT[:, i * LP + 2048:(i + 1) * LP] = metaT
        xs = x_sample[sq]
        if half == 0:
            own = np.arange(0, 2048); post = np.arange(2048, 2176); pre = np.arange(3968, 4096)
            rest = np.arange(2176, 3968)
            pre_valid, post_valid = 0.0, 1.0
        else:
            own = np.arange(2048, 4096); pre = np.arange(1920, 2048); post = np.arange(0, 128)
            rest = np.arange(128, 1920)
            pre_valid, post_valid = 1.0, 0.0
        order = np.concatenate([pre, own, post])
        b = 4 * LP
        xT[:, b:b + 2304] = xs[order].T
        xT[:, b + 2304:b + 2320] = metaT
        xT[:, b + 2320:b + LSK] = xs[rest].T
        posS = np.concatenate([16 + order, np.arange(16), 16 + rest])
        CS0, SS0 = mla_tabs(posS)
        CS1, SS1 = gqa_tabs(posS[:LSQ])
        tabs = np.concatenate([CP0, SP0, CS0, SS0, CP1, SP1, CS1, SS1], axis=1).astype(np.float32)
        assert tabs.shape == (128, TABW)
        masks = np.concatenate([tri_ge, tri_le, tri_ge * pre_valid, tri_le * post_valid], axis=1).astype(np.float32)
        in_maps.append({"xT": xT, "wall": wall, "tabs": np.ascontiguousarray(tabs), "masks": np.ascontiguousarray(masks), "vec": vec})

    if nblk not in _CACHE:
        _CACHE[nblk] = build_program(nblk)
    nc = _CACHE[nblk]
    res = run_bass_kernel_spmd(nc, in_maps, core_ids=list(range(NCORES)))
    y_prompt = np.empty((32, 2048, DM), np.float32)
    y_sample = np.empty((4, 4096, DM), np.float32)
    for c in range(NCORES):
        yT = res.results[c]["yT"]
        for i in range(4):
            y_prompt[4 * c + i] = yT[i].T
        sq, half = c // 2, c % 2
        y_sample[sq, half * 2048:(half + 1) * 2048] = yT[4].T
    return (y_prompt, y_sample)
```

```python
import contextlib
import numpy as np
import ml_dtypes
import concourse.bass as bass
import concourse.mybir as mybir
from concourse.bass_utils import run_bass_kernel_spmd

F32 = mybir.dt.float32
BF16 = mybir.dt.bfloat16
ALU = mybir.AluOpType
AF = mybir.ActivationFunctionType

PE, ACT, DVE, POOL, SP = "tensor", "scalar", "vector", "gpsimd", "sync"
ENGS = [PE, ACT, DVE, POOL, SP]
SEM_EPOCH = 30000

DM = 1024
NMETA = 16
NH = 16
QRANK = 768
KVRANK = 256
DFF = 4096
ALPHA = float((2.0 * 2) ** 0.25)
LN_EPS = 1e-5
RMS_EPS = 1e-6
THETA = 500000.0
SC0 = float(96 ** -0.5)
SC1 = float(64 ** -0.5)
NCORES = 8
AW = 256
BAR_D = False
USE_DIV = False
MERGE_EXP = False
HALFBANK = False
QW = 512


class T:
    __slots__ = ("name", "last_w", "readers")

    def __init__(self, name=""):
        self.name = name
        self.last_w = None
        self.readers = []


class Op:
    __slots__ = ("eng", "fn", "deps", "needs_inc", "is_dma", "semkey", "dma_val", "inc_val")

    def __init__(self, eng, fn):
        self.eng = eng
        self.fn = fn
        self.deps = []
        self.needs_inc = False
        self.is_dma = False
        self.semkey = None
        self.dma_val = 0
        self.inc_val = None


class Prog:
    def __init__(self):
        self.ops = {e: [] for e in ENGS}
        self.dma_last = {}
        self.dma_count = {}
        self.last_op = {e: None for e in ENGS}
        self.pending_dma = []

    def _track(self, op, reads, writes):
        deps = []
        for t in reads:
            if t.last_w is not None:
                deps.append(t.last_w)
        for t in writes:
            if t.last_w is not None:
                deps.append(t.last_w)
            deps.extend(t.readers)
        for t in reads:
            t.readers.append(op)
        for t in writes:
            t.last_w = op
            t.readers = []
        seen = set()
        for d in deps:
            if d is op or id(d) in seen:
                continue
            seen.add(id(d))
            if d.eng == PE and op.eng == PE and not d.is_dma and not op.is_dma:
                continue
            op.deps.append(d)
            if not d.is_dma:
                d.needs_inc = True

    def op(self, eng, fn, reads=(), writes=()):
        o = Op(eng, fn)
        self._track(o, reads, writes)
        self.ops[eng].append(o)
        self.last_op[eng] = o
        return o

    def dma(self, eng, fn, ndma, semkey, reads=(), writes=()):
        o = Op(eng, fn)
        o.is_dma = True
        o.semkey = semkey
        prev = self.dma_last.get(semkey)
        self._track(o, reads, writes)
        if prev is not None and all(d is not prev for d in o.deps):
            o.deps.append(prev)
        self.dma_last[semkey] = o
        self.dma_count[semkey] = self.dma_count.get(semkey, 0) + 16 * ndma
        o.dma_val = self.dma_count[semkey]
        self.ops[eng].append(o)
        self.pending_dma.append(o)
        return o

    def barrier(self):
        lasts = [self.last_op[e] for e in ENGS if self.last_op[e] is not None and not self.last_op[e].is_dma]
        comp = []
        for e in ENGS:
            for o in reversed(self.ops[e]):
                if not o.is_dma and o.fn is not None:
                    comp.append(o)
                    break
        dmas = list(self.pending_dma)
        self.pending_dma = []
        for e in ENGS:
            b = Op(e, None)
            for d in comp:
                if d.eng != e:
                    b.deps.append(d)
                    d.needs_inc = True
            b.deps.extend(dmas)
            self.ops[e].append(b)


def build_block(nc, prog, final_dmas):
    counts = {e: 0 for e in ENGS}
    for e in ENGS:
        for o in prog.ops[e]:
            if (not o.is_dma) and o.needs_inc:
                counts[e] += 1
                o.inc_val = counts[e]
    with contextlib.ExitStack() as st:
        esems = {}
        for e in ENGS:
            n_ep = counts[e] // SEM_EPOCH + 1
            esems[e] = [st.enter_context(nc.semaphore(f"s_{e}_{k}")) for k in range(n_ep)]
        dsems = {}
        for k in prog.dma_count:
            dsems[k] = st.enter_context(nc.semaphore(f"d_{len(dsems)}"))
        block = st.enter_context(nc.Block())

        def sem_of(op):
            if op.is_dma:
                return dsems[op.semkey], op.dma_val, ("d", op.semkey)
            ep = (op.inc_val - 1) // SEM_EPOCH
            return esems[op.eng][ep], op.inc_val - ep * SEM_EPOCH, (op.eng, ep)

        def make(e):
            def body(eng):
                seen = {}
                for o in prog.ops[e]:
                    for d in o.deps:
                        sem, val, key = sem_of(d)
                        if seen.get(key, 0) >= val:
                            continue
                        eng.wait_ge(sem, val)
                        seen[key] = val
                    if o.fn is None:
                        continue
                    if o.is_dma:
                        o.fn(eng, dsems[o.semkey])
                    else:
                        inst = o.fn(eng)
                        if o.needs_inc:
                            ep = (o.inc_val - 1) // SEM_EPOCH
                            inst.then_inc(esems[e][ep], 1)
                if e == SP:
                    for d in final_dmas:
                        sem, val, key = sem_of(d)
                        eng.wait_ge(sem, val)
            return body

        block.tensor(make(PE))
        block.scalar(make(ACT))
        block.vector(make(DVE))
        block.gpsimd(make(POOL))
        block.sync(make(SP))


def tile_w(W, cps):
    Kd, Nd = W.shape
    nkc = Kd // 128
    ns = Nd // cps
    a = W.reshape(nkc, 128, ns, cps).transpose(2, 1, 0, 3)
    return np.ascontiguousarray(a).reshape(-1)


WSPEC = {}


def build_wall(inp):
    parts = []
    off = 0

    def add(name, W, cps):
        nonlocal off
        W = np.asarray(W, np.float32)
        flat = tile_w(W, cps)
        WSPEC[name] = (off, W.shape[0] // 128, cps, W.shape[1] // cps)
        parts.append(flat)
        off += flat.size

    w_in = inp["mla_w_in"][0]
    p32 = (np.arange(32) + 16) % 32
    add("win", np.concatenate([w_in, w_in[:, 1024 + p32]], axis=1), 1088)
    wuq = inp["mla_w_uq"][0].reshape(QRANK, NH, 96)
    add("wqr", np.concatenate([wuq[:, :, 64:].reshape(QRANK, 512), wuq[:, :, 64 + p32].reshape(QRANK, 512)], axis=1), 1024)
    add("wqn", wuq[:, :, :64].reshape(QRANK, 1024), 128)
    wukv = inp["mla_w_ukv"][0].reshape(KVRANK, NH, 128)
    add("wkn", wukv[:, :, :64].reshape(KVRANK, 1024), 128)
    add("wvv", wukv[:, :, 64:].reshape(KVRANK, 1024), 128)
    add("wo0", inp["mla_w_o"][0], 512)
    for l in range(2):
        add(f"w1_{l}", inp["mlp_w1"][l], 512)
        add(f"w2_{l}", inp["mlp_w2"][l], 128)
    wqkv = inp["gqa_w_qkv"][0]
    p64 = np.arange(64)
    p64[:16] = (np.arange(16) + 8) % 16
    wq = wqkv[:, :1024].reshape(DM, 16, 64)
    add("wq1", wq.reshape(DM, 1024), 512)
    add("wq1s", wq[:, :, p64].reshape(DM, 1024), 512)
    wk = wqkv[:, 1024:1280].reshape(DM, 4, 64)
    wkd = np.concatenate([wk, wk], axis=2)
    wks = wk[:, :, p64]
    wksd = np.concatenate([wks, wks], axis=2)
    add("wk1", wkd.reshape(DM, 512), 512)
    add("wk1s", wksd.reshape(DM, 512), 512)
    add("wv1", wqkv[:, 1280:1536], 256)
    add("wo1", inp["gqa_w_o"][0], 512)
    tot = off
    blk = 128 * 2048
    nb = (tot + blk - 1) // blk
    wall = np.zeros(nb * blk, np.float32)
    wall[:tot] = np.concatenate(parts)
    return wall, nb


def rope_cs(pos, dim):
    inv = THETA ** (-np.arange(0, dim, 2, dtype=np.float32) / np.float32(dim))
    ang = pos.astype(np.float32)[:, None] * inv[None, :].astype(np.float32)
    return np.cos(ang).astype(np.float32), np.sin(ang).astype(np.float32)


def mla_tabs(pos):
    c, s = rope_cs(pos, 32)
    C32 = np.concatenate([c, c], axis=1).T
    S32 = np.concatenate([-s, s], axis=1).T
    return np.tile(C32, (4, 1)), np.tile(S32, (4, 1))


def gqa_tabs(pos):
    c, s = rope_cs(pos, 16)
    L = pos.shape[0]
    C64 = np.ones((64, L), np.float32)
    S64 = np.zeros((64, L), np.float32)
    C64[:8] = c.T
    C64[8:16] = c.T
    S64[:8] = -s.T
    S64[8:16] = s.T
    return np.tile(C64, (2, 1)), np.tile(S64, (2, 1))


class Job:
    def __init__(self, n_pre, n_own, n_post, Lk, xoff, t0off, t1off, yidx, special_masks):
        self.n_pre, self.n_own, self.n_post, self.Lk = n_pre, n_own, n_post, Lk
        self.R = n_pre + n_own + n_post
        self.Lq = self.R + NMETA
        self.meta = self.R
        self.xoff, self.t0off, self.t1off, self.yidx = xoff, t0off, t1off, yidx
        self.special = special_masks


LP = 2064
LSQ = 2320
LSK = 4112
XCOLS = 4 * LP + LSK
TAB_P0 = 0
TAB_S0 = 2 * LP
TAB_P1 = TAB_S0 + 2 * LSK
TAB_S1 = TAB_P1 + 2 * LP
TABW = TAB_S1 + 2 * LSQ
NVEC = 88


def chunks(lo, hi, w):
    out = []
    c = lo
    while c < hi:
        out.append((c, min(w, hi - c)))
        c += w
    return out


def build_program(nblk, dbg=None):
    nc = bass.Bass("TRN2", target_bir_lowering=False)
    blk = 128 * 2048
    xT = nc.dram_tensor("xT", [DM, XCOLS], F32, kind="ExternalInput")
    wall = nc.dram_tensor("wall", [nblk * blk], F32, kind="ExternalInput")
    tabs = nc.dram_tensor("tabs", [128, TABW], F32, kind="ExternalInput")
    masks = nc.dram_tensor("masks", [128, 4 * 512], F32, kind="ExternalInput")
    vecd = nc.dram_tensor("vec", [128, NVEC], F32, kind="ExternalInput")
    yT = nc.dram_tensor("yT", [5, DM, 2048], F32, kind="ExternalOutput")
    wbf = nc.dram_tensor("wbf", [nblk * blk], BF16)
    dk = dict(kind="ExternalOutput") if dbg else {}
    h2s = nc.dram_tensor("h2s", [DM, 2048], F32, **dk)
    h2b = nc.dram_tensor("h2b", [DM, LSQ], BF16, **dk)
    if dbg:
        d_cqn = nc.dram_tensor("d_cqn", [128, 6 * LSQ], BF16, kind="ExternalOutput")
        d_ckvn = nc.dram_tensor("d_ckvn", [128, 2 * LSK], BF16, kind="ExternalOutput")
        d_qr = nc.dram_tensor("d_qr", [128, 4 * LSQ], BF16, kind="ExternalOutput")
        d_k0 = nc.dram_tensor("d_k0", [128, LSK], BF16, kind="ExternalOutput")
        d_attn = nc.dram_tensor("d_attn", [128, 8 * LSQ], BF16, kind="ExternalOutput")

    def DAP(t, off, pairs):
        return bass.AP(t, off, [list(p) for p in pairs])

    jobs = [Job(0, 2048, 0, LP, i * LP, TAB_P0, TAB_P1, i, False) for i in range(4)]
    jobs.append(Job(128, 2048, 128, LSK, 4 * LP, TAB_S0, TAB_S1, 4, True))

    st = contextlib.ExitStack()
    with st:
        ARB = 206 * 1024
        arena = st.enter_context(nc.sbuf_tensor("arena", [128, ARB // 2], BF16))
        psall = st.enter_context(nc.psum_tensor("psall", [128, 4096], F32))
        ps = [psall[:, i * 512:(i + 1) * 512] for i in range(8)]
        tps = [T(f"ps{i}") for i in range(8)]
        p = Prog()

        class Alloc:
            def __init__(self, base):
                self.o = base

            def get(self, nbytes):
                o = self.o
                self.o += (nbytes + 63) // 64 * 64
                assert self.o <= ARB, ("SBUF arena overflow", self.o)
                return o

        def vw(off, dt, shape):
            n = int(np.prod(shape))
            if dt == BF16:
                a = arena[:, off // 2: off // 2 + n]
            else:
                a = arena[:, off // 2: off // 2 + 2 * n].bitcast(F32)
            if len(shape) == 2:
                return a.rearrange("p (a b) -> p a b", a=shape[0])
            if len(shape) == 3:
                return a.rearrange("p (a b c) -> p a b c", a=shape[0], b=shape[1])
            return a

        def buf(al, dt, shape):
            nb = int(np.prod(shape)) * (2 if dt == BF16 else 4)
            return vw(al.get(nb), dt, shape)

        com = Alloc(0)
        ones_bf = buf(com, BF16, [128])
        masks_bf = buf(com, BF16, [4, 512])
        estab = buf(com, F32, [4, 512])
        vec = buf(com, F32, [NVEC])
        PT = [buf(com, BF16, [512]) for _ in range(4)]
        tPT = [T() for _ in range(4)]
        tsh = [[T() for _ in range(2)] for _ in range(4)]
        t1b = [buf(com, F32, [512]) for _ in range(2)]
        t2b = [buf(com, F32, [512]) for _ in range(2)]
        tt1 = [T() for _ in range(2)]
        tt2 = [T() for _ in range(2)]
        rlb = [buf(com, F32, [512]) for _ in range(2)]
        trl = [T() for _ in range(2)]
        vbc = [buf(com, BF16, [512]) for _ in range(2)]
        sqc = [buf(com, BF16, [512]) for _ in range(2)]
        tvbc = [T() for _ in range(2)]
        tsqc = [T() for _ in range(2)]
        st_mean = buf(com, F32, [512])
        st_msq = buf(com, F32, [512])
        st_sd = buf(com, F32, [512])
        st_sd2 = buf(com, F32, [512])
        t_mean, t_msq, t_sd, t_sd2 = T(), T(), T(), T()
        tmpd = [buf(com, F32, [512]) for _ in range(2)]
        ttmpd = [T() for _ in range(2)]
        dtb = rlb
        tdtb = trl
        pw = [buf(com, BF16, [10, 128]) for _ in range(2)]
        tpw = [T() for _ in range(2)]
        t_const = T("const")
        COM_END = com.o

        main = Alloc(COM_END)
        attn_off = main.get(8 * LSQ * 2)
        attn_out = vw(attn_off, BF16, [8, LSQ])
        t_attn = [T() for _ in range(8)]
        AB0 = main.o
        ab = Alloc(AB0)
        cqn = buf(ab, BF16, [6, LSQ]); t_cqn = T()
        ckvn = buf(ab, BF16, [2, LSK]); t_ckvn = T()
        qr = buf(ab, BF16, [4, LSQ]); t_qr = T()
        Kb = [buf(ab, BF16, [LSK]) for _ in range(2)]; tK = [T() for _ in range(2)]; tKr = [T() for _ in range(2)]
        QV0 = ab.o
        Qb = [buf(ab, BF16, [LSQ]) for _ in range(2)]; tQ = [T() for _ in range(2)]
        NKT_MAX = (LSK + 127) // 128
        Vb = [buf(ab, BF16, [NKT_MAX, 192]) for _ in range(2)]; tV = [T() for _ in range(2)]
        AB_END = ab.o
        at = Alloc(attn_off)
        win = buf(at, BF16, [8, 1088]); t_win = T()
        wqr = buf(at, BF16, [6, 1024]); t_wqr = T()
        assert at.o <= AB0, at.o
        at2 = Alloc(QV0)
        xf = [buf(at2, F32, [8, AW]) for _ in range(1)] * 2; txf = [T()] * 2
        xb = [buf(at2, BF16, [8, AW]) for _ in range(2)]; txb = [T() for _ in range(2)]
        csb = buf(at2, F32, [8, AW]); tcsb = [T() for _ in range(8)]
        sqa = buf(at2, BF16, [8, AW]); tsqa = [T() for _ in range(8)]
        ctab = [buf(at2, F32, [AW]) for _ in range(2)]; stab = [buf(at2, F32, [AW]) for _ in range(2)]
        ttab = [T() for _ in range(2)]
        rq = st_mean; rkv = st_msq; t_rq, t_rkv = t_mean, t_msq
        A_END = at2.o
        assert A_END <= AB_END, (A_END, AB_END)
        pr = Alloc(AB0)
        NSTG = 6
        stg_f = [buf(pr, F32, [2048]) for _ in range(NSTG)]; tsf = [T() for _ in range(NSTG)]
        stg_b = [buf(pr, BF16, [2048]) for _ in range(NSTG)]; tsb = [T() for _ in range(NSTG)]
        cd = Alloc(AB0)
        slot = [buf(cd, F32, [8, 512]) for _ in range(2)]; tslot = [[T() for _ in range(8)] for _ in range(2)]
        hb = buf(cd, BF16, [8, 512]); thb = [T() for _ in range(8)]
        hid = buf(cd, BF16, [32, 512]); thid = [T() for _ in range(32)]
        wsb = [buf(cd, BF16, [4096]) for _ in range(3)]; tws = [T() for _ in range(3)]
        CD_END = cd.o
        cx = Alloc(CD_END)
        hb2 = buf(cx, BF16, [8, 512]); thb2 = [T() for _ in range(8)]
        hbs = buf(cx, BF16, [8, 512]); thbs = [T() for _ in range(8)]
        assert cx.o <= ARB, cx.o
        dx = Alloc(attn_off)
        WIN_MAX = 768 + NMETA
        hbw = buf(dx, BF16, [8, WIN_MAX]); thbw = T()
        q1 = buf(dx, BF16, [8, 512]); tq1 = T()
        k1 = buf(dx, BF16, [4, WIN_MAX]); tk1 = T()
        a1 = buf(dx, BF16, [8, 512]); ta1 = [T() for _ in range(8)]
        assert dx.o <= AB0, dx.o
        dx2 = Alloc(CD_END)
        v1a = buf(dx2, BF16, [7, 4, 192]); tv1 = T()
        c1t = buf(dx2, F32, [WIN_MAX]); s1t = buf(dx2, F32, [WIN_MAX]); tt1t = T()
        D_END = dx2.o
        print('SBUF', COM_END, AB0, AB_END, A_END, CD_END, D_END, ARB)
        assert max(A_END, D_END, AB_END) <= ARB

        state = {"ps": 0, "ws": 0, "t": 0, "rl": 0, "vs": 0, "pt": 0, "td": 0}

        def nps(lo=0, hi=8):
            i = lo + state["ps"] % (hi - lo)
            state["ps"] += 1
            return i

        def rr(key, n):
            i = state[key] % n
            state[key] += 1
            return i

        t_wbf = T("wbf")
        t_wbf_pro = [T("wbf_pro") for _ in range(8)]
        t_h2s, t_h2b = T("h2s"), T("h2b")

        def wslice(name, s):
            off, nkc, cps, ns = WSPEC[name]
            i = rr("ws", 3)
            n = nkc * cps
            src = DAP(wbf, off + s * 128 * n, [[n, 128], [1, n]])
            dst = wsb[i][:, 0:n]
            if not (dbg and dbg.get("nows") and state["ws"] > 3):
                p.dma(SP, lambda e, sem, dst=dst, src=src: e.dma_start(out=dst, in_=src).then_inc(sem, 16), 1, f"ws{i}",
                      reads=[t_wbf], writes=[tws[i]])
            return wsb[i][:, 0:n].rearrange("p (k c) -> p k c", k=nkc), tws[i]

        def mm_group(bank, rows, w, pieces, reads):
            def f(e, bank=bank, rows=rows, w=w, pieces=pieces):
                n = len(pieces)
                for i, (l, r) in enumerate(pieces):
                    inst = e.matmul(ps[bank][0:rows, 0:w], lhsT=l, rhs=r, start=(i == 0), stop=(i == n - 1))
                return inst
            return p.op(PE, f, reads=reads, writes=[tps[bank]])

        def ld_consts():
            mf = stg_f[0][:, 0:2048]
            p.dma(SP, lambda e, s: e.dma_start(out=mf, in_=masks.ap()).then_inc(s, 16), 1, "sf0", writes=[tsf[0]])
            p.op(DVE, lambda e: e.tensor_copy(out=masks_bf.rearrange("p a b -> p (a b)"), in_=mf), reads=[tsf[0]], writes=[t_const])
            p.dma(SP, lambda e, s: e.dma_start(out=vec, in_=vecd.ap()).then_inc(s, 16), 1, "vec", writes=[t_const])
            p.op(DVE, lambda e: e.memset(ones_bf, 1.0), writes=[t_const])
            p.op(ACT, lambda e: e.activation(out=vec[:, 72:88], in_=vec[:, 72:88], func=AF.Exp), reads=[t_const], writes=[t_const])
            p.op(DVE, lambda e: e.memset(estab.rearrange("p a b -> p (a b)"), 0.0), reads=[t_const], writes=[t_const])
            for kv in range(4):
                for bi, g in enumerate([0, 2, 1, 3]):
                    h = 4 * kv + g
                    p.op(DVE, lambda e, kv=kv, bi=bi, h=h: e.tensor_scalar(
                        out=estab[:, kv, bi * 128:(bi + 1) * 128], in0=estab[:, kv, bi * 128:(bi + 1) * 128],
                        scalar1=vec[:, 72 + h:73 + h], scalar2=None, op0=ALU.add), reads=[t_const], writes=[t_const])

        def prologue():
            engs = [DVE, POOL, ACT]
            for b in range(nblk):
                i = b % NSTG
                src = DAP(wall, b * blk, [[2048, 128], [1, 2048]])
                dstd = DAP(wbf, b * blk, [[2048, 128], [1, 2048]])
                p.dma(SP, lambda e, s, i=i, src=src: e.dma_start(out=stg_f[i], in_=src).then_inc(s, 16), 1, f"sf{i}", writes=[tsf[i]])
                en = engs[b % 3]
                if en == ACT:
                    p.op(ACT, lambda e, i=i: e.activation(out=stg_b[i], in_=stg_f[i], func=AF.Copy), reads=[tsf[i]], writes=[tsb[i]])
                else:
                    p.op(en, lambda e, i=i: e.tensor_copy(out=stg_b[i], in_=stg_f[i]), reads=[tsf[i]], writes=[tsb[i]])
                p.dma(ACT, lambda e, s, i=i, dstd=dstd: e.dma_start(out=dstd, in_=stg_b[i]).then_inc(s, 16), 1, f"sb{i}",
                      reads=[tsb[i]], writes=[t_wbf_pro[i]])

        def layernorm(si, w, gcol, bcol, want_hb=True, hbo=None, thbo=None):
            hbo = hb if hbo is None else hbo
            thbo = thb if thbo is None else thbo
            S = slot[si]
            bs = nps(6, 8)
            bq = nps(6, 8)
            pcs_s, pcs_q = [], []
            for oc in range(8):
                j = rr("vs", 2)
                p.op(ACT, lambda e, j=j, oc=oc: e.activation(out=vbc[j][:, 0:w], in_=S[:, oc, 0:w], func=AF.Copy), reads=[tslot[si][oc]], writes=[tvbc[j]])
                p.op(DVE, lambda e, j=j, oc=oc: e.tensor_tensor(out=sqc[j][:, 0:w], in0=S[:, oc, 0:w], in1=S[:, oc, 0:w], op=ALU.mult),
                     reads=[tslot[si][oc]], writes=[tsqc[j]])
                p.op(PE, lambda e, j=j, oc=oc: e.matmul(ps[bs][:, 0:w], lhsT=ones_bf, rhs=vbc[j][:, 0:w], start=(oc == 0), stop=(oc == 7)),
                     reads=[tvbc[j], t_const], writes=[tps[bs]])
                p.op(PE, lambda e, j=j, oc=oc: e.matmul(ps[bq][:, 0:w], lhsT=ones_bf, rhs=sqc[j][:, 0:w], start=(oc == 0), stop=(oc == 7)),
                     reads=[tsqc[j], t_const], writes=[tps[bq]])
            p.op(DVE, lambda e: e.tensor_scalar(out=st_mean[:, 0:w], in0=ps[bs][:, 0:w], scalar1=1.0 / DM, scalar2=None, op0=ALU.mult),
                 reads=[tps[bs]], writes=[t_mean])
            p.op(DVE, lambda e: e.tensor_tensor(out=st_msq[:, 0:w], in0=st_mean[:, 0:w], in1=st_mean[:, 0:w], op=ALU.mult),
                 reads=[t_mean], writes=[t_msq])
            p.op(DVE, lambda e: e.scalar_tensor_tensor(out=st_sd2[:, 0:w], in0=ps[bq][:, 0:w], scalar=1.0 / DM, in1=st_msq[:, 0:w],
                                                       op0=ALU.mult, op1=ALU.subtract), reads=[tps[bq], t_msq], writes=[t_sd2])
            p.op(ACT, lambda e: e.activation(out=st_sd[:, 0:w], in_=st_sd2[:, 0:w], func=AF.Sqrt, bias=LN_EPS, scale=1.0),
                 reads=[t_sd2], writes=[t_sd])
            if not USE_DIV:
                p.op(DVE, lambda e: e.reciprocal(out=st_sd2[:, 0:w], in_=st_sd[:, 0:w]), reads=[t_sd], writes=[t_sd2])
            for oc in range(8):
                p.op(DVE, lambda e, oc=oc: e.tensor_tensor(out=S[:, oc, 0:w], in0=S[:, oc, 0:w], in1=st_mean[:, 0:w], op=ALU.subtract),
                     reads=[t_mean, tslot[si][oc]], writes=[tslot[si][oc]])
                if USE_DIV:
                    p.op(DVE, lambda e, oc=oc: e.tensor_tensor(out=S[:, oc, 0:w], in0=S[:, oc, 0:w], in1=st_sd[:, 0:w], op=ALU.divide),
                         reads=[t_sd, tslot[si][oc]], writes=[tslot[si][oc]])
                else:
                    p.op(DVE, lambda e, oc=oc: e.tensor_tensor(out=S[:, oc, 0:w], in0=S[:, oc, 0:w], in1=st_sd2[:, 0:w], op=ALU.mult),
                         reads=[t_sd2, tslot[si][oc]], writes=[tslot[si][oc]])
                p.op(ACT, lambda e, oc=oc: e.activation(out=S[:, oc, 0:w], in_=S[:, oc, 0:w], func=AF.Identity,
                                                        bias=vec[:, bcol + oc:bcol + oc + 1], scale=vec[:, gcol + oc:gcol + oc + 1]),
                     reads=[t_const, tslot[si][oc]], writes=[tslot[si][oc]])
                if want_hb:
                    p.op(ACT, lambda e, oc=oc: e.activation(out=hbo[:, oc, 0:w], in_=S[:, oc, 0:w], func=AF.Copy),
                         reads=[tslot[si][oc]], writes=[thbo[oc]])

        def proj_resid(si, w, wname, src_fn, src_reads, pre=None):
            for s in range(2):
                wv, tw = pre[s] if pre is not None else wslice(wname, s)
                for m in range(4):
                    oc = 4 * s + m
                    bk = nps(0, 6)
                    mm_group(bk, 128, w, [(wv[:, kc, m * 128:(m + 1) * 128], src_fn(kc)) for kc in range(8)], [tw] + src_reads)
                    p.op(DVE, lambda e, oc=oc, bk=bk: e.scalar_tensor_tensor(
                        out=slot[si][:, oc, 0:w], in0=slot[si][:, oc, 0:w], scalar=ALPHA, in1=ps[bk][:, 0:w],
                        op0=ALU.mult, op1=ALU.add), reads=[tps[bk], tslot[si][oc]], writes=[tslot[si][oc]])

        def mlp_w1(si, w, l, hbi=None, thbi=None):
            hbi = hb if hbi is None else hbi
            thbi = thb if thbi is None else thbi
            for s in range(8):
                wv, tw = wslice(f"w1_{l}", s)
                for m in range(4):
                    hc = 4 * s + m
                    bk = nps(0, 6)
                    mm_group(bk, 128, w, [(wv[:, kc, m * 128:(m + 1) * 128], hbi[:, kc, 0:w]) for kc in range(8)], [tw] + thbi)
                    j = rr("rl", 2)
                    p.op(ACT, lambda e, j=j, bk=bk: e.activation(out=rlb[j][:, 0:w], in_=ps[bk][:, 0:w], func=AF.Relu),
                         reads=[tps[bk]], writes=[trl[j]])
                    p.op(POOL if hc % 4 == 3 else DVE, lambda e, j=j, hc=hc: e.tensor_tensor(out=hid[:, hc, 0:w], in0=rlb[j][:, 0:w], in1=rlb[j][:, 0:w], op=ALU.mult),
                         reads=[trl[j]], writes=[thid[hc]])

        def mlp_w2(si, w, l):
            for oc in range(8):
                wv, tw = wslice(f"w2_{l}", oc)
                bk = nps(0, 6)
                mm_group(bk, 128, w, [(wv[:, hc, :], hid[:, hc, 0:w]) for hc in range(32)], [tw] + thid)
                p.op(DVE, lambda e, oc=oc, bk=bk: e.scalar_tensor_tensor(
                    out=slot[si][:, oc, 0:w], in0=slot[si][:, oc, 0:w], scalar=ALPHA, in1=ps[bk][:, 0:w],
                    op0=ALU.mult, op1=ALU.add), reads=[tps[bk], tslot[si][oc]], writes=[tslot[si][oc]])

        def mlp(si, w, l):
            mlp_w1(si, w, l)
            mlp_w2(si, w, l)

        def phase_A(J):
            state['dmode'] = False
            o, nkc, cps, ns = WSPEC["win"]
            p.dma(SP, lambda e, s: e.dma_start(out=win.rearrange("p a b -> p (a b)"), in_=DAP(wbf, o, [[8 * 1088, 128], [1, 8 * 1088]])).then_inc(s, 16),
                  1, "win", reads=[t_wbf], writes=[t_win])
            o2 = WSPEC["wqr"][0]
            p.dma(SP, lambda e, s: e.dma_start(out=wqr.rearrange("p a b -> p (a b)"), in_=DAP(wbf, o2, [[6 * 1024, 128], [1, 6 * 1024]])).then_inc(s, 16),
                  1, "wqr", reads=[t_wbf], writes=[t_wqr])
            ci = 0
            for (c0, w) in chunks(0, J.Lq, AW) + chunks(J.Lq, J.Lk, AW):
                is_q = c0 < J.Lq
                i = ci % 2
                ci += 1
                src = DAP(xT, J.xoff + c0, [[XCOLS, 128], [128 * XCOLS, 8], [1, w]])
                p.dma(SP, lambda e, s, i=i, src=src, w=w: e.dma_start(out=xf[i][:, :, 0:w], in_=src).then_inc(s, 16), 1, "xf0", writes=[txf[i]])
                tc_src = DAP(tabs, J.t0off + c0, [[TABW, 128], [1, w]])
                ts_src = DAP(tabs, J.t0off + J.Lk + c0, [[TABW, 128], [1, w]])

                def ldt(e, s, i=i, w=w, tc_src=tc_src, ts_src=ts_src):
                    e.dma_start(out=ctab[i][:, 0:w], in_=tc_src).then_inc(s, 16)
                    e.dma_start(out=stab[i][:, 0:w], in_=ts_src).then_inc(s, 16)
                p.dma(SP, ldt, 2, f"tab{i}", writes=[ttab[i]])
                p.op(ACT, lambda e, i=i, w=w: e.activation(out=xb[i][:, :, 0:w], in_=xf[i][:, :, 0:w], func=AF.Copy), reads=[txf[i]], writes=[txb[i]])
                mlist = (list(range(6)) if is_q else []) + [6, 7]
                for m in mlist:
                    bk = nps(0, 5)
                    mm_group(bk, 128, w, [(win[:, kc, m * 128:(m + 1) * 128], xb[i][:, kc, 0:w]) for kc in range(8)], [t_win, txb[i]])
                    p.op(DVE, lambda e, m=m, bk=bk, w=w: e.tensor_copy(out=csb[:, m, 0:w], in_=ps[bk][:, 0:w]), reads=[tps[bk]], writes=[tcsb[m]])
                    p.op(DVE, lambda e, m=m, w=w: e.tensor_tensor(out=sqa[:, m, 0:w], in0=csb[:, m, 0:w], in1=csb[:, m, 0:w], op=ALU.mult),
                         reads=[tcsb[m]], writes=[tsqa[m]])
                groups = ([("q", range(6), 1.0 / QRANK, rq, t_rq)] if is_q else []) + [("kv", range(6, 8), 1.0 / KVRANK, rkv, t_rkv)]
                for (nm, ms, sc, rbuf, trb) in groups:
                    ms = list(ms)
                    bk = nps(5, 8)
                    mm_group(bk, 128, w, [(ones_bf, sqa[:, m, 0:w]) for m in ms], [t_const] + [tsqa[m] for m in ms])
                    p.op(ACT, lambda e, bk=bk, sc=sc, rbuf=rbuf, w=w: e.activation(out=rbuf[:, 0:w], in_=ps[bk][:, 0:w], func=AF.Sqrt, bias=RMS_EPS, scale=sc),
                         reads=[tps[bk]], writes=[trb])
                    p.op(DVE, lambda e, rbuf=rbuf, w=w: e.reciprocal(out=rbuf[:, 0:w], in_=rbuf[:, 0:w]), reads=[trb], writes=[trb])
                    for m in ms:
                        if m < 6:
                            dst, td, gc = cqn[:, m, c0:c0 + w], t_cqn, m
                        else:
                            dst, td, gc = ckvn[:, m - 6, c0:c0 + w], t_ckvn, m
                        p.op(DVE, lambda e, m=m, dst=dst, gc=gc, rbuf=rbuf, w=w: e.scalar_tensor_tensor(
                            out=dst, in0=csb[:, m, 0:w], scalar=vec[:, gc:gc + 1], in1=rbuf[:, 0:w], op0=ALU.mult, op1=ALU.mult),
                            reads=[tcsb[m], trb, t_const], writes=[td])
                b1 = nps(0, 5)
                mm_group(b1, 96, w, [(win[:, kc, 960:1056], xb[i][:, kc, 0:w]) for kc in range(8)], [t_win, txb[i]])
                b2 = nps(0, 5)
                mm_group(b2, 96, w, [(win[:, kc, 992:1088], xb[i][:, kc, 0:w]) for kc in range(8)], [t_win, txb[i]])
                j = rr("t", 2)
                p.op(DVE, lambda e, j=j, b1=b1, i=i, w=w: e.tensor_tensor(out=t1b[j][64:96, 0:w], in0=ps[b1][64:96, 0:w], in1=ctab[i][64:96, 0:w], op=ALU.mult),
                     reads=[tps[b1], ttab[i]], writes=[tt1[j]])
                p.op(DVE, lambda e, j=j, b2=b2, i=i, w=w: e.tensor_tensor(out=t2b[j][64:96, 0:w], in0=ps[b2][64:96, 0:w], in1=stab[i][64:96, 0:w], op=ALU.mult),
                     reads=[tps[b2], ttab[i]], writes=[tt2[j]])
                for kb in range(2):
                    p.op(POOL, lambda e, j=j, kb=kb, c0=c0, w=w: e.tensor_tensor(out=Kb[kb][64:96, c0:c0 + w], in0=t1b[j][64:96, 0:w], in1=t2b[j][64:96, 0:w], op=ALU.add),
                         reads=[tt1[j], tt2[j]], writes=[tKr[kb]])
                if is_q:
                    for rc in range(4):
                        ba = nps(0, 5)
                        mm_group(ba, 128, w, [(wqr[:, kc, rc * 128:(rc + 1) * 128], cqn[:, kc, c0:c0 + w]) for kc in range(6)], [t_wqr, t_cqn])
                        bb = nps(0, 5)
                        mm_group(bb, 128, w, [(wqr[:, kc, 512 + rc * 128:512 + (rc + 1) * 128], cqn[:, kc, c0:c0 + w]) for kc in range(6)], [t_wqr, t_cqn])
                        j = rr("t", 2)
                        p.op(DVE, lambda e, j=j, ba=ba, i=i, w=w: e.tensor_tensor(out=t1b[j][:, 0:w], in0=ps[ba][:, 0:w], in1=ctab[i][:, 0:w], op=ALU.mult),
                             reads=[tps[ba], ttab[i]], writes=[tt1[j]])
                        p.op(DVE, lambda e, j=j, bb=bb, i=i, w=w: e.tensor_tensor(out=t2b[j][:, 0:w], in0=ps[bb][:, 0:w], in1=stab[i][:, 0:w], op=ALU.mult),
                             reads=[tps[bb], ttab[i]], writes=[tt2[j]])
                        p.op(POOL, lambda e, j=j, rc=rc, c0=c0, w=w: e.tensor_tensor(out=qr[:, rc, c0:c0 + w], in0=t1b[j][:, 0:w], in1=t2b[j][:, 0:w], op=ALU.add),
                             reads=[tt1[j], tt2[j]], writes=[t_qr])

        def phase_B(J):
            nkt = (J.Lk + 127) // 128
            ktiles = [(t * 128, min(128, J.Lk - t * 128)) for t in range(nkt)]
            qch = chunks(0, J.Lq, QW)
            kch = chunks(0, J.Lk, QW)
            LA = 2
            for vb_ in range(2):
                p.op(POOL, lambda e, vb_=vb_: e.memset(Vb[vb_][:, :, 64:128], 1.0), writes=[tV[vb_]])
            oq, ok, ov = WSPEC["wqn"][0], WSPEC["wkn"][0], WSPEC["wvv"][0]

            def load_pw(pj):
                pi = pj % 2

                def ldp(e, s, pi=pi, pj=pj):
                    e.dma_start(out=pw[pi][:, 0:6, :], in_=DAP(wbf, oq + pj * 128 * 768, [[768, 128], [128, 6], [1, 128]])).then_inc(s, 16)
                    e.dma_start(out=pw[pi][:, 6:8, :], in_=DAP(wbf, ok + pj * 128 * 256, [[256, 128], [128, 2], [1, 128]])).then_inc(s, 16)
                    e.dma_start(out=pw[pi][:, 8:10, :], in_=DAP(wbf, ov + pj * 128 * 256, [[256, 128], [128, 2], [1, 128]])).then_inc(s, 16)
                p.dma(SP, ldp, 3, f"pw{pi}", reads=[t_wbf], writes=[tpw[pi]])

            def proj_head(h):
                pj, hh = h // 2, h % 2
                pi = pj % 2
                b = h % 2
                src = qr[(h % 4) * 32:(h % 4) * 32 + 32, h // 4, 0:J.Lq]
                p.dma(SP, lambda e, s, b=b, src=src: e.dma_start(out=Qb[b][64:96, 0:J.Lq], in_=src).then_inc(s, 16), 1, f"qrope{b}",
                      reads=[t_qr], writes=[tQ[b]])
                for (c0, w) in qch:
                    bk = nps(5, 8)
                    mm_group(bk, 64, w, [(pw[pi][:, kc, hh * 64:(hh + 1) * 64], cqn[:, kc, c0:c0 + w]) for kc in range(6)], [tpw[pi], t_cqn])
                    p.op(DVE, lambda e, bk=bk, c0=c0, w=w, b=b: e.tensor_copy(out=Qb[b][0:64, c0:c0 + w], in_=ps[bk][0:64, 0:w]), reads=[tps[bk]], writes=[tQ[b]])
                for (c0, w) in kch:
                    bk = nps(5, 8)
                    mm_group(bk, 64, w, [(pw[pi][:, 6 + kc, hh * 64:(hh + 1) * 64], ckvn[:, kc, c0:c0 + w]) for kc in range(2)], [tpw[pi], t_ckvn])
                    p.op(DVE, lambda e, bk=bk, c0=c0, w=w, b=b: e.tensor_copy(out=Kb[b][0:64, c0:c0 + w], in_=ps[bk][0:64, 0:w]), reads=[tps[bk]], writes=[tK[b]])

            def proj_v(pj):
                pi = pj % 2
                for g0 in range(0, nkt, 4):
                    grp = ktiles[g0:g0 + 4]
                    bk = nps(5, 8)

                    def fv(e, grp=grp, bk=bk, pi=pi):
                        for ti, (k0, rows) in enumerate(grp):
                            for kc in range(2):
                                inst = e.matmul(ps[bk][0:rows, ti * 128:(ti + 1) * 128], lhsT=ckvn[:, kc, k0:k0 + rows], rhs=pw[pi][:, 8 + kc, :],
                                                start=(kc == 0), stop=(kc == 1))
                        return inst
                    p.op(PE, fv, reads=[tpw[pi], t_ckvn], writes=[tps[bk]])
                    nfull = sum(1 for (_, r) in grp if r == 128)
                    if nfull:
                        pv = ps[bk][:, 0:nfull * 128].rearrange("p (t c) -> p t c", t=nfull)
                        p.op(DVE, lambda e, pv=pv, g0=g0, nfull=nfull, pi=pi: e.tensor_copy(out=Vb[pi][:, g0:g0 + nfull, 0:64], in_=pv[:, :, 0:64]),
                             reads=[tps[bk]], writes=[tV[pi]])
                        p.op(DVE, lambda e, pv=pv, g0=g0, nfull=nfull, pi=pi: e.tensor_copy(out=Vb[pi][:, g0:g0 + nfull, 128:192], in_=pv[:, :, 64:128]),
                             reads=[tps[bk]], writes=[tV[pi]])
                    if nfull < len(grp):
                        k0, rows = grp[-1]
                        ti = len(grp) - 1
                        t = g0 + ti
                        p.op(DVE, lambda e, bk=bk, ti=ti, t=t, rows=rows, pi=pi: e.tensor_copy(out=Vb[pi][0:rows, t, 0:64], in_=ps[bk][0:rows, ti * 128:ti * 128 + 64]),
                             reads=[tps[bk]], writes=[tV[pi]])
                        p.op(DVE, lambda e, bk=bk, ti=ti, t=t, rows=rows, pi=pi: e.tensor_copy(out=Vb[pi][0:rows, t, 128:192], in_=ps[bk][0:rows, ti * 128 + 64:ti * 128 + 128]),
                             reads=[tps[bk]], writes=[tV[pi]])

            def side_work(h):
                if h >= NH:
                    return
                if h % 2 == 0:
                    load_pw(h // 2)
                    proj_head(h)
                    proj_v(h // 2)
                else:
                    proj_head(h)

            side_work(0)
            side_work(1)
            steps = []
            for h in range(NH):
                for ci, (c0, w) in enumerate(qch):
                    for t, (k0, rows) in enumerate(ktiles):
                        steps.append((h, ci, c0, w, t, k0, rows))
            N = len(steps)
            sinfo = {}
            obank = {}
            for s in range(N + LA):
                if s < N:
                    h, ci, c0, w, t, k0, rows = steps[s]
                    b = h % 2
                    sbk = nps(0, 3)
                    p.op(PE, lambda e, sbk=sbk, rows=rows, w=w, k0=k0, c0=c0, b=b: e.matmul(
                        ps[sbk][0:rows, 0:w], lhsT=Kb[b][0:96, k0:k0 + rows], rhs=Qb[b][0:96, c0:c0 + w], start=True, stop=True),
                        reads=[tK[b], tKr[b], tQ[b]], writes=[tps[sbk]])
                    pt = rr("pt", 3)
                    p.op(ACT, lambda e, pt=pt, sbk=sbk, rows=rows, w=w: e.activation(out=PT[pt][0:rows, 0:w], in_=ps[sbk][0:rows, 0:w], func=AF.Exp, scale=SC0),
                         reads=[tps[sbk]], writes=[tPT[pt]])
                    sinfo[s] = pt
                s2 = s - LA
                if s2 >= 0:
                    h, ci, c0, w, t, k0, rows = steps[s2]
                    pt = sinfo.pop(s2)
                    pj, hh = h // 2, h % 2
                    pi = pj % 2
                    vlo = 0 if hh == 0 else 64
                    if t == 0:
                        obank[(h, ci)] = 3 + rr("td", 2)
                    ob = obank[(h, ci)]
                    p.op(PE, lambda e, ob=ob, pt=pt, rows=rows, w=w, t=t, vlo=vlo, pi=pi, last=(t == nkt - 1): e.matmul(
                        ps[ob][:, 0:w], lhsT=Vb[pi][0:rows, t, vlo:vlo + 128], rhs=PT[pt][0:rows, 0:w], start=(t == 0), stop=last),
                        reads=[tV[pi], tPT[pt]], writes=[tps[ob]])
                    if t == nkt - 1:
                        j = rr("t", 2)
                        if hh == 0:
                            olo, dlo = 0, 64
                        else:
                            olo, dlo = 64, 0
                        p.op(DVE, lambda e, j=j, ob=ob, dlo=dlo, w=w: e.reciprocal(out=t1b[j][dlo:dlo + 64, 0:w], in_=ps[ob][dlo:dlo + 64, 0:w]),
                             reads=[tps[ob]], writes=[tt1[j]])
                        p.op(POOL, lambda e, j=j, olo=olo, dlo=dlo, w=w: e.tensor_copy(out=t2b[j][olo:olo + 64, 0:w], in_=t1b[j][dlo:dlo + 64, 0:w]),
                             reads=[tt1[j]], writes=[tt2[j]])
                        p.op(DVE, lambda e, j=j, ob=ob, olo=olo, w=w, c0=c0, pj=pj: e.tensor_tensor(
                            out=attn_out[olo:olo + 64, pj, c0:c0 + w], in0=ps[ob][olo:olo + 64, 0:w], in1=t2b[j][olo:olo + 64, 0:w], op=ALU.mult),
                            reads=[tps[ob], tt2[j]], writes=[t_attn[pj]])
                        if ci == len(qch) - 1:
                            side_work(h + 2)

        def phase_C(J):
            blocks = chunks(0, J.Lq, QW)
            nb_ = len(blocks)
            hbufs = [(hb, thb), (hb2, thb2)]

            def pr_pre(bi):
                return [wslice("wo0", 0), wslice("wo0", 1)]

            def pr(bi, pre=None):
                c0, w = blocks[bi]
                si = bi % 2
                src = DAP(xT, J.xoff + c0, [[XCOLS, 128], [128 * XCOLS, 8], [1, w]])
                p.dma(SP, lambda e, s, si=si, src=src, w=w: e.dma_start(out=slot[si][:, :, 0:w], in_=src).then_inc(s, 16), 1, f"slot{si}", writes=tslot[si])
                proj_resid(si, w, "wo0", lambda kc, c0=c0, w=w: attn_out[:, kc, c0:c0 + w], t_attn, pre=pre)

            def ln1(bi):
                c0, w = blocks[bi]
                hbo, thbo = hbufs[bi % 2]
                layernorm(bi % 2, w, 8, 24, hbo=hbo, thbo=thbo)

            def spill(bi):
                c0, w = blocks[bi]
                si = bi % 2
                lo = max(c0, J.n_pre)
                hi = min(c0 + w, J.n_pre + J.n_own)
                if hi > lo:
                    dst = DAP(h2s, lo - J.n_pre, [[2048, 128], [128 * 2048, 8], [1, hi - lo]])
                    p.dma(SP, lambda e, s, si=si, dst=dst, a=lo - c0, b=hi - c0: e.dma_start(out=dst, in_=slot[si][:, :, a:b]).then_inc(s, 16), 1, "h2s_w",
                          reads=tslot[si], writes=[t_h2s])
                dstb = DAP(h2b, c0, [[LSQ, 128], [128 * LSQ, 8], [1, w]])
                p.dma(SP, lambda e, s, dstb=dstb, w=w: e.dma_start(out=dstb, in_=hbs[:, :, 0:w]).then_inc(s, 16), 1, "h2b_w", reads=thbs, writes=[t_h2b])

            pr(0)
            ln1(0)
            if nb_ > 1:
                pr(1)
            mlp_w1(0, blocks[0][1], 0, *hbufs[0])
            for bi, (c0, w) in enumerate(blocks):
                si = bi % 2
                if bi + 1 < nb_:
                    ln1(bi + 1)
                mlp_w2(si, w, 0)
                layernorm(si, w, 40, 56, hbo=hbs, thbo=thbs)
                if bi + 1 < nb_:
                    mlp_w1((bi + 1) % 2, blocks[bi + 1][1], 0, *hbufs[(bi + 1) % 2])
                pre = pr_pre(bi + 2) if bi + 2 < nb_ else None
                spill(bi)
                if bi + 2 < nb_:
                    pr(bi + 2, pre)

        def phase_D(J, finals):
            state['dmode'] = True
            p.op(POOL, lambda e: e.memset(v1a[:, :, :, 64:128], 1.0), writes=[tv1])
            nblk_d = J.n_own // 512
            def geom(jb):
                si = jb % 2
                s0 = J.n_pre + 512 * jb
                lo = max(s0 - 128, 0)
                hi = min(s0 + 640, J.R)
                nwt = (hi - lo) // 128
                ww = hi - lo
                WT = ww + NMETA
                return si, s0, lo, hi, nwt, ww, WT

            def load_proj(jb):
                si, s0, lo, hi, nwt, ww, WT = geom(jb)
                src = DAP(h2s, 512 * jb, [[2048, 128], [128 * 2048, 8], [1, 512]])
                p.dma(SP, lambda e, s, si=si, src=src: e.dma_start(out=slot[si], in_=src).then_inc(s, 16), 1, f"slot{si}", reads=[t_h2s], writes=tslot[si])

                def ldw(e, s, lo=lo, ww=ww):
                    e.dma_start(out=hbw[:, :, 0:ww], in_=DAP(h2b, lo, [[LSQ, 128], [128 * LSQ, 8], [1, ww]])).then_inc(s, 16)
                    e.dma_start(out=hbw[:, :, ww:ww + NMETA], in_=DAP(h2b, J.meta, [[LSQ, 128], [128 * LSQ, 8], [1, NMETA]])).then_inc(s, 16)
                p.dma(SP, ldw, 2, "hbw", reads=[t_h2b], writes=[thbw])
                t1L = J.Lq

                def ldt1(e, s, lo=lo, ww=ww):
                    e.dma_start(out=c1t[:, 0:ww], in_=DAP(tabs, J.t1off + lo, [[TABW, 128], [1, ww]])).then_inc(s, 16)
                    e.dma_start(out=c1t[:, ww:ww + NMETA], in_=DAP(tabs, J.t1off + J.meta, [[TABW, 128], [1, NMETA]])).then_inc(s, 16)
                    e.dma_start(out=s1t[:, 0:ww], in_=DAP(tabs, J.t1off + t1L + lo, [[TABW, 128], [1, ww]])).then_inc(s, 16)
                    e.dma_start(out=s1t[:, ww:ww + NMETA], in_=DAP(tabs, J.t1off + t1L + J.meta, [[TABW, 128], [1, NMETA]])).then_inc(s, 16)
                p.dma(SP, ldt1, 4, "t1t", writes=[tt1t])
                wk, twk = wslice("wk1", 0)
                wks, twks = wslice("wk1s", 0)
                for kv in range(4):
                    for (c0, w) in chunks(0, WT, 512):
                        ba = nps(0, 6)
                        mm_group(ba, 128, w, [(wk[:, kc, kv * 128:(kv + 1) * 128], hbw[:, kc, c0:c0 + w]) for kc in range(8)], [twk, thbw])
                        bb = nps(0, 6)
                        mm_group(bb, 128, w, [(wks[:, kc, kv * 128:(kv + 1) * 128], hbw[:, kc, c0:c0 + w]) for kc in range(8)], [twks, thbw])
                        j = rr("t", 2)
                        p.op(DVE, lambda e, j=j, ba=ba, c0=c0, w=w: e.tensor_tensor(out=t1b[j][:, 0:w], in0=ps[ba][:, 0:w], in1=c1t[:, c0:c0 + w], op=ALU.mult),
                             reads=[tps[ba], tt1t], writes=[tt1[j]])
                        p.op(DVE, lambda e, j=j, bb=bb, c0=c0, w=w: e.tensor_tensor(out=t2b[j][:, 0:w], in0=ps[bb][:, 0:w], in1=s1t[:, c0:c0 + w], op=ALU.mult),
                             reads=[tps[bb], tt1t], writes=[tt2[j]])
                        p.op(POOL, lambda e, j=j, kv=kv, c0=c0, w=w: e.tensor_tensor(out=k1[:, kv, c0:c0 + w], in0=t1b[j][:, 0:w], in1=t2b[j][:, 0:w], op=ALU.add),
                             reads=[tt1[j], tt2[j]], writes=[tk1])
                wv_, twv = wslice("wv1", 0)
                vt = [(t * 128, 128) for t in range(nwt)] + [(ww, NMETA)]
                for ti, (k0, rows) in enumerate(vt):
                    bk = nps(0, 6)
                    mm_group(bk, rows, 256, [(hbw[:, kc, k0:k0 + rows], wv_[:, kc, :]) for kc in range(8)], [twv, thbw])
                    pv = ps[bk][0:rows, 0:256].rearrange("p (k c) -> p k c", k=4)
                    p.op(DVE, lambda e, pv=pv, ti=ti, rows=rows: e.tensor_copy(out=v1a[0:rows, ti, :, 0:64], in_=pv), reads=[tps[bk]], writes=[tv1])
                    p.op(DVE, lambda e, pv=pv, ti=ti, rows=rows: e.tensor_copy(out=v1a[0:rows, ti, :, 128:192], in_=pv), reads=[tps[bk]], writes=[tv1])
                qoff = s0 - lo
                for s in range(2):
                    wq_, twq = wslice("wq1", s)
                    wqs_, twqs = wslice("wq1s", s)
                    for m in range(4):
                        oc = 4 * s + m
                        ba = nps(0, 6)
                        mm_group(ba, 128, 512, [(wq_[:, kc, m * 128:(m + 1) * 128], hbw[:, kc, qoff:qoff + 512]) for kc in range(8)], [twq, thbw])
                        bb = nps(0, 6)
                        mm_group(bb, 128, 512, [(wqs_[:, kc, m * 128:(m + 1) * 128], hbw[:, kc, qoff:qoff + 512]) for kc in range(8)], [twqs, thbw])
                        j = rr("t", 2)
                        p.op(DVE, lambda e, j=j, ba=ba, qoff=qoff: e.tensor_tensor(out=t1b[j], in0=ps[ba][:, :], in1=c1t[:, qoff:qoff + 512], op=ALU.mult),
                             reads=[tps[ba], tt1t], writes=[tt1[j]])
                        p.op(DVE, lambda e, j=j, bb=bb, qoff=qoff: e.tensor_tensor(out=t2b[j], in0=ps[bb][:, :], in1=s1t[:, qoff:qoff + 512], op=ALU.mult),
                             reads=[tps[bb], tt1t], writes=[tt2[j]])
                        p.op(POOL, lambda e, j=j, oc=oc: e.tensor_tensor(out=q1[:, oc, :], in0=t1b[j], in1=t2b[j], op=ALU.add),
                             reads=[tt1[j], tt2[j]], writes=[tq1])

            def attn(jb):
                si, s0, lo, hi, nwt, ww, WT = geom(jb)
                nob = J.n_own // 128
                dsteps = []
                for qb in range(4):
                    gq = (s0 - J.n_pre) // 128 + qb
                    wt_own = (s0 - lo) // 128 + qb
                    tiles = []
                    if wt_own - 1 >= 0:
                        first = (gq == 0)
                        tiles.append((wt_own - 1, 128, 2 if (first and J.special) else 0))
                    tiles.append((wt_own, 128, None))
                    if wt_own + 1 < nwt:
                        lastb = (gq == nob - 1)
                        tiles.append((wt_own + 1, 128, 3 if (lastb and J.special) else 1))
                    tiles.append((nwt, NMETA, None))
                    for kv in range(4):
                        for ii, (ti, rows, mk) in enumerate(tiles):
                            dsteps.append((qb, kv, ii, ti, rows, mk, ii == len(tiles) - 1))
                ND = len(dsteps)
                LAD = 2
                pend = []
                dinfo = {}
                oset = {}
                for s_ in range(ND + LAD):
                    if s_ < ND:
                        qb, kv, ii, ti, rows, mk, lastt = dsteps[s_]
                        hf = 0
                        sbk = 2 * (s_ % 3)
                        sbk2 = sbk + 1
                        c_lo = hf * 256
                        k0 = ti * 128 if ti < nwt else ww
                        p.op(PE, lambda e, sbk=sbk, rows=rows, k0=k0, kv=kv, qb=qb, c_lo=c_lo: e.matmul(
                            ps[sbk][0:rows, c_lo:c_lo + 256], lhsT=k1[0:64, kv, k0:k0 + rows],
                            rhs=q1[0:64, 2 * kv:2 * kv + 2, qb * 128:(qb + 1) * 128], start=True, stop=True),
                            reads=[tk1, tq1], writes=[tps[sbk]])
                        p.op(PE, lambda e, sbk2=sbk2, rows=rows, k0=k0, kv=kv, qb=qb, c_lo=c_lo: e.matmul(
                            ps[sbk2][0:rows, c_lo:c_lo + 256], lhsT=k1[64:128, kv, k0:k0 + rows],
                            rhs=q1[64:128, 2 * kv:2 * kv + 2, qb * 128:(qb + 1) * 128], start=True, stop=True),
                            reads=[tk1, tq1], writes=[tps[sbk2]])
                        pt = rr("pt", 4)
                        def emit_exp(pt=pt, sbk=sbk, sbk2=sbk2, rows=rows, c_lo=c_lo, mk=mk, hf=hf):
                            if MERGE_EXP:
                                src = psall[0:rows, sbk * 512:(sbk + 2) * 512].rearrange("p (k c) -> p k c", k=2)[:, :, c_lo:c_lo + 256]
                                dstp = PT[pt][0:rows, :].rearrange("p (k c) -> p k c", k=2)
                                p.op(ACT, lambda e, src=src, dstp=dstp: e.activation(out=dstp, in_=src, func=AF.Exp, scale=SC1),
                                     reads=[tsh[sbk][hf], tsh[sbk2][hf]], writes=[tPT[pt]])
                            else:
                                p.op(ACT, lambda e, pt=pt, sbk=sbk, rows=rows, c_lo=c_lo: e.activation(out=PT[pt][0:rows, 0:256], in_=ps[sbk][0:rows, c_lo:c_lo + 256], func=AF.Exp, scale=SC1),
                                     reads=[tps[sbk]], writes=[tPT[pt]])
                                p.op(ACT, lambda e, pt=pt, sbk2=sbk2, rows=rows, c_lo=c_lo: e.activation(out=PT[pt][0:rows, 256:512], in_=ps[sbk2][0:rows, c_lo:c_lo + 256], func=AF.Exp, scale=SC1),
                                     reads=[tps[sbk2]], writes=[tPT[pt]])
                            if mk is not None:
                                p.op(DVE, lambda e, pt=pt, mk=mk: e.tensor_tensor(out=PT[pt], in0=PT[pt], in1=masks_bf[:, mk, :], op=ALU.mult),
                                     reads=[tPT[pt], t_const], writes=[tPT[pt]])

                        exp_fn = emit_exp
                        dinfo[s_] = pt
                    else:
                        exp_fn = None
                    s2 = s_ - LAD
                    if s2 >= 0:
                        qb, kv, ii, ti, rows, mk, lastt = dsteps[s2]
                        pt = dinfo.pop(s2)
                        obe, obo = 6, 7
                        p.op(PE, lambda e, pt=pt, rows=rows, ti=ti, kv=kv, ii=ii, lastt=lastt, obe=obe: e.matmul(
                            ps[obe][:, 0:256], lhsT=v1a[0:rows, ti, kv, 0:128], rhs=PT[pt][0:rows, 0:256], start=(ii == 0), stop=lastt),
                            reads=[tv1, tPT[pt]], writes=[tps[obe]])
                        p.op(PE, lambda e, pt=pt, rows=rows, ti=ti, kv=kv, ii=ii, lastt=lastt, obo=obo: e.matmul(
                            ps[obo][:, 0:256], lhsT=v1a[0:rows, ti, kv, 64:192], rhs=PT[pt][0:rows, 256:512], start=(ii == 0), stop=lastt),
                            reads=[tv1, tPT[pt]], writes=[tps[obo]])
                        if lastt:
                            jo = rr("t", 2)
                            p.op(ACT, lambda e, jo=jo: e.activation(out=t1b[jo][:, 0:256], in_=ps[6][:, 0:256], func=AF.Copy), reads=[tps[6]], writes=[tt1[jo]])
                            p.op(ACT, lambda e, jo=jo: e.activation(out=t1b[jo][:, 256:512], in_=ps[7][:, 0:256], func=AF.Copy), reads=[tps[7]], writes=[tt1[jo]])
                            j = rr("rl", 2)
                            p.op(DVE, lambda e, j=j, jo=jo, kv=kv: e.tensor_tensor(
                                out=dtb[j][64:128, 0:256], in0=t1b[jo][64:128, 0:256], in1=estab[64:128, kv, 0:256], op=ALU.add),
                                reads=[tt1[jo], t_const], writes=[tdtb[j]])
                            p.op(DVE, lambda e, j=j, jo=jo, kv=kv: e.tensor_tensor(
                                out=dtb[j][0:64, 0:256], in0=t1b[jo][0:64, 256:512], in1=estab[0:64, kv, 256:512], op=ALU.add),
                                reads=[tt1[jo], t_const], writes=[tdtb[j]])
                            p.op(DVE, lambda e, j=j: e.reciprocal(out=dtb[j][:, 0:256], in_=dtb[j][:, 0:256]), reads=[tdtb[j]], writes=[tdtb[j]])
                            p.op(POOL, lambda e, j=j: e.tensor_copy(out=tmpd[j][0:64, 0:256], in_=dtb[j][64:128, 0:256]), reads=[tdtb[j]], writes=[ttmpd[j]])
                            p.op(POOL, lambda e, j=j: e.tensor_copy(out=tmpd[j][64:128, 0:256], in_=dtb[j][0:64, 0:256]), reads=[tdtb[j]], writes=[ttmpd[j]])
                            for fnp in pend:
                                fnp()
                            pend.clear()

                            def fin(j=j, jo=jo, kv=kv, qb=qb):
                                for (olo, ecol) in [(0, 0), (64, 256)]:
                                    p.op(DVE, lambda e, olo=olo, ecol=ecol: e.tensor_tensor(
                                        out=a1[olo:olo + 64, 2 * kv:2 * kv + 2, qb * 128:(qb + 1) * 128],
                                        in0=t1b[jo][olo:olo + 64, ecol:ecol + 256].rearrange("p (a b) -> p a b", a=2),
                                        in1=tmpd[j][olo:olo + 64, 0:256].rearrange("p (a b) -> p a b", a=2), op=ALU.mult),
                                        reads=[tt1[jo], ttmpd[j]], writes=[ta1[2 * kv], ta1[2 * kv + 1]])
                            pend.append(fin)
                    if exp_fn is not None:
                        exp_fn()
                for fnp in pend:
                    fnp()
                pend.clear()

            load_proj(0)
            attn(0)
            state['dmode_keep'] = True
            for jb in range(nblk_d):
                si = jb % 2
                proj_resid(si, 512, "wo1", lambda kc: a1[:, kc, :], ta1)
                layernorm(si, 512, 16, 32)
                if jb + 1 < nblk_d:
                    load_proj(jb + 1)
                mlp(si, 512, 1)
                layernorm(si, 512, 48, 64, want_hb=False)
                dst = DAP(yT, J.yidx * DM * 2048 + 512 * jb, [[2048, 128], [128 * 2048, 8], [1, 512]])
                finals.append(p.dma(SP, lambda e, s, si=si, dst=dst: e.dma_start(out=dst, in_=slot[si]).then_inc(s, 16), 1, f"y{si}", reads=tslot[si]))
                if jb + 1 < nblk_d:
                    attn(jb + 1)

        finals = []
        ld_consts()
        prologue()
        p.barrier()
        phs = dbg["phases"] if dbg else "ABCD"
        if dbg:
            jobs = [jobs[i] for i in dbg["jobs"]]

        def dump(dt_, view2d, key):
            finals.append(p.dma(SP, lambda e, s: e.dma_start(out=dt_.ap(), in_=view2d).then_inc(s, 16), 1, key))
        for J in jobs:
            if "A" in phs:
                phase_A(J)
                p.barrier()
                if dbg:
                    dump(d_cqn, cqn.rearrange("p a b -> p (a b)"), "dc1")
                    dump(d_ckvn, ckvn.rearrange("p a b -> p (a b)"), "dc2")
                    dump(d_qr, qr.rearrange("p a b -> p (a b)"), "dc3")
                    dump(d_k0, Kb[0], "dc4")
                    p.barrier()
            if "B" in phs:
                phase_B(J)
                p.barrier()
                if dbg:
                    dump(d_attn, attn_out.rearrange("p a b -> p (a b)"), "dc5")
                    p.barrier()
            if "C" in phs:
                phase_C(J)
                p.barrier()
            if "D" in phs:
                phase_D(J, finals)
                p.barrier()
        last = {}
        for d in finals:
            last[d.semkey] = d
        build_block(nc, p, list(last.values()))
    return nc


_CACHE = {}


def kernel(x_prompt, x_sample, meta_tokens, mla_w_in, mla_g_q, mla_w_uq, mla_g_kv, mla_w_ukv, mla_w_o,
           gqa_w_qkv, gqa_sink, gqa_w_o, mlp_w1, mlp_w2, ln1_g, ln1_b, ln2_g, ln2_b):
    inp = dict(mla_w_in=np.asarray(mla_w_in), mla_w_uq=np.asarray(mla_w_uq), mla_w_ukv=np.asarray(mla_w_ukv), mla_w_o=np.asarray(mla_w_o),
               gqa_w_qkv=np.asarray(gqa_w_qkv), gqa_w_o=np.asarray(gqa_w_o), mlp_w1=np.asarray(mlp_w1), mlp_w2=np.asarray(mlp_w2))
    wall, nblk = build_wall(inp)
    x_prompt = np.asarray(x_prompt, np.float32)
    x_sample = np.asarray(x_sample, np.float32)
    metaT = np.asarray(meta_tokens, np.float32).T

    def pc(v):
        return np.asarray(v, np.float32).reshape(-1, 128).T
    vec = np.zeros((128, NVEC), np.float32)
    vec[:, 0:6] = pc(mla_g_q[0])
    vec[:, 6:8] = pc(mla_g_kv[0])
    vec[:, 8:16] = pc(ln1_g[0]); vec[:, 16:24] = pc(ln1_g[1])
    vec[:, 24:32] = pc(ln1_b[0]); vec[:, 32:40] = pc(ln1_b[1])
    vec[:, 40:48] = pc(ln2_g[0]); vec[:, 48:56] = pc(ln2_g[1])
    vec[:, 56:64] = pc(ln2_b[0]); vec[:, 64:72] = pc(ln2_b[1])
    vec[:, 72:88] = np.broadcast_to(np.asarray(gqa_sink, np.float32).reshape(1, 16), (128, 16))

    posP = np.concatenate([16 + np.arange(2048), np.arange(16)])
    CP0, SP0 = mla_tabs(posP)
    CP1, SP1 = gqa_tabs(posP)
    kk = np.arange(128)[:, None]
    ii = np.arange(128)[None, :]
    tri_ge = np.tile((kk >= ii).astype(np.float32), (1, 4))
    tri_le = np.tile((kk <= ii).astype(np.float32), (1, 4))

    in_maps = []
    for c in range(NCORES):
        sq, half = c // 2, c % 2
        xT = np.empty((DM, XCOLS), np.float32)
        for i in range(4):
            xT[:, i * LP:i * LP + 2048] = x_prompt[4 * c + i].T
            xT[:, i * LP + 2048:(i + 1) * LP] = metaT
        xs = x_sample[sq]
        if half == 0:
            own = np.arange(0, 2048); post = np.arange(2048, 2176); pre = np.arange(3968, 4096)
            rest = np.arange(2176, 3968)
            pre_valid, post_valid = 0.0, 1.0
        else:
            own = np.arange(2048, 4096); pre = np.arange(1920, 2048); post = np.arange(0, 128)
            rest = np.arange(128, 1920)
            pre_valid, post_valid = 1.0, 0.0
        order = np.concatenate([pre, own, post])
        b = 4 * LP
        xT[:, b:b + 2304] = xs[order].T
        xT[:, b + 2304:b + 2320] = metaT
        xT[:, b + 2320:b + LSK] = xs[rest].T
        posS = np.concatenate([16 + order, np.arange(16), 16 + rest])
        CS0, SS0 = mla_tabs(posS)
        CS1, SS1 = gqa_tabs(posS[:LSQ])
        tabs = np.concatenate([CP0, SP0, CS0, SS0, CP1, SP1, CS1, SS1], axis=1).astype(np.float32)
        assert tabs.shape == (128, TABW)
        masks = np.concatenate([tri_ge, tri_le, tri_ge * pre_valid, tri_le * post_valid], axis=1).astype(np.float32)
        in_maps.append({"xT": xT, "wall": wall, "tabs": np.ascontiguousarray(tabs), "masks": np.ascontiguousarray(masks), "vec": vec})

    if nblk not in _CACHE:
        _CACHE[nblk] = build_program(nblk)
    nc = _CACHE[nblk]
    res = run_bass_kernel_spmd(nc, in_maps, core_ids=list(range(NCORES)))
    y_prompt = np.empty((32, 2048, DM), np.float32)
    y_sample = np.empty((4, 4096, DM), np.float32)
    for c in range(NCORES):
        yT = res.results[c]["yT"]
        for i in range(4):
            y_prompt[4 * c + i] = yT[i].T
        sq, half = c // 2, c % 2
        y_sample[sq, half * 2048:(half + 1) * 2048] = yT[4].T
    return (y_prompt, y_sample)
```

```python
import contextlib
import numpy as np
import ml_dtypes
import concourse.bass as bass
import concourse.mybir as mybir
from concourse.bass_utils import run_bass_kernel_spmd

F32 = mybir.dt.float32
BF16 = mybir.dt.bfloat16
ALU = mybir.AluOpType
AF = mybir.ActivationFunctionType

PE, ACT, DVE, POOL, SP = "tensor", "scalar", "vector", "gpsimd", "sync"
ENGS = [PE, ACT, DVE, POOL, SP]
SEM_EPOCH = 30000

DM = 1024
NMETA = 16
NH = 16
QRANK = 768
KVRANK = 256
DFF = 4096
ALPHA = float((2.0 * 2) ** 0.25)
LN_EPS = 1e-5
RMS_EPS = 1e-6
THETA = 500000.0
SC0 = float(96 ** -0.5)
SC1 = float(64 ** -0.5)
NCORES = 8
AW = 256
BAR_D = False
USE_DIV = False
MERGE_EXP = False
HALFBANK = False
QW = 512


class T:
    __slots__ = ("name", "last_w", "readers")

    def __init__(self, name=""):
        self.name = name
        self.last_w = None
        self.readers = []


class Op:
    __slots__ = ("eng", "fn", "deps", "needs_inc", "is_dma", "semkey", "dma_val", "inc_val")

    def __init__(self, eng, fn):
        self.eng = eng
        self.fn = fn
        self.deps = []
        self.needs_inc = False
        self.is_dma = False
        self.semkey = None
        self.dma_val = 0
        self.inc_val = None


class Prog:
    def __init__(self):
        self.ops = {e: [] for e in ENGS}
        self.dma_last = {}
        self.dma_count = {}
        self.last_op = {e: None for e in ENGS}
        self.pending_dma = []

    def _track(self, op, reads, writes):
        deps = []
        for t in reads:
            if t.last_w is not None:
                deps.append(t.last_w)
        for t in writes:
            if t.last_w is not None:
                deps.append(t.last_w)
            deps.extend(t.readers)
        for t in reads:
            t.readers.append(op)
        for t in writes:
            t.last_w = op
            t.readers = []
        seen = set()
        for d in deps:
            if d is op or id(d) in seen:
                continue
            seen.add(id(d))
            if d.eng == PE and op.eng == PE and not d.is_dma and not op.is_dma:
                continue
            op.deps.append(d)
            if not d.is_dma:
                d.needs_inc = True

    def op(self, eng, fn, reads=(), writes=()):
        o = Op(eng, fn)
        self._track(o, reads, writes)
        self.ops[eng].append(o)
        self.last_op[eng] = o
        return o

    def dma(self, eng, fn, ndma, semkey, reads=(), writes=()):
        o = Op(eng, fn)
        o.is_dma = True
        o.semkey = semkey
        prev = self.dma_last.get(semkey)
        self._track(o, reads, writes)
        if prev is not None and all(d is not prev for d in o.deps):
            o.deps.append(prev)
        self.dma_last[semkey] = o
        self.dma_count[semkey] = self.dma_count.get(semkey, 0) + 16 * ndma
        o.dma_val = self.dma_count[semkey]
        self.ops[eng].append(o)
        self.pending_dma.append(o)
        return o

    def barrier(self):
        lasts = [self.last_op[e] for e in ENGS if self.last_op[e] is not None and not self.last_op[e].is_dma]
        comp = []
        for e in ENGS:
            for o in reversed(self.ops[e]):
                if not o.is_dma and o.fn is not None:
                    comp.append(o)
                    break
        dmas = list(self.pending_dma)
        self.pending_dma = []
        for e in ENGS:
            b = Op(e, None)
            for d in comp:
                if d.eng != e:
                    b.deps.append(d)
                    d.needs_inc = True
            b.deps.extend(dmas)
            self.ops[e].append(b)


def build_block(nc, prog, final_dmas):
    counts = {e: 0 for e in ENGS}
    for e in ENGS:
        for o in prog.ops[e]:
            if (not o.is_dma) and o.needs_inc:
                counts[e] += 1
                o.inc_val = counts[e]
    with contextlib.ExitStack() as st:
        esems = {}
        for e in ENGS:
            n_ep = counts[e] // SEM_EPOCH + 1
            esems[e] = [st.enter_context(nc.semaphore(f"s_{e}_{k}")) for k in range(n_ep)]
        dsems = {}
        for k in prog.dma_count:
            dsems[k] = st.enter_context(nc.semaphore(f"d_{len(dsems)}"))
        block = st.enter_context(nc.Block())

        def sem_of(op):
            if op.is_dma:
                return dsems[op.semkey], op.dma_val, ("d", op.semkey)
            ep = (op.inc_val - 1) // SEM_EPOCH
            return esems[op.eng][ep], op.inc_val - ep * SEM_EPOCH, (op.eng, ep)

        def make(e):
            def body(eng):
                seen = {}
                for o in prog.ops[e]:
                    for d in o.deps:
                        sem, val, key = sem_of(d)
                        if seen.get(key, 0) >= val:
                            continue
                        eng.wait_ge(sem, val)
                        seen[key] = val
                    if o.fn is None:
                        continue
                    if o.is_dma:
                        o.fn(eng, dsems[o.semkey])
                    else:
                        inst = o.fn(eng)
                        if o.needs_inc:
                            ep = (o.inc_val - 1) // SEM_EPOCH
                            inst.then_inc(esems[e][ep], 1)
                if e == SP:
                    for d in final_dmas:
                        sem, val, key = sem_of(d)
                        eng.wait_ge(sem, val)
            return body

        block.tensor(make(PE))
        block.scalar(make(ACT))
        block.vector(make(DVE))
        block.gpsimd(make(POOL))
        block.sync(make(SP))


def tile_w(W, cps):
    Kd, Nd = W.shape
    nkc = Kd // 128
    ns = Nd // cps
    a = W.reshape(nkc, 128, ns, cps).transpose(2, 1, 0, 3)
    return np.ascontiguousarray(a).reshape(-1)


WSPEC = {}


def build_wall(inp):
    parts = []
    off = 0

    def add(name, W, cps):
        nonlocal off
        W = np.asarray(W, np.float32)
        flat = tile_w(W, cps)
        WSPEC[name] = (off, W.shape[0] // 128, cps, W.shape[1] // cps)
        parts.append(flat)
        off += flat.size

    w_in = inp["mla_w_in"][0]
    p32 = (np.arange(32) + 16) % 32
    add("win", np.concatenate([w_in, w_in[:, 1024 + p32]], axis=1), 1088)
    wuq = inp["mla_w_uq"][0].reshape(QRANK, NH, 96)
    add("wqr", np.concatenate([wuq[:, :, 64:].reshape(QRANK, 512), wuq[:, :, 64 + p32].reshape(QRANK, 512)], axis=1), 1024)
    add("wqn", wuq[:, :, :64].reshape(QRANK, 1024), 128)
    wukv = inp["mla_w_ukv"][0].reshape(KVRANK, NH, 128)
    add("wkn", wukv[:, :, :64].reshape(KVRANK, 1024), 128)
    add("wvv", wukv[:, :, 64:].reshape(KVRANK, 1024), 128)
    add("wo0", inp["mla_w_o"][0], 512)
    for l in range(2):
        add(f"w1_{l}", inp["mlp_w1"][l], 512)
        add(f"w2_{l}", inp["mlp_w2"][l], 128)
    wqkv = inp["gqa_w_qkv"][0]
    p64 = np.arange(64)
    p64[:16] = (np.arange(16) + 8) % 16
    wq = wqkv[:, :1024].reshape(DM, 16, 64)
    add("wq1", wq.reshape(DM, 1024), 512)
    add("wq1s", wq[:, :, p64].reshape(DM, 1024), 512)
    wk = wqkv[:, 1024:1280].reshape(DM, 4, 64)
    wkd = np.concatenate([wk, wk], axis=2)
    wks = wk[:, :, p64]
    wksd = np.concatenate([wks, wks], axis=2)
    add("wk1", wkd.reshape(DM, 512), 512)
    add("wk1s", wksd.reshape(DM, 512), 512)
    add("wv1", wqkv[:, 1280:1536], 256)
    add("wo1", inp["gqa_w_o"][0], 512)
    tot = off
    blk = 128 * 2048
    nb = (tot + blk - 1) // blk
    wall = np.zeros(nb * blk, np.float32)
    wall[:tot] = np.concatenate(parts)
    return wall, nb


def rope_cs(pos, dim):
    inv = THETA ** (-np.arange(0, dim, 2, dtype=np.float32) / np.float32(dim))
    ang = pos.astype(np.float32)[:, None] * inv[None, :].astype(np.float32)
    return np.cos(ang).astype(np.float32), np.sin(ang).astype(np.float32)


def mla_tabs(pos):
    c, s = rope_cs(pos, 32)
    C32 = np.concatenate([c, c], axis=1).T
    S32 = np.concatenate([-s, s], axis=1).T
    return np.tile(C32, (4, 1)), np.tile(S32, (4, 1))


def gqa_tabs(pos):
    c, s = rope_cs(pos, 16)
    L = pos.shape[0]
    C64 = np.ones((64, L), np.float32)
    S64 = np.zeros((64, L), np.float32)
    C64[:8] = c.T
    C64[8:16] = c.T
    S64[:8] = -s.T
    S64[8:16] = s.T
    return np.tile(C64, (2, 1)), np.tile(S64, (2, 1))


class Job:
    def __init__(self, n_pre, n_own, n_post, Lk, xoff, t0off, t1off, yidx, special_masks):
        self.n_pre, self.n_own, self.n_post, self.Lk = n_pre, n_own, n_post, Lk
        self.R = n_pre + n_own + n_post
        self.Lq = self.R + NMETA
        self.meta = self.R
        self.xoff, self.t0off, self.t1off, self.yidx = xoff, t0off, t1off, yidx
        self.special = special_masks


LP = 2064
LSQ = 2320
LSK = 4112
XCOLS = 4 * LP + LSK
TAB_P0 = 0
TAB_S0 = 2 * LP
TAB_P1 = TAB_S0 + 2 * LSK
TAB_S1 = TAB_P1 + 2 * LP
TABW = TAB_S1 + 2 * LSQ
NVEC = 88


def chunks(lo, hi, w):
    out = []
    c = lo
    while c < hi:
        out.append((c, min(w, hi - c)))
        c += w
    return out


def build_program(nblk, dbg=None):
    nc = bass.Bass("TRN2", target_bir_lowering=False)
    blk = 128 * 2048
    xT = nc.dram_tensor("xT", [DM, XCOLS], F32, kind="ExternalInput")
    wall = nc.dram_tensor("wall", [nblk * blk], F32, kind="ExternalInput")
    tabs = nc.dram_tensor("tabs", [128, TABW], F32, kind="ExternalInput")
    masks = nc.dram_tensor("masks", [128, 4 * 512], F32, kind="ExternalInput")
    vecd = nc.dram_tensor("vec", [128, NVEC], F32, kind="ExternalInput")
    yT = nc.dram_tensor("yT", [5, DM, 2048], F32, kind="ExternalOutput")
    wbf = nc.dram_tensor("wbf", [nblk * blk], BF16)
    dk = dict(kind="ExternalOutput") if dbg else {}
    h2s = nc.dram_tensor("h2s", [DM, 2048], F32, **dk)
    h2b = nc.dram_tensor("h2b", [DM, LSQ], BF16, **dk)
    if dbg:
        d_cqn = nc.dram_tensor("d_cqn", [128, 6 * LSQ], BF16, kind="ExternalOutput")
        d_ckvn = nc.dram_tensor("d_ckvn", [128, 2 * LSK], BF16, kind="ExternalOutput")
        d_qr = nc.dram_tensor("d_qr", [128, 4 * LSQ], BF16, kind="ExternalOutput")
        d_k0 = nc.dram_tensor("d_k0", [128, LSK], BF16, kind="ExternalOutput")
        d_attn = nc.dram_tensor("d_attn", [128, 8 * LSQ], BF16, kind="ExternalOutput")

    def DAP(t, off, pairs):
        return bass.AP(t, off, [list(p) for p in pairs])

    jobs = [Job(0, 2048, 0, LP, i * LP, TAB_P0, TAB_P1, i, False) for i in range(4)]
    jobs.append(Job(128, 2048, 128, LSK, 4 * LP, TAB_S0, TAB_S1, 4, True))

    st = contextlib.ExitStack()
    with st:
        ARB = 206 * 1024
        arena = st.enter_context(nc.sbuf_tensor("arena", [128, ARB // 2], BF16))
        psall = st.enter_context(nc.psum_tensor("psall", [128, 4096], F32))
        ps = [psall[:, i * 512:(i + 1) * 512] for i in range(8)]
        tps = [T(f"ps{i}") for i in range(8)]
        p = Prog()

        class Alloc:
            def __init__(self, base):
                self.o = base

            def get(self, nbytes):
                o = self.o
                self.o += (nbytes + 63) // 64 * 64
                assert self.o <= ARB, ("SBUF arena overflow", self.o)
                return o

        def vw(off, dt, shape):
            n = int(np.prod(shape))
            if dt == BF16:
                a = arena[:, off // 2: off // 2 + n]
            else:
                a = arena[:, off // 2: off // 2 + 2 * n].bitcast(F32)
            if len(shape) == 2:
                return a.rearrange("p (a b) -> p a b", a=shape[0])
            if len(shape) == 3:
                return a.rearrange("p (a b c) -> p a b c", a=shape[0], b=shape[1])
            return a

        def buf(al, dt, shape):
            nb = int(np.prod(shape)) * (2 if dt == BF16 else 4)
            return vw(al.get(nb), dt, shape)

        com = Alloc(0)
        ones_bf = buf(com, BF16, [128])
        masks_bf = buf(com, BF16, [4, 512])
        estab = buf(com, F32, [4, 512])
        vec = buf(com, F32, [NVEC])
        PT = [buf(com, BF16, [512]) for _ in range(4)]
        tPT = [T() for _ in range(4)]
        tsh = [[T() for _ in range(2)] for _ in range(4)]
        t1b = [buf(com, F32, [512]) for _ in range(2)]
        t2b = [buf(com, F32, [512]) for _ in range(2)]
        tt1 = [T() for _ in range(2)]
        tt2 = [T() for _ in range(2)]
        rlb = [buf(com, F32, [512]) for _ in range(2)]
        trl = [T() for _ in range(2)]
        vbc = [buf(com, BF16, [512]) for _ in range(2)]
        sqc = [buf(com, BF16, [512]) for _ in range(2)]
        tvbc = [T() for _ in range(2)]
        tsqc = [T() for _ in range(2)]
        st_mean = buf(com, F32, [512])
        st_msq = buf(com, F32, [512])
        st_sd = buf(com, F32, [512])
        st_sd2 = buf(com, F32, [512])
        t_mean, t_msq, t_sd, t_sd2 = T(), T(), T(), T()
        tmpd = [buf(com, F32, [512]) for _ in range(2)]
        ttmpd = [T() for _ in range(2)]
        dtb = rlb
        tdtb = trl
        pw = [buf(com, BF16, [10, 128]) for _ in range(2)]
        tpw = [T() for _ in range(2)]
        t_const = T("const")
        COM_END = com.o

        main = Alloc(COM_END)
        attn_off = main.get(8 * LSQ * 2)
        attn_out = vw(attn_off, BF16, [8, LSQ])
        t_attn = [T() for _ in range(8)]
        AB0 = main.o
        ab = Alloc(AB0)
        cqn = buf(ab, BF16, [6, LSQ]); t_cqn = T()
        ckvn = buf(ab, BF16, [2, LSK]); t_ckvn = T()
        qr = buf(ab, BF16, [4, LSQ]); t_qr = T()
        Kb = [buf(ab, BF16, [LSK]) for _ in range(2)]; tK = [T() for _ in range(2)]; tKr = [T() for _ in range(2)]
        QV0 = ab.o
        Qb = [buf(ab, BF16, [LSQ]) for _ in range(2)]; tQ = [T() for _ in range(2)]
        NKT_MAX = (LSK + 127) // 128
        Vb = [buf(ab, BF16, [NKT_MAX, 192]) for _ in range(2)]; tV = [T() for _ in range(2)]
        AB_END = ab.o
        at = Alloc(attn_off)
        win = buf(at, BF16, [8, 1088]); t_win = T()
        wqr = buf(at, BF16, [6, 1024]); t_wqr = T()
        assert at.o <= AB0, at.o
        at2 = Alloc(QV0)
        xf = [buf(at2, F32, [8, AW]) for _ in range(1)] * 2; txf = [T()] * 2
        xb = [buf(at2, BF16, [8, AW]) for _ in range(2)]; txb = [T() for _ in range(2)]
        csb = buf(at2, F32, [8, AW]); tcsb = [T() for _ in range(8)]
        sqa = buf(at2, BF16, [8, AW]); tsqa = [T() for _ in range(8)]
        ctab = [buf(at2, F32, [AW]) for _ in range(2)]; stab = [buf(at2, F32, [AW]) for _ in range(2)]
        ttab = [T() for _ in range(2)]
        rq = st_mean; rkv = st_msq; t_rq, t_rkv = t_mean, t_msq
        A_END = at2.o
        assert A_END <= AB_END, (A_END, AB_END)
        pr = Alloc(AB0)
        NSTG = 6
        stg_f = [buf(pr, F32, [2048]) for _ in range(NSTG)]; tsf = [T() for _ in range(NSTG)]
        stg_b = [buf(pr, BF16, [2048]) for _ in range(NSTG)]; tsb = [T() for _ in range(NSTG)]
        cd = Alloc(AB0)
        slot = [buf(cd, F32, [8, 512]) for _ in range(2)]; tslot = [[T() for _ in range(8)] for _ in range(2)]
        hb = buf(cd, BF16, [8, 512]); thb = [T() for _ in range(8)]
        hid = buf(cd, BF16, [32, 512]); thid = [T() for _ in range(32)]
        wsb = [buf(cd, BF16, [4096]) for _ in range(3)]; tws = [T() for _ in range(3)]
        CD_END = cd.o
        cx = Alloc(CD_END)
        hb2 = buf(cx, BF16, [8, 512]); thb2 = [T() for _ in range(8)]
        hbs = buf(cx, BF16, [8, 512]); thbs = [T() for _ in range(8)]
        assert cx.o <= ARB, cx.o
        dx = Alloc(attn_off)
        WIN_MAX = 768 + NMETA
        hbw = buf(dx, BF16, [8, WIN_MAX]); thbw = T()
        q1 = buf(dx, BF16, [8, 512]); tq1 = T()
        k1 = buf(dx, BF16, [4, WIN_MAX]); tk1 = T()
        a1 = buf(dx, BF16, [8, 512]); ta1 = [T() for _ in range(8)]
        assert dx.o <= AB0, dx.o
        dx2 = Alloc(CD_END)
        v1a = buf(dx2, BF16, [7, 4, 192]); tv1 = T()
        c1t = buf(dx2, F32, [WIN_MAX]); s1t = buf(dx2, F32, [WIN_MAX]); tt1t = T()
        D_END = dx2.o
        dx3 = Alloc(D_END)
        k1o = buf(dx3, BF16, [4, WIN_MAX])
        assert dx3.o <= ARB, dx3.o
        print('SBUF', COM_END, AB0, AB_END, A_END, CD_END, D_END, ARB)
        assert max(A_END, D_END, AB_END) <= ARB

        state = {"ps": 0, "ws": 0, "t": 0, "rl": 0, "vs": 0, "pt": 0, "td": 0}

        def nps(lo=0, hi=8):
            i = lo + state["ps"] % (hi - lo)
            state["ps"] += 1
            return i

        def rr(key, n):
            i = state[key] % n
            state[key] += 1
            return i

        t_wbf = T("wbf")
        t_wbf_pro = [T("wbf_pro") for _ in range(8)]
        t_h2s, t_h2b = T("h2s"), T("h2b")

        def wslice(name, s):
            off, nkc, cps, ns = WSPEC[name]
            i = rr("ws", 3)
            n = nkc * cps
            src = DAP(wbf, off + s * 128 * n, [[n, 128], [1, n]])
            dst = wsb[i][:, 0:n]
            if not (dbg and dbg.get("nows") and state["ws"] > 3):
                p.dma(SP, lambda e, sem, dst=dst, src=src: e.dma_start(out=dst, in_=src).then_inc(sem, 16), 1, f"ws{i}",
                      reads=[t_wbf], writes=[tws[i]])
            return wsb[i][:, 0:n].rearrange("p (k c) -> p k c", k=nkc), tws[i]

        def mm_group(bank, rows, w, pieces, reads):
            def f(e, bank=bank, rows=rows, w=w, pieces=pieces):
                n = len(pieces)
                for i, (l, r) in enumerate(pieces):
                    inst = e.matmul(ps[bank][0:rows, 0:w], lhsT=l, rhs=r, start=(i == 0), stop=(i == n - 1))
                return inst
            return p.op(PE, f, reads=reads, writes=[tps[bank]])

        def ld_consts():
            mf = stg_f[0][:, 0:2048]
            p.dma(SP, lambda e, s: e.dma_start(out=mf, in_=masks.ap()).then_inc(s, 16), 1, "sf0", writes=[tsf[0]])
            p.op(DVE, lambda e: e.tensor_copy(out=masks_bf.rearrange("p a b -> p (a b)"), in_=mf), reads=[tsf[0]], writes=[t_const])
            p.dma(SP, lambda e, s: e.dma_start(out=vec, in_=vecd.ap()).then_inc(s, 16), 1, "vec", writes=[t_const])
            p.op(DVE, lambda e: e.memset(ones_bf, 1.0), writes=[t_const])
            p.op(ACT, lambda e: e.activation(out=vec[:, 72:88], in_=vec[:, 72:88], func=AF.Exp), reads=[t_const], writes=[t_const])
            p.op(DVE, lambda e: e.memset(estab.rearrange("p a b -> p (a b)"), 0.0), reads=[t_const], writes=[t_const])
            for kv in range(4):
                for bi, g in enumerate([0, 2, 1, 3]):
                    h = 4 * kv + g
                    p.op(DVE, lambda e, kv=kv, bi=bi, h=h: e.tensor_scalar(
                        out=estab[:, kv, bi * 128:(bi + 1) * 128], in0=estab[:, kv, bi * 128:(bi + 1) * 128],
                        scalar1=vec[:, 72 + h:73 + h], scalar2=None, op0=ALU.add), reads=[t_const], writes=[t_const])

        def prologue():
            engs = [DVE, POOL, ACT]
            for b in range(nblk):
                i = b % NSTG
                src = DAP(wall, b * blk, [[2048, 128], [1, 2048]])
                dstd = DAP(wbf, b * blk, [[2048, 128], [1, 2048]])
                p.dma(SP, lambda e, s, i=i, src=src: e.dma_start(out=stg_f[i], in_=src).then_inc(s, 16), 1, f"sf{i}", writes=[tsf[i]])
                en = engs[b % 3]
                if en == ACT:
                    p.op(ACT, lambda e, i=i: e.activation(out=stg_b[i], in_=stg_f[i], func=AF.Copy), reads=[tsf[i]], writes=[tsb[i]])
                else:
                    p.op(en, lambda e, i=i: e.tensor_copy(out=stg_b[i], in_=stg_f[i]), reads=[tsf[i]], writes=[tsb[i]])
                p.dma(ACT, lambda e, s, i=i, dstd=dstd: e.dma_start(out=dstd, in_=stg_b[i]).then_inc(s, 16), 1, f"sb{i}",
                      reads=[tsb[i]], writes=[t_wbf_pro[i]])

        def layernorm(si, w, gcol, bcol, want_hb=True, hbo=None, thbo=None):
            hbo = hb if hbo is None else hbo
            thbo = thb if thbo is None else thbo
            S = slot[si]
            bs = nps(6, 8)
            bq = nps(6, 8)
            pcs_s, pcs_q = [], []
            for oc in range(8):
                j = rr("vs", 2)
                p.op(ACT, lambda e, j=j, oc=oc: e.activation(out=vbc[j][:, 0:w], in_=S[:, oc, 0:w], func=AF.Copy), reads=[tslot[si][oc]], writes=[tvbc[j]])
                p.op(DVE, lambda e, j=j, oc=oc: e.tensor_tensor(out=sqc[j][:, 0:w], in0=S[:, oc, 0:w], in1=S[:, oc, 0:w], op=ALU.mult),
                     reads=[tslot[si][oc]], writes=[tsqc[j]])
                p.op(PE, lambda e, j=j, oc=oc: e.matmul(ps[bs][:, 0:w], lhsT=ones_bf, rhs=vbc[j][:, 0:w], start=(oc == 0), stop=(oc == 7)),
                     reads=[tvbc[j], t_const], writes=[tps[bs]])
                p.op(PE, lambda e, j=j, oc=oc: e.matmul(ps[bq][:, 0:w], lhsT=ones_bf, rhs=sqc[j][:, 0:w], start=(oc == 0), stop=(oc == 7)),
                     reads=[tsqc[j], t_const], writes=[tps[bq]])
            p.op(DVE, lambda e: e.tensor_scalar(out=st_mean[:, 0:w], in0=ps[bs][:, 0:w], scalar1=1.0 / DM, scalar2=None, op0=ALU.mult),
                 reads=[tps[bs]], writes=[t_mean])
            p.op(DVE, lambda e: e.tensor_tensor(out=st_msq[:, 0:w], in0=st_mean[:, 0:w], in1=st_mean[:, 0:w], op=ALU.mult),
                 reads=[t_mean], writes=[t_msq])
            p.op(DVE, lambda e: e.scalar_tensor_tensor(out=st_sd2[:, 0:w], in0=ps[bq][:, 0:w], scalar=1.0 / DM, in1=st_msq[:, 0:w],
                                                       op0=ALU.mult, op1=ALU.subtract), reads=[tps[bq], t_msq], writes=[t_sd2])
            p.op(ACT, lambda e: e.activation(out=st_sd[:, 0:w], in_=st_sd2[:, 0:w], func=AF.Sqrt, bias=LN_EPS, scale=1.0),
                 reads=[t_sd2], writes=[t_sd])
            if not USE_DIV:
                p.op(DVE, lambda e: e.reciprocal(out=st_sd2[:, 0:w], in_=st_sd[:, 0:w]), reads=[t_sd], writes=[t_sd2])
            for oc in range(8):
                p.op(DVE, lambda e, oc=oc: e.tensor_tensor(out=S[:, oc, 0:w], in0=S[:, oc, 0:w], in1=st_mean[:, 0:w], op=ALU.subtract),
                     reads=[t_mean, tslot[si][oc]], writes=[tslot[si][oc]])
                if USE_DIV:
                    p.op(DVE, lambda e, oc=oc: e.tensor_tensor(out=S[:, oc, 0:w], in0=S[:, oc, 0:w], in1=st_sd[:, 0:w], op=ALU.divide),
                         reads=[t_sd, tslot[si][oc]], writes=[tslot[si][oc]])
                else:
                    p.op(DVE, lambda e, oc=oc: e.tensor_tensor(out=S[:, oc, 0:w], in0=S[:, oc, 0:w], in1=st_sd2[:, 0:w], op=ALU.mult),
                         reads=[t_sd2, tslot[si][oc]], writes=[tslot[si][oc]])
                p.op(ACT, lambda e, oc=oc: e.activation(out=S[:, oc, 0:w], in_=S[:, oc, 0:w], func=AF.Identity,
                                                        bias=vec[:, bcol + oc:bcol + oc + 1], scale=vec[:, gcol + oc:gcol + oc + 1]),
                     reads=[t_const, tslot[si][oc]], writes=[tslot[si][oc]])
                if want_hb:
                    p.op(ACT, lambda e, oc=oc: e.activation(out=hbo[:, oc, 0:w], in_=S[:, oc, 0:w], func=AF.Copy),
                         reads=[tslot[si][oc]], writes=[thbo[oc]])

        def proj_resid(si, w, wname, src_fn, src_reads, pre=None):
            for s in range(2):
                wv, tw = pre[s] if pre is not None else wslice(wname, s)
                for m in range(4):
                    oc = 4 * s + m
                    bk = nps(0, 6)
                    mm_group(bk, 128, w, [(wv[:, kc, m * 128:(m + 1) * 128], src_fn(kc)) for kc in range(8)], [tw] + src_reads)
                    p.op(DVE, lambda e, oc=oc, bk=bk: e.scalar_tensor_tensor(
                        out=slot[si][:, oc, 0:w], in0=slot[si][:, oc, 0:w], scalar=ALPHA, in1=ps[bk][:, 0:w],
                        op0=ALU.mult, op1=ALU.add), reads=[tps[bk], tslot[si][oc]], writes=[tslot[si][oc]])

        def mlp_w1(si, w, l, hbi=None, thbi=None):
            hbi = hb if hbi is None else hbi
            thbi = thb if thbi is None else thbi
            for s in range(8):
                wv, tw = wslice(f"w1_{l}", s)
                for m in range(4):
                    hc = 4 * s + m
                    bk = nps(0, 6)
                    mm_group(bk, 128, w, [(wv[:, kc, m * 128:(m + 1) * 128], hbi[:, kc, 0:w]) for kc in range(8)], [tw] + thbi)
                    j = rr("rl", 2)
                    p.op(ACT, lambda e, j=j, bk=bk: e.activation(out=rlb[j][:, 0:w], in_=ps[bk][:, 0:w], func=AF.Relu),
                         reads=[tps[bk]], writes=[trl[j]])
                    p.op(POOL if hc % 4 == 3 else DVE, lambda e, j=j, hc=hc: e.tensor_tensor(out=hid[:, hc, 0:w], in0=rlb[j][:, 0:w], in1=rlb[j][:, 0:w], op=ALU.mult),
                         reads=[trl[j]], writes=[thid[hc]])

        def mlp_w2(si, w, l):
            for oc in range(8):
                wv, tw = wslice(f"w2_{l}", oc)
                bk = nps(0, 6)
                mm_group(bk, 128, w, [(wv[:, hc, :], hid[:, hc, 0:w]) for hc in range(32)], [tw] + thid)
                p.op(DVE, lambda e, oc=oc, bk=bk: e.scalar_tensor_tensor(
                    out=slot[si][:, oc, 0:w], in0=slot[si][:, oc, 0:w], scalar=ALPHA, in1=ps[bk][:, 0:w],
                    op0=ALU.mult, op1=ALU.add), reads=[tps[bk], tslot[si][oc]], writes=[tslot[si][oc]])

        def mlp(si, w, l):
            mlp_w1(si, w, l)
            mlp_w2(si, w, l)

        def phase_A(J):
            state['dmode'] = False
            o, nkc, cps, ns = WSPEC["win"]
            p.dma(SP, lambda e, s: e.dma_start(out=win.rearrange("p a b -> p (a b)"), in_=DAP(wbf, o, [[8 * 1088, 128], [1, 8 * 1088]])).then_inc(s, 16),
                  1, "win", reads=[t_wbf], writes=[t_win])
            o2 = WSPEC["wqr"][0]
            p.dma(SP, lambda e, s: e.dma_start(out=wqr.rearrange("p a b -> p (a b)"), in_=DAP(wbf, o2, [[6 * 1024, 128], [1, 6 * 1024]])).then_inc(s, 16),
                  1, "wqr", reads=[t_wbf], writes=[t_wqr])
            ci = 0
            for (c0, w) in chunks(0, J.Lq, AW) + chunks(J.Lq, J.Lk, AW):
                is_q = c0 < J.Lq
                i = ci % 2
                ci += 1
                src = DAP(xT, J.xoff + c0, [[XCOLS, 128], [128 * XCOLS, 8], [1, w]])
                p.dma(SP, lambda e, s, i=i, src=src, w=w: e.dma_start(out=xf[i][:, :, 0:w], in_=src).then_inc(s, 16), 1, "xf0", writes=[txf[i]])
                tc_src = DAP(tabs, J.t0off + c0, [[TABW, 128], [1, w]])
                ts_src = DAP(tabs, J.t0off + J.Lk + c0, [[TABW, 128], [1, w]])

                def ldt(e, s, i=i, w=w, tc_src=tc_src, ts_src=ts_src):
                    e.dma_start(out=ctab[i][:, 0:w], in_=tc_src).then_inc(s, 16)
                    e.dma_start(out=stab[i][:, 0:w], in_=ts_src).then_inc(s, 16)
                p.dma(SP, ldt, 2, f"tab{i}", writes=[ttab[i]])
                p.op(ACT, lambda e, i=i, w=w: e.activation(out=xb[i][:, :, 0:w], in_=xf[i][:, :, 0:w], func=AF.Copy), reads=[txf[i]], writes=[txb[i]])
                mlist = (list(range(6)) if is_q else []) + [6, 7]
                for m in mlist:
                    bk = nps(0, 5)
                    mm_group(bk, 128, w, [(win[:, kc, m * 128:(m + 1) * 128], xb[i][:, kc, 0:w]) for kc in range(8)], [t_win, txb[i]])
                    p.op(DVE, lambda e, m=m, bk=bk, w=w: e.tensor_copy(out=csb[:, m, 0:w], in_=ps[bk][:, 0:w]), reads=[tps[bk]], writes=[tcsb[m]])
                    p.op(DVE, lambda e, m=m, w=w: e.tensor_tensor(out=sqa[:, m, 0:w], in0=csb[:, m, 0:w], in1=csb[:, m, 0:w], op=ALU.mult),
                         reads=[tcsb[m]], writes=[tsqa[m]])
                groups = ([("q", range(6), 1.0 / QRANK, rq, t_rq)] if is_q else []) + [("kv", range(6, 8), 1.0 / KVRANK, rkv, t_rkv)]
                for (nm, ms, sc, rbuf, trb) in groups:
                    ms = list(ms)
                    bk = nps(5, 8)
                    mm_group(bk, 128, w, [(ones_bf, sqa[:, m, 0:w]) for m in ms], [t_const] + [tsqa[m] for m in ms])
                    p.op(ACT, lambda e, bk=bk, sc=sc, rbuf=rbuf, w=w: e.activation(out=rbuf[:, 0:w], in_=ps[bk][:, 0:w], func=AF.Sqrt, bias=RMS_EPS, scale=sc),
                         reads=[tps[bk]], writes=[trb])
                    p.op(DVE, lambda e, rbuf=rbuf, w=w: e.reciprocal(out=rbuf[:, 0:w], in_=rbuf[:, 0:w]), reads=[trb], writes=[trb])
                    for m in ms:
                        if m < 6:
                            dst, td, gc = cqn[:, m, c0:c0 + w], t_cqn, m
                        else:
                            dst, td, gc = ckvn[:, m - 6, c0:c0 + w], t_ckvn, m
                        p.op(DVE, lambda e, m=m, dst=dst, gc=gc, rbuf=rbuf, w=w: e.scalar_tensor_tensor(
                            out=dst, in0=csb[:, m, 0:w], scalar=vec[:, gc:gc + 1], in1=rbuf[:, 0:w], op0=ALU.mult, op1=ALU.mult),
                            reads=[tcsb[m], trb, t_const], writes=[td])
                b1 = nps(0, 5)
                mm_group(b1, 96, w, [(win[:, kc, 960:1056], xb[i][:, kc, 0:w]) for kc in range(8)], [t_win, txb[i]])
                b2 = nps(0, 5)
                mm_group(b2, 96, w, [(win[:, kc, 992:1088], xb[i][:, kc, 0:w]) for kc in range(8)], [t_win, txb[i]])
                j = rr("t", 2)
                p.op(DVE, lambda e, j=j, b1=b1, i=i, w=w: e.tensor_tensor(out=t1b[j][64:96, 0:w], in0=ps[b1][64:96, 0:w], in1=ctab[i][64:96, 0:w], op=ALU.mult),
                     reads=[tps[b1], ttab[i]], writes=[tt1[j]])
                p.op(DVE, lambda e, j=j, b2=b2, i=i, w=w: e.tensor_tensor(out=t2b[j][64:96, 0:w], in0=ps[b2][64:96, 0:w], in1=stab[i][64:96, 0:w], op=ALU.mult),
                     reads=[tps[b2], ttab[i]], writes=[tt2[j]])
                for kb in range(2):
                    p.op(POOL, lambda e, j=j, kb=kb, c0=c0, w=w: e.tensor_tensor(out=Kb[kb][64:96, c0:c0 + w], in0=t1b[j][64:96, 0:w], in1=t2b[j][64:96, 0:w], op=ALU.add),
                         reads=[tt1[j], tt2[j]], writes=[tKr[kb]])
                if is_q:
                    for rc in range(4):
                        ba = nps(0, 5)
                        mm_group(ba, 128, w, [(wqr[:, kc, rc * 128:(rc + 1) * 128], cqn[:, kc, c0:c0 + w]) for kc in range(6)], [t_wqr, t_cqn])
                        bb = nps(0, 5)
                        mm_group(bb, 128, w, [(wqr[:, kc, 512 + rc * 128:512 + (rc + 1) * 128], cqn[:, kc, c0:c0 + w]) for kc in range(6)], [t_wqr, t_cqn])
                        j = rr("t", 2)
                        p.op(DVE, lambda e, j=j, ba=ba, i=i, w=w: e.tensor_tensor(out=t1b[j][:, 0:w], in0=ps[ba][:, 0:w], in1=ctab[i][:, 0:w], op=ALU.mult),
                             reads=[tps[ba], ttab[i]], writes=[tt1[j]])
                        p.op(DVE, lambda e, j=j, bb=bb, i=i, w=w: e.tensor_tensor(out=t2b[j][:, 0:w], in0=ps[bb][:, 0:w], in1=stab[i][:, 0:w], op=ALU.mult),
                             reads=[tps[bb], ttab[i]], writes=[tt2[j]])
                        p.op(POOL, lambda e, j=j, rc=rc, c0=c0, w=w: e.tensor_tensor(out=qr[:, rc, c0:c0 + w], in0=t1b[j][:, 0:w], in1=t2b[j][:, 0:w], op=ALU.add),
                             reads=[tt1[j], tt2[j]], writes=[t_qr])

        def phase_B(J):
            nkt = (J.Lk + 127) // 128
            ktiles = [(t * 128, min(128, J.Lk - t * 128)) for t in range(nkt)]
            qch = chunks(0, J.Lq, QW)
            kch = chunks(0, J.Lk, QW)
            LA = 2
            for vb_ in range(2):
                p.op(POOL, lambda e, vb_=vb_: e.memset(Vb[vb_][:, :, 64:128], 1.0), writes=[tV[vb_]])
            oq, ok, ov = WSPEC["wqn"][0], WSPEC["wkn"][0], WSPEC["wvv"][0]

            def load_pw(pj):
                pi = pj % 2

                def ldp(e, s, pi=pi, pj=pj):
                    e.dma_start(out=pw[pi][:, 0:6, :], in_=DAP(wbf, oq + pj * 128 * 768, [[768, 128], [128, 6], [1, 128]])).then_inc(s, 16)
                    e.dma_start(out=pw[pi][:, 6:8, :], in_=DAP(wbf, ok + pj * 128 * 256, [[256, 128], [128, 2], [1, 128]])).then_inc(s, 16)
                    e.dma_start(out=pw[pi][:, 8:10, :], in_=DAP(wbf, ov + pj * 128 * 256, [[256, 128], [128, 2], [1, 128]])).then_inc(s, 16)
                p.dma(SP, ldp, 3, f"pw{pi}", reads=[t_wbf], writes=[tpw[pi]])

            def proj_head(h):
                pj, hh = h // 2, h % 2
                pi = pj % 2
                b = h % 2
                src = qr[(h % 4) * 32:(h % 4) * 32 + 32, h // 4, 0:J.Lq]
                p.dma(SP, lambda e, s, b=b, src=src: e.dma_start(out=Qb[b][64:96, 0:J.Lq], in_=src).then_inc(s, 16), 1, f"qrope{b}",
                      reads=[t_qr], writes=[tQ[b]])
                for (c0, w) in qch:
                    bk = nps(5, 8)
                    mm_group(bk, 64, w, [(pw[pi][:, kc, hh * 64:(hh + 1) * 64], cqn[:, kc, c0:c0 + w]) for kc in range(6)], [tpw[pi], t_cqn])
                    p.op(DVE, lambda e, bk=bk, c0=c0, w=w, b=b: e.tensor_copy(out=Qb[b][0:64, c0:c0 + w], in_=ps[bk][0:64, 0:w]), reads=[tps[bk]], writes=[tQ[b]])
                for (c0, w) in kch:
                    bk = nps(5, 8)
                    mm_group(bk, 64, w, [(pw[pi][:, 6 + kc, hh * 64:(hh + 1) * 64], ckvn[:, kc, c0:c0 + w]) for kc in range(2)], [tpw[pi], t_ckvn])
                    p.op(DVE, lambda e, bk=bk, c0=c0, w=w, b=b: e.tensor_copy(out=Kb[b][0:64, c0:c0 + w], in_=ps[bk][0:64, 0:w]), reads=[tps[bk]], writes=[tK[b]])

            def proj_v(pj):
                pi = pj % 2
                for g0 in range(0, nkt, 4):
                    grp = ktiles[g0:g0 + 4]
                    bk = nps(5, 8)

                    def fv(e, grp=grp, bk=bk, pi=pi):
                        for ti, (k0, rows) in enumerate(grp):
                            for kc in range(2):
                                inst = e.matmul(ps[bk][0:rows, ti * 128:(ti + 1) * 128], lhsT=ckvn[:, kc, k0:k0 + rows], rhs=pw[pi][:, 8 + kc, :],
                                                start=(kc == 0), stop=(kc == 1))
                        return inst
                    p.op(PE, fv, reads=[tpw[pi], t_ckvn], writes=[tps[bk]])
                    nfull = sum(1 for (_, r) in grp if r == 128)
                    if nfull:
                        pv = ps[bk][:, 0:nfull * 128].rearrange("p (t c) -> p t c", t=nfull)
                        p.op(DVE, lambda e, pv=pv, g0=g0, nfull=nfull, pi=pi: e.tensor_copy(out=Vb[pi][:, g0:g0 + nfull, 0:64], in_=pv[:, :, 0:64]),
                             reads=[tps[bk]], writes=[tV[pi]])
                        p.op(DVE, lambda e, pv=pv, g0=g0, nfull=nfull, pi=pi: e.tensor_copy(out=Vb[pi][:, g0:g0 + nfull, 128:192], in_=pv[:, :, 64:128]),
                             reads=[tps[bk]], writes=[tV[pi]])
                    if nfull < len(grp):
                        k0, rows = grp[-1]
                        ti = len(grp) - 1
                        t = g0 + ti
                        p.op(DVE, lambda e, bk=bk, ti=ti, t=t, rows=rows, pi=pi: e.tensor_copy(out=Vb[pi][0:rows, t, 0:64], in_=ps[bk][0:rows, ti * 128:ti * 128 + 64]),
                             reads=[tps[bk]], writes=[tV[pi]])
                        p.op(DVE, lambda e, bk=bk, ti=ti, t=t, rows=rows, pi=pi: e.tensor_copy(out=Vb[pi][0:rows, t, 128:192], in_=ps[bk][0:rows, ti * 128 + 64:ti * 128 + 128]),
                             reads=[tps[bk]], writes=[tV[pi]])

            def side_work(h):
                if h >= NH:
                    return
                if h % 2 == 0:
                    load_pw(h // 2)
                    proj_head(h)
                    proj_v(h // 2)
                else:
                    proj_head(h)

            side_work(0)
            side_work(1)
            steps = []
            for h in range(NH):
                for ci, (c0, w) in enumerate(qch):
                    for t, (k0, rows) in enumerate(ktiles):
                        steps.append((h, ci, c0, w, t, k0, rows))
            N = len(steps)
            sinfo = {}
            obank = {}
            for s in range(N + LA):
                if s < N:
                    h, ci, c0, w, t, k0, rows = steps[s]
                    b = h % 2
                    sbk = nps(0, 3)
                    p.op(PE, lambda e, sbk=sbk, rows=rows, w=w, k0=k0, c0=c0, b=b: e.matmul(
                        ps[sbk][0:rows, 0:w], lhsT=Kb[b][0:96, k0:k0 + rows], rhs=Qb[b][0:96, c0:c0 + w], start=True, stop=True),
                        reads=[tK[b], tKr[b], tQ[b]], writes=[tps[sbk]])
                    pt = rr("pt", 3)
                    p.op(ACT, lambda e, pt=pt, sbk=sbk, rows=rows, w=w: e.activation(out=PT[pt][0:rows, 0:w], in_=ps[sbk][0:rows, 0:w], func=AF.Exp, scale=SC0),
                         reads=[tps[sbk]], writes=[tPT[pt]])
                    sinfo[s] = pt
                s2 = s - LA
                if s2 >= 0:
                    h, ci, c0, w, t, k0, rows = steps[s2]
                    pt = sinfo.pop(s2)
                    pj, hh = h // 2, h % 2
                    pi = pj % 2
                    vlo = 0 if hh == 0 else 64
                    if t == 0:
                        obank[(h, ci)] = 3 + rr("td", 2)
                    ob = obank[(h, ci)]
                    p.op(PE, lambda e, ob=ob, pt=pt, rows=rows, w=w, t=t, vlo=vlo, pi=pi, last=(t == nkt - 1): e.matmul(
                        ps[ob][:, 0:w], lhsT=Vb[pi][0:rows, t, vlo:vlo + 128], rhs=PT[pt][0:rows, 0:w], start=(t == 0), stop=last),
                        reads=[tV[pi], tPT[pt]], writes=[tps[ob]])
                    if t == nkt - 1:
                        j = rr("t", 2)
                        if hh == 0:
                            olo, dlo = 0, 64
                        else:
                            olo, dlo = 64, 0
                        p.op(DVE, lambda e, j=j, ob=ob, dlo=dlo, w=w: e.reciprocal(out=t1b[j][dlo:dlo + 64, 0:w], in_=ps[ob][dlo:dlo + 64, 0:w]),
                             reads=[tps[ob]], writes=[tt1[j]])
                        p.op(POOL, lambda e, j=j, olo=olo, dlo=dlo, w=w: e.tensor_copy(out=t2b[j][olo:olo + 64, 0:w], in_=t1b[j][dlo:dlo + 64, 0:w]),
                             reads=[tt1[j]], writes=[tt2[j]])
                        p.op(DVE, lambda e, j=j, ob=ob, olo=olo, w=w, c0=c0, pj=pj: e.tensor_tensor(
                            out=attn_out[olo:olo + 64, pj, c0:c0 + w], in0=ps[ob][olo:olo + 64, 0:w], in1=t2b[j][olo:olo + 64, 0:w], op=ALU.mult),
                            reads=[tps[ob], tt2[j]], writes=[t_attn[pj]])
                        if ci == len(qch) - 1:
                            side_work(h + 2)

        def phase_C(J):
            blocks = chunks(0, J.Lq, QW)
            nb_ = len(blocks)
            hbufs = [(hb, thb), (hb2, thb2)]

            def pr_pre(bi):
                return [wslice("wo0", 0), wslice("wo0", 1)]

            def pr(bi, pre=None):
                c0, w = blocks[bi]
                si = bi % 2
                src = DAP(xT, J.xoff + c0, [[XCOLS, 128], [128 * XCOLS, 8], [1, w]])
                p.dma(SP, lambda e, s, si=si, src=src, w=w: e.dma_start(out=slot[si][:, :, 0:w], in_=src).then_inc(s, 16), 1, f"slot{si}", writes=tslot[si])
                proj_resid(si, w, "wo0", lambda kc, c0=c0, w=w: attn_out[:, kc, c0:c0 + w], t_attn, pre=pre)

            def ln1(bi):
                c0, w = blocks[bi]
                hbo, thbo = hbufs[bi % 2]
                layernorm(bi % 2, w, 8, 24, hbo=hbo, thbo=thbo)

            def spill(bi):
                c0, w = blocks[bi]
                si = bi % 2
                lo = max(c0, J.n_pre)
                hi = min(c0 + w, J.n_pre + J.n_own)
                if hi > lo:
                    dst = DAP(h2s, lo - J.n_pre, [[2048, 128], [128 * 2048, 8], [1, hi - lo]])
                    p.dma(SP, lambda e, s, si=si, dst=dst, a=lo - c0, b=hi - c0: e.dma_start(out=dst, in_=slot[si][:, :, a:b]).then_inc(s, 16), 1, "h2s_w",
                          reads=tslot[si], writes=[t_h2s])
                dstb = DAP(h2b, c0, [[LSQ, 128], [128 * LSQ, 8], [1, w]])
                p.dma(SP, lambda e, s, dstb=dstb, w=w: e.dma_start(out=dstb, in_=hbs[:, :, 0:w]).then_inc(s, 16), 1, "h2b_w", reads=thbs, writes=[t_h2b])

            pr(0)
            ln1(0)
            if nb_ > 1:
                pr(1)
            mlp_w1(0, blocks[0][1], 0, *hbufs[0])
            for bi, (c0, w) in enumerate(blocks):
                si = bi % 2
                if bi + 1 < nb_:
                    ln1(bi + 1)
                mlp_w2(si, w, 0)
                layernorm(si, w, 40, 56, hbo=hbs, thbo=thbs)
                if bi + 1 < nb_:
                    mlp_w1((bi + 1) % 2, blocks[bi + 1][1], 0, *hbufs[(bi + 1) % 2])
                pre = pr_pre(bi + 2) if bi + 2 < nb_ else None
                spill(bi)
                if bi + 2 < nb_:
                    pr(bi + 2, pre)

        def phase_D(J, finals):
            state['dmode'] = True
            p.op(POOL, lambda e: e.memset(v1a[:, :, :, 64:128], 1.0), writes=[tv1])
            p.op(POOL, lambda e: e.memset(k1[64:128, :, :], 0.0), writes=[tk1])
            p.op(POOL, lambda e: e.memset(k1o[0:64, :, :], 0.0), writes=[tk1])
            nblk_d = J.n_own // 512
            def geom(jb):
                si = jb % 2
                s0 = J.n_pre + 512 * jb
                lo = max(s0 - 128, 0)
                hi = min(s0 + 640, J.R)
                nwt = (hi - lo) // 128
                ww = hi - lo
                WT = ww + NMETA
                return si, s0, lo, hi, nwt, ww, WT

            def load_proj(jb):
                si, s0, lo, hi, nwt, ww, WT = geom(jb)
                src = DAP(h2s, 512 * jb, [[2048, 128], [128 * 2048, 8], [1, 512]])
                p.dma(SP, lambda e, s, si=si, src=src: e.dma_start(out=slot[si], in_=src).then_inc(s, 16), 1, f"slot{si}", reads=[t_h2s], writes=tslot[si])

                def ldw(e, s, lo=lo, ww=ww):
                    e.dma_start(out=hbw[:, :, 0:ww], in_=DAP(h2b, lo, [[LSQ, 128], [128 * LSQ, 8], [1, ww]])).then_inc(s, 16)
                    e.dma_start(out=hbw[:, :, ww:ww + NMETA], in_=DAP(h2b, J.meta, [[LSQ, 128], [128 * LSQ, 8], [1, NMETA]])).then_inc(s, 16)
                p.dma(SP, ldw, 2, "hbw", reads=[t_h2b], writes=[thbw])
                t1L = J.Lq

                def ldt1(e, s, lo=lo, ww=ww):
                    e.dma_start(out=c1t[:, 0:ww], in_=DAP(tabs, J.t1off + lo, [[TABW, 128], [1, ww]])).then_inc(s, 16)
                    e.dma_start(out=c1t[:, ww:ww + NMETA], in_=DAP(tabs, J.t1off + J.meta, [[TABW, 128], [1, NMETA]])).then_inc(s, 16)
                    e.dma_start(out=s1t[:, 0:ww], in_=DAP(tabs, J.t1off + t1L + lo, [[TABW, 128], [1, ww]])).then_inc(s, 16)
                    e.dma_start(out=s1t[:, ww:ww + NMETA], in_=DAP(tabs, J.t1off + t1L + J.meta, [[TABW, 128], [1, NMETA]])).then_inc(s, 16)
                p.dma(SP, ldt1, 4, "t1t", writes=[tt1t])
                wk, twk = wslice("wk1", 0)
                wks, twks = wslice("wk1s", 0)
                for kv in range(4):
                    for (c0, w) in chunks(0, WT, 512):
                        ba = nps(0, 6)
                        mm_group(ba, 128, w, [(wk[:, kc, kv * 128:(kv + 1) * 128], hbw[:, kc, c0:c0 + w]) for kc in range(8)], [twk, thbw])
                        bb = nps(0, 6)
                        mm_group(bb, 128, w, [(wks[:, kc, kv * 128:(kv + 1) * 128], hbw[:, kc, c0:c0 + w]) for kc in range(8)], [twks, thbw])
                        j = rr("t", 2)
                        p.op(DVE, lambda e, j=j, ba=ba, c0=c0, w=w: e.tensor_tensor(out=t1b[j][:, 0:w], in0=ps[ba][:, 0:w], in1=c1t[:, c0:c0 + w], op=ALU.mult),
                             reads=[tps[ba], tt1t], writes=[tt1[j]])
                        p.op(DVE, lambda e, j=j, bb=bb, c0=c0, w=w: e.tensor_tensor(out=t2b[j][:, 0:w], in0=ps[bb][:, 0:w], in1=s1t[:, c0:c0 + w], op=ALU.mult),
                             reads=[tps[bb], tt1t], writes=[tt2[j]])
                        p.op(POOL, lambda e, j=j, kv=kv, c0=c0, w=w: e.tensor_tensor(out=k1[0:64, kv, c0:c0 + w], in0=t1b[j][0:64, 0:w], in1=t2b[j][0:64, 0:w], op=ALU.add),
                             reads=[tt1[j], tt2[j]], writes=[tk1])
                        p.op(POOL, lambda e, j=j, kv=kv, c0=c0, w=w: e.tensor_tensor(out=k1o[64:128, kv, c0:c0 + w], in0=t1b[j][64:128, 0:w], in1=t2b[j][64:128, 0:w], op=ALU.add),
                             reads=[tt1[j], tt2[j]], writes=[tk1])
                wv_, twv = wslice("wv1", 0)
                vt = [(t * 128, 128) for t in range(nwt)] + [(ww, NMETA)]
                for ti, (k0, rows) in enumerate(vt):
                    bk = nps(0, 6)
                    mm_group(bk, rows, 256, [(hbw[:, kc, k0:k0 + rows], wv_[:, kc, :]) for kc in range(8)], [twv, thbw])
                    pv = ps[bk][0:rows, 0:256].rearrange("p (k c) -> p k c", k=4)
                    p.op(DVE, lambda e, pv=pv, ti=ti, rows=rows: e.tensor_copy(out=v1a[0:rows, ti, :, 0:64], in_=pv), reads=[tps[bk]], writes=[tv1])
                    p.op(DVE, lambda e, pv=pv, ti=ti, rows=rows: e.tensor_copy(out=v1a[0:rows, ti, :, 128:192], in_=pv), reads=[tps[bk]], writes=[tv1])
                qoff = s0 - lo
                for s in range(2):
                    wq_, twq = wslice("wq1", s)
                    wqs_, twqs = wslice("wq1s", s)
                    for m in range(4):
                        oc = 4 * s + m
                        ba = nps(0, 6)
                        mm_group(ba, 128, 512, [(wq_[:, kc, m * 128:(m + 1) * 128], hbw[:, kc, qoff:qoff + 512]) for kc in range(8)], [twq, thbw])
                        bb = nps(0, 6)
                        mm_group(bb, 128, 512, [(wqs_[:, kc, m * 128:(m + 1) * 128], hbw[:, kc, qoff:qoff + 512]) for kc in range(8)], [twqs, thbw])
                        j = rr("t", 2)
                        p.op(DVE, lambda e, j=j, ba=ba, qoff=qoff: e.tensor_tensor(out=t1b[j], in0=ps[ba][:, :], in1=c1t[:, qoff:qoff + 512], op=ALU.mult),
                             reads=[tps[ba], tt1t], writes=[tt1[j]])
                        p.op(DVE, lambda e, j=j, bb=bb, qoff=qoff: e.tensor_tensor(out=t2b[j], in0=ps[bb][:, :], in1=s1t[:, qoff:qoff + 512], op=ALU.mult),
                             reads=[tps[bb], tt1t], writes=[tt2[j]])
                        p.op(POOL, lambda e, j=j, oc=oc: e.tensor_tensor(out=q1[:, oc, :], in0=t1b[j], in1=t2b[j], op=ALU.add),
                             reads=[tt1[j], tt2[j]], writes=[tq1])

            def attn(jb):
                si, s0, lo, hi, nwt, ww, WT = geom(jb)
                nob = J.n_own // 128
                dsteps = []
                for qb in range(4):
                    gq = (s0 - J.n_pre) // 128 + qb
                    wt_own = (s0 - lo) // 128 + qb
                    tiles = []
                    if wt_own - 1 >= 0:
                        first = (gq == 0)
                        tiles.append((wt_own - 1, 128, 2 if (first and J.special) else 0))
                    tiles.append((wt_own, 128, None))
                    if wt_own + 1 < nwt:
                        lastb = (gq == nob - 1)
                        tiles.append((wt_own + 1, 128, 3 if (lastb and J.special) else 1))
                    tiles.append((nwt, NMETA, None))
                    for kv in range(4):
                        for ii, (ti, rows, mk) in enumerate(tiles):
                            dsteps.append((qb, kv, ii, ti, rows, mk, ii == len(tiles) - 1))
                ND = len(dsteps)
                LAD = 3
                pend = []
                dinfo = {}
                oset = {}
                for s_ in range(ND + LAD):
                    if s_ < ND:
                        qb, kv, ii, ti, rows, mk, lastt = dsteps[s_]
                        sbk = s_ % 4
                        k0 = ti * 128 if ti < nwt else ww

                        def fqk(e, sbk=sbk, rows=rows, k0=k0, kv=kv, qb=qb):
                            e.matmul(ps[sbk][0:rows, 0:256], lhsT=k1[:, kv, k0:k0 + rows],
                                     rhs=q1[:, 2 * kv:2 * kv + 2, qb * 128:(qb + 1) * 128], start=True, stop=True)
                            return e.matmul(ps[sbk][0:rows, 256:512], lhsT=k1o[:, kv, k0:k0 + rows],
                                            rhs=q1[:, 2 * kv:2 * kv + 2, qb * 128:(qb + 1) * 128], start=True, stop=True)
                        p.op(PE, fqk, reads=[tk1, tq1], writes=[tps[sbk]])
                        pt = rr("pt", 4)
                        p.op(ACT, lambda e, pt=pt, sbk=sbk, rows=rows: e.activation(out=PT[pt][0:rows, :], in_=ps[sbk][0:rows, :], func=AF.Exp, scale=SC1),
                             reads=[tps[sbk]], writes=[tPT[pt]])
                        if mk is not None:
                            p.op(DVE, lambda e, pt=pt, mk=mk: e.tensor_tensor(out=PT[pt], in0=PT[pt], in1=masks_bf[:, mk, :], op=ALU.mult),
                                 reads=[tPT[pt], t_const], writes=[tPT[pt]])
                        dinfo[s_] = pt
                    s2 = s_ - LAD
                    if s2 >= 0:
                        qb, kv, ii, ti, rows, mk, lastt = dsteps[s2]
                        pt = dinfo.pop(s2)
                        if ii == 0:
                            oset[(qb, kv)] = 4 + 2 * rr("td", 2)
                        obe = oset[(qb, kv)]
                        obo = obe + 1
                        p.op(PE, lambda e, pt=pt, rows=rows, ti=ti, kv=kv, ii=ii, lastt=lastt, obe=obe: e.matmul(
                            ps[obe][:, 0:256], lhsT=v1a[0:rows, ti, kv, 0:128], rhs=PT[pt][0:rows, 0:256], start=(ii == 0), stop=lastt),
                            reads=[tv1, tPT[pt]], writes=[tps[obe]])
                        p.op(PE, lambda e, pt=pt, rows=rows, ti=ti, kv=kv, ii=ii, lastt=lastt, obo=obo: e.matmul(
                            ps[obo][:, 0:256], lhsT=v1a[0:rows, ti, kv, 64:192], rhs=PT[pt][0:rows, 256:512], start=(ii == 0), stop=lastt),
                            reads=[tv1, tPT[pt]], writes=[tps[obo]])
                        if lastt:
                            jo = rr("t", 2)
                            p.op(ACT, lambda e, jo=jo, obe=obe: e.activation(out=t1b[jo][:, 0:256], in_=ps[obe][:, 0:256], func=AF.Copy), reads=[tps[obe]], writes=[tt1[jo]])
                            p.op(ACT, lambda e, jo=jo, obo=obo: e.activation(out=t1b[jo][:, 256:512], in_=ps[obo][:, 0:256], func=AF.Copy), reads=[tps[obo]], writes=[tt1[jo]])
                            j = rr("rl", 2)
                            p.op(DVE, lambda e, j=j, jo=jo, kv=kv: e.tensor_tensor(
                                out=dtb[j][64:128, 0:256], in0=t1b[jo][64:128, 0:256], in1=estab[64:128, kv, 0:256], op=ALU.add),
                                reads=[tt1[jo], t_const], writes=[tdtb[j]])
                            p.op(DVE, lambda e, j=j, jo=jo, kv=kv: e.tensor_tensor(
                                out=dtb[j][0:64, 0:256], in0=t1b[jo][0:64, 256:512], in1=estab[0:64, kv, 256:512], op=ALU.add),
                                reads=[tt1[jo], t_const], writes=[tdtb[j]])
                            p.op(DVE, lambda e, j=j: e.reciprocal(out=dtb[j][:, 0:256], in_=dtb[j][:, 0:256]), reads=[tdtb[j]], writes=[tdtb[j]])
                            p.op(POOL, lambda e, j=j: e.tensor_copy(out=tmpd[j][0:64, 0:256], in_=dtb[j][64:128, 0:256]), reads=[tdtb[j]], writes=[ttmpd[j]])
                            p.op(POOL, lambda e, j=j: e.tensor_copy(out=tmpd[j][64:128, 0:256], in_=dtb[j][0:64, 0:256]), reads=[tdtb[j]], writes=[ttmpd[j]])
                            for fnp in pend:
                                fnp()
                            pend.clear()

                            def fin(j=j, jo=jo, kv=kv, qb=qb):
                                for (olo, ecol) in [(0, 0), (64, 256)]:
                                    p.op(DVE, lambda e, olo=olo, ecol=ecol: e.tensor_tensor(
                                        out=a1[olo:olo + 64, 2 * kv:2 * kv + 2, qb * 128:(qb + 1) * 128],
                                        in0=t1b[jo][olo:olo + 64, ecol:ecol + 256].rearrange("p (a b) -> p a b", a=2),
                                        in1=tmpd[j][olo:olo + 64, 0:256].rearrange("p (a b) -> p a b", a=2), op=ALU.mult),
                                        reads=[tt1[jo], ttmpd[j]], writes=[ta1[2 * kv], ta1[2 * kv + 1]])
                            pend.append(fin)
                for fnp in pend:
                    fnp()
                pend.clear()

            load_proj(0)
            attn(0)
            state['dmode_keep'] = True
            for jb in range(nblk_d):
                si = jb % 2
                proj_resid(si, 512, "wo1", lambda kc: a1[:, kc, :], ta1)
                layernorm(si, 512, 16, 32)
                if jb + 1 < nblk_d:
                    load_proj(jb + 1)
                mlp(si, 512, 1)
                layernorm(si, 512, 48, 64, want_hb=False)
                dst = DAP(yT, J.yidx * DM * 2048 + 512 * jb, [[2048, 128], [128 * 2048, 8], [1, 512]])
                finals.append(p.dma(SP, lambda e, s, si=si, dst=dst: e.dma_start(out=dst, in_=slot[si]).then_inc(s, 16), 1, f"y{si}", reads=tslot[si]))
                if jb + 1 < nblk_d:
                    attn(jb + 1)

        finals = []
        ld_consts()
        prologue()
        p.barrier()
        phs = dbg["phases"] if dbg else "ABCD"
        if dbg:
            jobs = [jobs[i] for i in dbg["jobs"]]

        def dump(dt_, view2d, key):
            finals.append(p.dma(SP, lambda e, s: e.dma_start(out=dt_.ap(), in_=view2d).then_inc(s, 16), 1, key))
        for J in jobs:
            if "A" in phs:
                phase_A(J)
                p.barrier()
                if dbg:
                    dump(d_cqn, cqn.rearrange("p a b -> p (a b)"), "dc1")
                    dump(d_ckvn, ckvn.rearrange("p a b -> p (a b)"), "dc2")
                    dump(d_qr, qr.rearrange("p a b -> p (a b)"), "dc3")
                    dump(d_k0, Kb[0], "dc4")
                    p.barrier()
            if "B" in phs:
                phase_B(J)
                p.barrier()
                if dbg:
                    dump(d_attn, attn_out.rearrange("p a b -> p (a b)"), "dc5")
                    p.barrier()
            if "C" in phs:
                phase_C(J)
                p.barrier()
            if "D" in phs:
                phase_D(J, finals)
                p.barrier()
        last = {}
        for d in finals:
            last[d.semkey] = d
        build_block(nc, p, list(last.values()))
    return nc


_CACHE = {}


def kernel(x_prompt, x_sample, meta_tokens, mla_w_in, mla_g_q, mla_w_uq, mla_g_kv, mla_w_ukv, mla_w_o,
           gqa_w_qkv, gqa_sink, gqa_w_o, mlp_w1, mlp_w2, ln1_g, ln1_b, ln2_g, ln2_b):
    inp = dict(mla_w_in=np.asarray(mla_w_in), mla_w_uq=np.asarray(mla_w_uq), mla_w_ukv=np.asarray(mla_w_ukv), mla_w_o=np.asarray(mla_w_o),
               gqa_w_qkv=np.asarray(gqa_w_qkv), gqa_w_o=np.asarray(gqa_w_o), mlp_w1=np.asarray(mlp_w1), mlp_w2=np.asarray(mlp_w2))
    wall, nblk = build_wall(inp)
    x_prompt = np.asarray(x_prompt, np.float32)
    x_sample = np.asarray(x_sample, np.float32)
    metaT = np.asarray(meta_tokens, np.float32).T

    def pc(v):
        return np.asarray(v, np.float32).reshape(-1, 128).T
    vec = np.zeros((128, NVEC), np.float32)
    vec[:, 0:6] = pc(mla_g_q[0])
    vec[:, 6:8] = pc(mla_g_kv[0])
    vec[:, 8:16] = pc(ln1_g[0]); vec[:, 16:24] = pc(ln1_g[1])
    vec[:, 24:32] = pc(ln1_b[0]); vec[:, 32:40] = pc(ln1_b[1])
    vec[:, 40:48] = pc(ln2_g[0]); vec[:, 48:56] = pc(ln2_g[1])
    vec[:, 56:64] = pc(ln2_b[0]); vec[:, 64:72] = pc(ln2_b[1])
    vec[:, 72:88] = np.broadcast_to(np.asarray(gqa_sink, np.float32).reshape(1, 16), (128, 16))

    posP = np.concatenate([16 + np.arange(2048), np.arange(16)])
    CP0, SP0 = mla_tabs(posP)
    CP1, SP1 = gqa_tabs(posP)
    kk = np.arange(128)[:, None]
    ii = np.arange(128)[None, :]
    tri_ge = np.tile((kk >= ii).astype(np.float32), (1, 4))
    tri_le = np.tile((kk <= ii).astype(np.float32), (1, 4))

    in_maps = []
    for c in range(NCORES):
        sq, half = c // 2, c % 2
        xT = np.empty((DM, XCOLS), np.float32)
        for i in range(4):
            xT[:, i * LP:i * LP + 2048] = x_prompt[4 * c + i].T
            xT[:, i * LP + 2048:(i + 1) * LP] = metaT
        xs = x_sample[sq]
        if half == 0:
            own = np.arange(0, 2048); post = np.arange(2048, 2176); pre = np.arange(3968, 4096)
            rest = np.arange(2176, 3968)
            pre_valid, post_valid = 0.0, 1.0
        else:
            own = np.arange(2048, 4096); pre = np.arange(1920, 2048); post = np.arange(0, 128)
            rest = np.arange(128, 1920)
            pre_valid, post_valid = 1.0, 0.0
        order = np.concatenate([pre, own, post])
        b = 4 * LP
        xT[:, b:b + 2304] = xs[order].T
        xT[:, b + 2304:b + 2320] = metaT
        xT[:, b + 2320:b + LSK] = xs[rest].T
        posS = np.concatenate([16 + order, np.arange(16), 16 + rest])
        CS0, SS0 = mla_tabs(posS)
        CS1, SS1 = gqa_tabs(posS[:LSQ])
        tabs = np.concatenate([CP0, SP0, CS0, SS0, CP1, SP1, CS1, SS1], axis=1).astype(np.float32)
        assert tabs.shape == (128, TABW)
        masks = np.concatenate([tri_ge, tri_le, tri_ge * pre_valid, tri_le * post_valid], axis=1).astype(np.float32)
        in_maps.append({"xT": xT, "wall": wall, "tabs": np.ascontiguousarray(tabs), "masks": np.ascontiguousarray(masks), "vec": vec})

    if nblk not in _CACHE:
        _CACHE[nblk] = build_program(nblk)
    nc = _CACHE[nblk]
    res = run_bass_kernel_spmd(nc, in_maps, core_ids=list(range(NCORES)))
    y_prompt = np.empty((32, 2048, DM), np.float32)
    y_sample = np.empty((4, 4096, DM), np.float32)
    for c in range(NCORES):
        yT = res.results[c]["yT"]
        for i in range(4):
            y_prompt[4 * c + i] = yT[i].T
        sq, half = c // 2, c % 2
        y_sample[sq, half * 2048:(half + 1) * 2048] = yT[4].T
    return (y_prompt, y_sample)
```
